# Optimizing a Trainium2 kernel written in Bass

```python
import jax, jax.numpy as jnp
from jax import lax
import numpy as np


D_MODEL = 2048
BATCH = 16
SEQ = 2048
DEPTH = 1
DEC_BATCH = 4
DEC_SEQ = 4096
PAST_LEN = 128

GRID_W = 64
HEAD_DIM = 64
RWKV_HEADS = 16
NAT_HEADS = 16
RWKV_WIDTH = RWKV_HEADS * HEAD_DIM
NAT_WIDTH = NAT_HEADS * HEAD_DIM
D_MIX = RWKV_WIDTH + NAT_WIDTH
DECAY_RANK = 64
ICLR_RANK = 64
GATE_RANK = 128
RWKV_COLS = 3 * RWKV_WIDTH + DECAY_RANK + ICLR_RANK + GATE_RANK
IN_COLS = RWKV_COLS + 3 * NAT_WIDTH
D_FF = 5632
WIN_H = 8
WIN_W = 16
RMS_EPS = 1e-6
GN_EPS = 64e-5
DECAY_OFFSET = 0.5
FFN_RESIDUAL = 0.5

kernel_name = 'bi_rwkv7_natten2d_macaron_encoder'


def rmsnorm(x, g):
    xf = x.astype(jnp.float32)
    y = xf * lax.rsqrt(jnp.mean(xf * xf, axis=-1, keepdims=True) + RMS_EPS) * g.astype(jnp.float32)
    return y.astype(x.dtype)


def swiglu(h, w_gate, w_up, w_down):
    return (jax.nn.silu(h @ w_gate) * (h @ w_up)) @ w_down


def centred_shift(u, mu):
    prev = jnp.pad(u[:, :-1], ((0, 0), (1, 0), (0, 0)))
    nxt = jnp.pad(u[:, 1:], ((0, 0), (0, 1), (0, 0)))
    return u + mu * (0.5 * (prev + nxt) - u)


def wkv7_scan(r, w, k, v, a, b, reverse):
    B, T, H, N = r.shape

    def step(S, inp):
        r_t, w_t, k_t, v_t, a_t, b_t = inp
        sa = jnp.einsum('bhij,bhj->bhi', S, a_t)
        S = S * w_t[:, :, None, :] + sa[..., :, None] * b_t[:, :, None, :] + v_t[..., :, None] * k_t[:, :, None, :]
        return S, jnp.einsum('bhij,bhj->bhi', S, r_t)

    xs = (jnp.moveaxis(r, 1, 0), jnp.moveaxis(w, 1, 0), jnp.moveaxis(k, 1, 0),
          jnp.moveaxis(v, 1, 0), jnp.moveaxis(a, 1, 0), jnp.moveaxis(b, 1, 0))
    S0 = jnp.zeros((B, H, N, N), jnp.float32)
    _, ys = lax.scan(step, S0, xs, reverse=reverse)
    return jnp.moveaxis(ys, 0, 1)


def rwkv7_time_mix(u, decay_bias_fwd, decay_up_fwd, decay_bias_bwd, decay_up_bwd,
                   iclr_bias_fwd, iclr_up_fwd, iclr_bias_bwd, iclr_up_bwd, gate_up,
                   key_norm_scale, key_iclr_mix, bonus_scale, gn_w, gn_b):
    B, T, _ = u.shape
    u = u.astype(jnp.float32)
    W = RWKV_WIDTH
    r = u[..., :W]
    k = u[..., W:2 * W]
    v = u[..., 2 * W:3 * W]
    o = 3 * W
    dec_lo = jnp.tanh(u[..., o:o + DECAY_RANK])
    o = o + DECAY_RANK
    iclr_lo = u[..., o:o + ICLR_RANK]
    o = o + ICLR_RANK
    gate_lo = jax.nn.sigmoid(u[..., o:o + GATE_RANK])

    def heads(t):
        return t.reshape(B, T, RWKV_HEADS, HEAD_DIM)

    kk = heads(k * key_norm_scale)
    kk = kk / jnp.maximum(jnp.linalg.norm(kk, axis=-1, keepdims=True), 1e-12)

    def direction(decay_bias, decay_up, iclr_bias, iclr_up, reverse):
        log_w = -jax.nn.softplus(-(decay_bias + dec_lo @ decay_up)) - DECAY_OFFSET
        w = jnp.exp(-jnp.exp(log_w))
        a = jax.nn.sigmoid(iclr_bias + iclr_lo @ iclr_up)
        k_d = k * (1.0 + (a - 1.0) * key_iclr_mix)
        y = wkv7_scan(heads(r), heads(w), heads(k_d), heads(v), -kk, kk * heads(a), reverse)
        return y, k_d

    y_f, k_f = direction(decay_bias_fwd, decay_up_fwd, iclr_bias_fwd, iclr_up_fwd, False)
    y_b, k_b = direction(decay_bias_bwd, decay_up_bwd, iclr_bias_bwd, iclr_up_bwd, True)
    y = y_f + y_b
    mean = jnp.mean(y, axis=-1, keepdims=True)
    var = jnp.var(y, axis=-1, keepdims=True)
    y = ((y - mean) * lax.rsqrt(var + GN_EPS)).reshape(B, T, W) * gn_w + gn_b
    bonus = jnp.sum(heads(r * (0.5 * (k_f + k_b)) * bonus_scale), axis=-1, keepdims=True) * heads(v)
    return (y + bonus.reshape(B, T, W)) * (gate_lo @ gate_up)


def neighbourhood_attention(q, k, v, rpb):
    B, T, H, Dh = q.shape
    rows = T // GRID_W
    kh = min(WIN_H, rows)
    qg = q.reshape(B, rows, GRID_W, H, Dh)
    kg = k.reshape(B, rows, GRID_W, H, Dh)
    vg = v.reshape(B, rows, GRID_W, H, Dh)
    cols = jnp.arange(GRID_W)
    col_start = jnp.clip(cols - WIN_W // 2, 0, GRID_W - WIN_W)
    col_idx = col_start[:, None] + jnp.arange(WIN_W)[None, :]
    col_rel = col_idx - cols[:, None] + (WIN_W - 1)
    scale = Dh ** -0.5
    rpb = rpb.astype(jnp.float32)

    def row_block(i):
        row_start = jnp.clip(i - kh // 2, 0, rows - kh)
        q_i = lax.dynamic_index_in_dim(qg, i, axis=1, keepdims=False)
        k_rows = lax.dynamic_slice_in_dim(kg, row_start, kh, axis=1)
        v_rows = lax.dynamic_slice_in_dim(vg, row_start, kh, axis=1)
        k_nb = k_rows[:, :, col_idx]
        v_nb = v_rows[:, :, col_idx]
        s = jnp.einsum('bqhd,baqchd->bhqac', q_i, k_nb).astype(jnp.float32) * scale
        row_rel = row_start + jnp.arange(kh) - i + (WIN_H - 1)
        bias = rpb[:, row_rel[:, None, None], col_rel[None, :, :]]
        s = s + jnp.transpose(bias, (0, 2, 1, 3))[None]
        p = jax.nn.softmax(s.reshape(B, H, GRID_W, kh * WIN_W), axis=-1).reshape(B, H, GRID_W, kh, WIN_W)
        return jnp.einsum('bhqac,baqchd->bqhd', p.astype(v.dtype), v_nb)

    out = lax.map(row_block, jnp.arange(rows))
    return jnp.moveaxis(out, 0, 1).reshape(B, T, H * Dh)


def encoder_layer(x, ffn1_pre_g, ffn1_post_g, ffn1_w_gate, ffn1_w_up, ffn1_w_down,
                  mix_pre_g, w_in, rwkv_shift_mix,
                  decay_bias_fwd, decay_up_fwd, decay_bias_bwd, decay_up_bwd,
                  iclr_bias_fwd, iclr_up_fwd, iclr_bias_bwd, iclr_up_bwd, gate_up,
                  key_norm_scale, key_iclr_mix, bonus_scale, gn_w, gn_b,
                  nat_rpb, w_out, mix_post_g,
                  ffn2_pre_g, ffn2_post_g, ffn2_w_gate, ffn2_w_up, ffn2_w_down):
    B, T, _ = x.shape
    h = rmsnorm(x, ffn1_pre_g)
    x = x + FFN_RESIDUAL * rmsnorm(swiglu(h, ffn1_w_gate, ffn1_w_up, ffn1_w_down), ffn1_post_g)
    h = rmsnorm(x, mix_pre_g)
    z = h @ w_in
    z_rwkv = centred_shift(z[..., :RWKV_COLS], rwkv_shift_mix)
    o_rwkv = rwkv7_time_mix(z_rwkv, decay_bias_fwd, decay_up_fwd, decay_bias_bwd, decay_up_bwd,
                            iclr_bias_fwd, iclr_up_fwd, iclr_bias_bwd, iclr_up_bwd, gate_up,
                            key_norm_scale, key_iclr_mix, bonus_scale, gn_w, gn_b).astype(x.dtype)
    z_nat = z[..., RWKV_COLS:]
    q = z_nat[..., :NAT_WIDTH].reshape(B, T, NAT_HEADS, HEAD_DIM)
    k = z_nat[..., NAT_WIDTH:2 * NAT_WIDTH].reshape(B, T, NAT_HEADS, HEAD_DIM)
    v = z_nat[..., 2 * NAT_WIDTH:].reshape(B, T, NAT_HEADS, HEAD_DIM)
    o_nat = neighbourhood_attention(q, k, v, nat_rpb).astype(x.dtype)
    o = jnp.concatenate([o_rwkv, o_nat], axis=-1) @ w_out
    x = x + rmsnorm(o, mix_post_g)
    h = rmsnorm(x, ffn2_pre_g)
    x = x + FFN_RESIDUAL * rmsnorm(swiglu(h, ffn2_w_gate, ffn2_w_up, ffn2_w_down), ffn2_post_g)
    return x


def run_trunk(x, ffn1_pre_g, ffn1_post_g, ffn1_w_gate, ffn1_w_up, ffn1_w_down,
              mix_pre_g, w_in, rwkv_shift_mix,
              decay_bias_fwd, decay_up_fwd, decay_bias_bwd, decay_up_bwd,
              iclr_bias_fwd, iclr_up_fwd, iclr_bias_bwd, iclr_up_bwd, gate_up,
              key_norm_scale, key_iclr_mix, bonus_scale, gn_w, gn_b,
              nat_rpb, w_out, mix_post_g,
              ffn2_pre_g, ffn2_post_g, ffn2_w_gate, ffn2_w_up, ffn2_w_down):
    for l in range(DEPTH):
        x = encoder_layer(x, ffn1_pre_g[l], ffn1_post_g[l], ffn1_w_gate[l], ffn1_w_up[l], ffn1_w_down[l],
                          mix_pre_g[l], w_in[l], rwkv_shift_mix[l],
                          decay_bias_fwd[l], decay_up_fwd[l], decay_bias_bwd[l], decay_up_bwd[l],
                          iclr_bias_fwd[l], iclr_up_fwd[l], iclr_bias_bwd[l], iclr_up_bwd[l], gate_up[l],
                          key_norm_scale[l], key_iclr_mix[l], bonus_scale[l], gn_w[l], gn_b[l],
                          nat_rpb[l], w_out[l], mix_post_g[l],
                          ffn2_pre_g[l], ffn2_post_g[l], ffn2_w_gate[l], ffn2_w_up[l], ffn2_w_down[l])
    return x


def setup_inputs(seed: int = 0) -> dict:
    key = jax.random.key(seed)
    ks = jax.random.split(key, 32)
    f32 = jnp.float32

    def nrm(k, shape, scale):
        return jax.random.normal(k, shape, f32) * scale

    def gain(k, shape):
        return 1.0 + 0.02 * jax.random.normal(k, shape, f32)

    D, W = D_MODEL, RWKV_WIDTH
    return {
        'x_prompt': nrm(ks[0], (BATCH, SEQ, D), 1.0),
        'x_sample': nrm(ks[1], (DEC_BATCH, DEC_SEQ, D), 1.0),
        'ffn1_pre_g': gain(ks[2], (DEPTH, D)),
        'ffn1_post_g': gain(ks[3], (DEPTH, D)),
        'ffn1_w_gate': nrm(ks[4], (DEPTH, D, D_FF), D ** -0.5),
        'ffn1_w_up': nrm(ks[5], (DEPTH, D, D_FF), D ** -0.5),
        'ffn1_w_down': nrm(ks[6], (DEPTH, D_FF, D), D_FF ** -0.5),
        'mix_pre_g': gain(ks[7], (DEPTH, D)),
        'w_in': nrm(ks[8], (DEPTH, D, IN_COLS), D ** -0.5),
        'rwkv_shift_mix': jax.random.uniform(ks[9], (DEPTH, RWKV_COLS), f32),
        'decay_bias_fwd': jax.random.uniform(ks[10], (DEPTH, W), f32, -6.0, 1.0),
        'decay_up_fwd': nrm(ks[11], (DEPTH, DECAY_RANK, W), 0.1),
        'decay_bias_bwd': jax.random.uniform(ks[12], (DEPTH, W), f32, -6.0, 1.0),
        'decay_up_bwd': nrm(ks[13], (DEPTH, DECAY_RANK, W), 0.1),
        'iclr_bias_fwd': nrm(ks[14], (DEPTH, W), 0.5),
        'iclr_up_fwd': nrm(ks[15], (DEPTH, ICLR_RANK, W), ICLR_RANK ** -0.5),
        'iclr_bias_bwd': nrm(ks[16], (DEPTH, W), 0.5),
        'iclr_up_bwd': nrm(ks[17], (DEPTH, ICLR_RANK, W), ICLR_RANK ** -0.5),
        'gate_up': nrm(ks[18], (DEPTH, GATE_RANK, W), GATE_RANK ** -0.5),
        'key_norm_scale': 0.85 + 0.02 * jax.random.normal(ks[19], (DEPTH, W), f32),
        'key_iclr_mix': gain(ks[20], (DEPTH, W)),
        'bonus_scale': nrm(ks[21], (DEPTH, W), 0.1),
        'gn_w': gain(ks[22], (DEPTH, W)),
        'gn_b': nrm(ks[23], (DEPTH, W), 0.02),
        'nat_rpb': nrm(ks[24], (DEPTH, NAT_HEADS, 2 * WIN_H - 1, 2 * WIN_W - 1), 0.1),
        'w_out': nrm(ks[25], (DEPTH, D_MIX, D), D_MIX ** -0.5),
        'mix_post_g': gain(ks[26], (DEPTH, D)),
        'ffn2_pre_g': gain(ks[27], (DEPTH, D)),
        'ffn2_post_g': gain(ks[28], (DEPTH, D)),
        'ffn2_w_gate': nrm(ks[29], (DEPTH, D, D_FF), D ** -0.5),
        'ffn2_w_up': nrm(ks[30], (DEPTH, D, D_FF), D ** -0.5),
        'ffn2_w_down': nrm(ks[31], (DEPTH, D_FF, D), D_FF ** -0.5),
    }


def reference(x_prompt, x_sample, ffn1_pre_g, ffn1_post_g, ffn1_w_gate, ffn1_w_up, ffn1_w_down,
              mix_pre_g, w_in, rwkv_shift_mix,
              decay_bias_fwd, decay_up_fwd, decay_bias_bwd, decay_up_bwd,
              iclr_bias_fwd, iclr_up_fwd, iclr_bias_bwd, iclr_up_bwd, gate_up,
              key_norm_scale, key_iclr_mix, bonus_scale, gn_w, gn_b,
              nat_rpb, w_out, mix_post_g,
              ffn2_pre_g, ffn2_post_g, ffn2_w_gate, ffn2_w_up, ffn2_w_down):
    y_prompt = run_trunk(x_prompt, ffn1_pre_g, ffn1_post_g, ffn1_w_gate, ffn1_w_up, ffn1_w_down,
                         mix_pre_g, w_in, rwkv_shift_mix,
                         decay_bias_fwd, decay_up_fwd, decay_bias_bwd, decay_up_bwd,
                         iclr_bias_fwd, iclr_up_fwd, iclr_bias_bwd, iclr_up_bwd, gate_up,
                         key_norm_scale, key_iclr_mix, bonus_scale, gn_w, gn_b,
                         nat_rpb, w_out, mix_post_g,
                         ffn2_pre_g, ffn2_post_g, ffn2_w_gate, ffn2_w_up, ffn2_w_down)
    y_sample = run_trunk(x_sample, ffn1_pre_g, ffn1_post_g, ffn1_w_gate, ffn1_w_up, ffn1_w_down,
                         mix_pre_g, w_in, rwkv_shift_mix,
                         decay_bias_fwd, decay_up_fwd, decay_bias_bwd, decay_up_bwd,
                         iclr_bias_fwd, iclr_up_fwd, iclr_bias_bwd, iclr_up_bwd, gate_up,
                         key_norm_scale, key_iclr_mix, bonus_scale, gn_w, gn_b,
                         nat_rpb, w_out, mix_post_g,
                         ffn2_pre_g, ffn2_post_g, ffn2_w_gate, ffn2_w_up, ffn2_w_down)
    return (y_prompt, y_sample)
```

```python
import numpy as np
from contextlib import ExitStack
import concourse.bass as bass
import concourse.mybir as mybir
from concourse.bass_utils import run_bass_kernel_spmd

F32 = mybir.dt.float32
BF16 = mybir.dt.bfloat16
AF = mybir.ActivationFunctionType
ALU = mybir.AluOpType

D = 2048
DFF = 5632
NDC = 16
NFC = 44
TT = 512
UNIT = 2048
NUNIT = 3
NTOK = UNIT * NUNIT
RW_CH = 26
C0 = float(np.exp(-0.5))
NDS = 8


class Tk:
    __slots__ = ("w", "r")

    def __init__(self):
        self.w = None
        self.r = {}


class Prog:
    ENGS = ("sync", "gpsimd", "scalar", "vector", "tensor")

    def __init__(self, nc, es):
        self.nc = nc
        self.q = {k: [] for k in self.ENGS}
        self.csem = {}
        for k in ("scalar", "vector", "tensor", "gpsimd"):
            self.csem[k] = es.enter_context(nc.semaphore("c_" + k))
        self.ccnt = {k: 0 for k in self.csem}
        self.dpool = {}
        self.dnext = {}
        for k in ("sync", "gpsimd"):
            self.dpool[k] = [[es.enter_context(nc.semaphore("d_%s%d" % (k, i))), 0] for i in range(NDS)]
            self.dnext[k] = 0
        self.waited = {}

    def op(self, eng, fn, reads=(), writes=(), dma=False, sig=True, nowait=False, reg=True):
        deps = {}

        def need(tok):
            if tok is None:
                return
            s, v = tok
            if deps.get(s, 0) < v:
                deps[s] = v

        if not nowait:
            for t in reads:
                need(t.w)
            for t in writes:
                need(t.w)
                for s, v in t.r.items():
                    need((s, v))
        tok = None
        inc = 0
        if dma:
            slot = self.dpool[eng][self.dnext[eng]]
            self.dnext[eng] = (self.dnext[eng] + 1) % NDS
            if slot[1] > 0:
                need((slot[0], slot[1]))
            slot[1] += 16
            tok = (slot[0], slot[1])
            inc = 16
        elif sig:
            self.ccnt[eng] += 1
            tok = (self.csem[eng], self.ccnt[eng])
            inc = 1
        waits = []
        for s, v in deps.items():
            key = (eng, s)
            if self.waited.get(key, 0) >= v:
                continue
            if eng == "tensor" and s is self.csem["tensor"]:
                continue
            self.waited[key] = v
            waits.append((s, v))
        if tok is not None and reg:
            for t in reads:
                if t.r.get(tok[0], 0) < tok[1]:
                    t.r[tok[0]] = tok[1]
            for t in writes:
                t.w = tok
                t.r = {}
        self.q[eng].append((waits, fn, tok, inc))
        return tok

    def mm_group(self, mms, reads, writes):
        n = len(mms)
        for i, (o, l, r) in enumerate(mms):
            fn = (lambda e, o=o, l=l, r=r, st=(i == 0), sp=(i == n - 1): e.matmul(o, l, r, start=st, stop=sp))
            if n == 1:
                self.op("tensor", fn, reads, writes)
            elif i == 0:
                self.op("tensor", fn, reads, writes, sig=False)
            elif i == n - 1:
                self.op("tensor", fn, reads, writes, nowait=True)
            else:
                self.op("tensor", fn, (), (), sig=False, nowait=True)

    def barrier(self):
        toks = []
        for k, s in self.csem.items():
            if self.ccnt[k] > 0:
                toks.append((s, self.ccnt[k]))
        for k in self.dpool:
            for s, v in self.dpool[k]:
                if v > 0:
                    toks.append((s, v))
        for eng in self.ENGS:
            waits = []
            for s, v in toks:
                if self.waited.get((eng, s), 0) >= v:
                    continue
                self.waited[(eng, s)] = v
                waits.append((s, v))
            self.q[eng].append((waits, None, None, 0))

    def finish(self):
        waits = []
        for k in self.dpool:
            for s, v in self.dpool[k]:
                if v > 0:
                    waits.append((s, v))
        self.q["sync"].append((waits, None, None, 0))

    def emit(self, eng, e):
        for waits, fn, tok, inc in self.q[eng]:
            for s, v in waits:
                e.wait_ge(s, v)
            if fn is None:
                continue
            inst = fn(e)
            if tok is not None:
                inst.then_inc(tok[0], inc)


PC = {}
_o = 0
for _n, _w in (("ffn1_pre", 16), ("ffn1_post", 16), ("mix_pre", 16), ("mix_post", 16), ("ffn2_pre", 16),
               ("ffn2_post", 16), ("mu", 26), ("dbf", 8), ("dbb", 8), ("ibf", 8), ("ibb", 8), ("kns", 8),
               ("kim", 8), ("bsc", 8), ("gnw", 8), ("gnb", 8), ("flag", 1)):
    PC[_n] = _o
    _o += _w
NPCOL = _o
DC = {}
_o = 0
for _n, _w in (("ffn1_posth", 16), ("ffn2_posth", 16), ("hmu", 26), ("omu", 26), ("omk", 8), ("hbs", 8)):
    DC[_n] = _o
    _o += _w
NDCOL = _o

CC = {"ident": 0, "ones": 128, "bones": 256, "m4f": 384, "m4b": 896, "mlf": 1408, "mlb": 1536}
NCCOL = 1664


def build_program(dbg=None):
    dbg = dbg or {}
    ntiles = dbg.get("ntiles", NTOK // TT)
    do_a = dbg.get("A", True)
    do_b = dbg.get("B", True)
    do_c = dbg.get("C", True)
    nc = bass.Bass("TRN2", target_bir_lowering=False)
    ext = lambda n, s, dt=F32: nc.dram_tensor(n, s, dt, kind="ExternalInput").ap()
    x_d = ext("x", [NTOK, D])
    par_d = ext("params", [128, NPCOL])
    con_d = ext("consts", [128, NCCOL])
    w1g = ext("ffn1_w_gate", [D, DFF]); w1u = ext("ffn1_w_up", [D, DFF]); w1d = ext("ffn1_w_down", [DFF, D])
    w2g = ext("ffn2_w_gate", [D, DFF]); w2u = ext("ffn2_w_up", [D, DFF]); w2d = ext("ffn2_w_down", [DFF, D])
    win = ext("w_in", [D, 6400]); wout = ext("w_out", [D, D])
    lupf_d = ext("lupf", [128, 1024]); lupb_d = ext("lupb", [128, 1024]); gup_d = ext("gup", [128, 1024])
    yf_d = nc.dram_tensor("yf", [8, 128, NTOK], F32, kind="ExternalOutput").ap()
    G_d = ext("natG", [8, 128, 2, 2, 7, 64])
    S_d = ext("natS", [7, 16, 128, 6, 64])
    y_d = nc.dram_tensor("y", [NTOK, D], F32, kind="ExternalOutput").ap()
    skind = "ExternalOutput"
    x1T_d = nc.dram_tensor("x1T", [NDC, 128, NTOK], F32, kind=skind).ap()
    zr_d = nc.dram_tensor("zr", [RW_CH, 128, NTOK], F32, kind=skind).ap()
    zqk_d = nc.dram_tensor("zqk", [16, 128, NTOK], BF16, kind=skind).ap()
    zv_d = nc.dram_tensor("zv", [NTOK, 1024], BF16, kind=skind).ap()
    if dbg.get("oT_in"):
        oT_d = ext("oT", [16, 128, NTOK], BF16)
    else:
        oT_d = nc.dram_tensor("oT", [16, 128, NTOK], BF16, kind=skind).ap()

    es = ExitStack()
    P = Prog(nc, es)
    sb = lambda n, s, dt=F32: es.enter_context(nc.sbuf_tensor(n, s, dt))
    par = sb("par", [128, NPCOL]); der = sb("der", [128, NDCOL]); con = sb("con", [128, NCCOL])
    identb = sb("identb", [128, 128], BF16)
    onesb128 = sb("onesb128", [128, 128], BF16)
    epsr = sb("epsr", [128, 1]); t_epsr = Tk()
    P.op("vector", lambda e: e.memset(epsr[:], 1e-6), (), (t_epsr,))
    t_par, t_der, t_con, t_identb = Tk(), Tk(), Tk(), Tk()
    P.op("sync", lambda e: e.dma_start(out=par[:], in_=par_d[:, :]), (), (t_par,), dma=True)
    P.op("sync", lambda e: e.dma_start(out=con[:], in_=con_d[:, :]), (), (t_con,), dma=True)

    def pcol(name, i=0, n=1):
        return par[:, PC[name] + i:PC[name] + i + n]

    def dcol(name, i=0, n=1):
        return der[:, DC[name] + i:DC[name] + i + n]

    def dslice(name, n):
        return der[:, DC[name]:DC[name] + n]

    def pslice(name, n):
        return par[:, PC[name]:PC[name] + n]

    V = "vector"
    P.op(V, lambda e: e.tensor_scalar(out=dslice("ffn1_posth", 16), in0=pslice("ffn1_post", 16), scalar1=0.5,
                                      scalar2=None, op0=ALU.mult), (t_par,), (t_der,))
    P.op(V, lambda e: e.tensor_scalar(out=dslice("ffn2_posth", 16), in0=pslice("ffn2_post", 16), scalar1=0.5,
                                      scalar2=None, op0=ALU.mult), (t_par,), (t_der,))
    P.op(V, lambda e: e.tensor_scalar(out=dslice("hmu", 26), in0=pslice("mu", 26), scalar1=0.5,
                                      scalar2=None, op0=ALU.mult), (t_par,), (t_der,))
    P.op(V, lambda e: e.tensor_scalar(out=dslice("omu", 26), in0=pslice("mu", 26), scalar1=-1.0,
                                      scalar2=1.0, op0=ALU.mult, op1=ALU.add), (t_par,), (t_der,))
    P.op(V, lambda e: e.tensor_scalar(out=dslice("omk", 8), in0=pslice("kim", 8), scalar1=-1.0,
                                      scalar2=1.0, op0=ALU.mult, op1=ALU.add), (t_par,), (t_der,))
    P.op(V, lambda e: e.tensor_scalar(out=dslice("hbs", 8), in0=pslice("bsc", 8), scalar1=0.5,
                                      scalar2=None, op0=ALU.mult), (t_par,), (t_der,))
    P.op(V, lambda e: e.tensor_copy(out=identb[:], in_=con[:, 0:128]), (t_con,), (t_identb,))
    P.op(V, lambda e: e.tensor_copy(out=onesb128[:], in_=con[:, 128:256]), (t_con,), (t_identb,))
    identf = con[:, CC["ident"]:CC["ident"] + 128]
    onesf = con[:, CC["ones"]:CC["ones"] + 128]
    bonesf = con[:, CC["bones"]:CC["bones"] + 128]

    def dense_phases(do_a, do_c, tiles):
        sfx = "A" if do_a else "C"
        with ExitStack() as ds:
            sbd = lambda n, s, dt=F32: ds.enter_context(nc.sbuf_tensor(n + sfx, s, dt))
            psd = lambda n, s, dt=F32: ds.enter_context(nc.psum_tensor(n + sfx, s, dt))
            big = sbd("big", [128, 8192])
            xT = sbd("xT", [128, NDC, TT])
            hT = sbd("hT", [128, NDC, TT], BF16)
            aT = sbd("aT", [128, NFC, TT], BF16)
            wgu = [sbd("wgu%d" % i, [128, NDC, 256], BF16) for i in range(3)]
            wdb = [sbd("wdb%d" % i, [128, NFC, 128], BF16) for i in range(2)]
            sq = [sbd("sq%d" % i, [128, TT], BF16) for i in range(2)]
            sl = [sbd("sl%d" % i, [128, TT]) for i in range(2)]
            rstd = sbd("rstd", [128, TT]); rtmp = sbd("rtmp", [128, TT])
            zst = [sbd("zst%d" % i, [128, 2, TT]) for i in range(1)]
            zsb = [sbd("zsb%d" % i, [128, 2, TT], BF16) for i in range(1)]
            vst = [sbd("vst%d" % i, [128, 4, 256], BF16) for i in range(1)]
            psb = [psd("psb%d" % i, [128, TT]) for i in range(8)]
            t_big = [Tk() for _ in range(16)]
            t_xT = [Tk() for _ in range(NDC)]
            t_hT = [Tk() for _ in range(NDC)]
            t_aT = [Tk() for _ in range(NFC)]
            t_wgu = [Tk() for _ in range(3)]
            t_wdb = [Tk() for _ in range(2)]
            t_sq = [Tk(), Tk()]; t_sl = [Tk(), Tk()]
            t_rstd, t_rtmp = Tk(), Tk()
            t_zst = [Tk(), Tk()]; t_zsb = [Tk(), Tk()]; t_vst = [Tk(), Tk()]
            t_ps = [Tk() for _ in range(8)]
            cnt = {"wgu": 0, "wdb": 0, "sq": 0, "sl": 0, "zst": 0, "ps_g": 0, "ps_u": 0, "ps_d": 0, "vst": 0}

            def rot(name, n):
                i = cnt[name] % n
                cnt[name] += 1
                return i

            def fT(dc):
                return big[:, dc * TT:(dc + 1) * TT]

            def stat_begin():
                pass

            def sumsq_chunk(src_ap, src_tk, dc, from_psum_eng="scalar"):
                i = rot("sq", 2)
                P.op("scalar", lambda e: e.activation(out=sq[i][:], in_=src_ap, func=AF.Square), (src_tk,), (t_sq[i],))
                P.op("tensor", lambda e: e.matmul(psb[6][:], onesb128[:], sq[i][:], start=(dc == 0), stop=(dc == NDC - 1)),
                     (t_sq[i], t_identb), (t_ps[6],))

            def make_rstd():
                P.op("scalar", lambda e: e.activation(out=rtmp[:], in_=psb[6][:], func=AF.Ln, bias=epsr[:, 0:1],
                                                      scale=1.0 / D), (t_ps[6], t_epsr), (t_rtmp,))
                P.op("scalar", lambda e: e.activation(out=rstd[:], in_=rtmp[:], func=AF.Exp, scale=-0.5), (t_rtmp,), (t_rstd,))

            def make_h(gname):
                for dc in range(NDC):
                    P.op(V, lambda e, dc=dc: e.scalar_tensor_tensor(out=hT[:, dc, :], in0=xT[:, dc, :],
                                                                    scalar=pcol(gname, dc), in1=rstd[:],
                                                                    op0=ALU.mult, op1=ALU.mult),
                         (t_xT[dc], t_rstd, t_par), (t_hT[dc],))

            def load_w_cols(wd, c0, ncols):
                i = rot("wgu", 3)
                src = wd.rearrange("(c p) f -> p c f", p=128)[:, :, c0:c0 + ncols]
                P.op("gpsimd", lambda e: e.dma_start(out=wgu[i][:, :, 0:ncols], in_=src), (), (t_wgu[i],), dma=True)
                return i

            def residual_update(gcols_ap_fn, src_fT=True):
                for dc in range(NDC):
                    P.op(V, lambda e, dc=dc: e.scalar_tensor_tensor(out=fT(dc), in0=fT(dc), scalar=gcols_ap_fn(dc),
                                                                    in1=rstd[:], op0=ALU.mult, op1=ALU.mult),
                         (t_big[dc], t_rstd, t_par, t_der), (t_big[dc],))
                    P.op(V, lambda e, dc=dc: e.tensor_tensor(out=xT[:, dc, :], in0=xT[:, dc, :], in1=fT(dc), op=ALU.add),
                         (t_big[dc], t_xT[dc]), (t_xT[dc],))

            def ffn(pre_name, posth_name, wg, wu, wd):
                for dc in range(NDC):
                    sumsq_chunk(xT[:, dc, :], t_xT[dc], dc)
                make_rstd()
                make_h(pre_name)
                for fb in range(NFC // 2):
                    ig = load_w_cols(wg, fb * 256, 256)
                    iu = load_w_cols(wu, fb * 256, 256)
                    for j in range(2):
                        fc = fb * 2 + j
                        pg = rot("ps_g", 2)
                        pu = 2 + rot("ps_u", 2)
                        P.mm_group([(psb[pg][:], wgu[ig][:, dc, j * 128:(j + 1) * 128], hT[:, dc, :]) for dc in range(NDC)],
                                   [t_wgu[ig]] + t_hT, [t_ps[pg]])
                        P.mm_group([(psb[pu][:], wgu[iu][:, dc, j * 128:(j + 1) * 128], hT[:, dc, :]) for dc in range(NDC)],
                                   [t_wgu[iu]] + t_hT, [t_ps[pu]])
                        si = rot("sl", 2)
                        P.op("scalar", lambda e, pg=pg, si=si: e.activation(out=sl[si][:], in_=psb[pg][:], func=AF.Silu),
                             (t_ps[pg],), (t_sl[si],))
                        P.op(V, lambda e, pu=pu, si=si, fc=fc: e.tensor_tensor(out=aT[:, fc, :], in0=sl[si][:], in1=psb[pu][:],
                                                                               op=ALU.mult),
                             (t_sl[si], t_ps[pu]), (t_aT[fc],))
                for dc in range(NDC):
                    i = rot("wdb", 2)
                    src = wd.rearrange("(c p) d -> p c d", p=128)[:, :, dc * 128:(dc + 1) * 128]
                    P.op("gpsimd", lambda e, i=i, src=src: e.dma_start(out=wdb[i][:], in_=src), (), (t_wdb[i],), dma=True)
                    pd = 4 + rot("ps_d", 2)
                    P.mm_group([(psb[pd][:], wdb[i][:, fc, :], aT[:, fc, :]) for fc in range(NFC)],
                               [t_wdb[i]] + t_aT, [t_ps[pd]])
                    P.op(V, lambda e, dc=dc, pd=pd: e.tensor_copy(out=fT(dc), in_=psb[pd][:]), (t_ps[pd],), (t_big[dc],))
                    sumsq_chunk(fT(dc), t_big[dc], dc)
                make_rstd()
                residual_update(lambda dc: dcol(posth_name, dc))

            for tt in tiles:
                t0 = tt * TT
                if do_a:
                    def x_load(tq):
                        src = x_d[tq * TT:(tq + 1) * TT, :].rearrange("(b p) d -> p b d", p=128)
                        P.op("sync", lambda e, src=src: e.dma_start(out=big[:, :].rearrange("p (b d) -> p b d", b=4), in_=src),
                             (), t_big, dma=True)

                    def x_transpose():
                        for dc in range(NDC):
                            for tb in range(4):
                                P.op("tensor", lambda e, dc=dc, tb=tb: e.transpose(
                                    psb[7][:, tb * 128:(tb + 1) * 128],
                                    big[:, tb * 2048 + dc * 128: tb * 2048 + (dc + 1) * 128], identf),
                                     t_big + [t_con], (t_ps[7],), sig=(tb == 3), nowait=(tb != 0))
                            P.op("scalar", lambda e, dc=dc: e.copy(out=xT[:, dc, :], in_=psb[7][:]), (t_ps[7],), (t_xT[dc],))
                    if tt == tiles[0]:
                        x_load(tt)
                        x_transpose()
                    ffn("ffn1_pre", "ffn1_posth", w1g, w1u, w1d)
                    dst = x1T_d.rearrange("c p t -> p c t")[:, :, t0:t0 + TT]
                    P.op("sync", lambda e, dst=dst: e.dma_start(out=dst, in_=xT[:]), t_xT, (), dma=True)
                    for dc in range(NDC):
                        sumsq_chunk(xT[:, dc, :], t_xT[dc], dc)
                    make_rstd()
                    make_h("mix_pre")
                    has_next = (tt != tiles[-1])
                    if has_next:
                        x_load(tt + 1)
                    for cb in range(25):
                        if has_next and cb == 10:
                            x_transpose()
                        iw = load_w_cols(win, cb * 256, 256)
                        if cb < 21:
                            zi = 0
                            for j in range(2):
                                pd = 4 + rot("ps_d", 2)
                                P.mm_group([(psb[pd][:], wgu[iw][:, dc, j * 128:(j + 1) * 128], hT[:, dc, :]) for dc in range(NDC)],
                                           [t_wgu[iw]] + t_hT, [t_ps[pd]])
                                if cb < 13:
                                    P.op("scalar", lambda e, zi=zi, j=j, pd=pd: e.copy(out=zst[zi][:, j, :], in_=psb[pd][:]),
                                         (t_ps[pd],), (t_zst[zi],))
                                else:
                                    P.op("scalar", lambda e, zi=zi, j=j, pd=pd: e.copy(out=zsb[zi][:, j, :], in_=psb[pd][:]),
                                         (t_ps[pd],), (t_zsb[zi],))
                            if cb < 13:
                                dst = zr_d.rearrange("c p t -> p c t")[:, cb * 2:cb * 2 + 2, t0:t0 + TT]
                                P.op("sync", lambda e, dst=dst, zi=zi: e.dma_start(out=dst, in_=zst[zi][:]), (t_zst[zi],), (), dma=True)
                            else:
                                c2 = (cb - 13) * 2
                                dst = zqk_d.rearrange("c p t -> p c t")[:, c2:c2 + 2, t0:t0 + TT]
                                P.op("sync", lambda e, dst=dst, zi=zi: e.dma_start(out=dst, in_=zsb[zi][:]), (t_zsb[zi],), (), dma=True)
                        else:
                            vb = cb - 21
                            vi = 0
                            for tb in range(4):
                                pd = 4 + rot("ps_d", 2)
                                P.mm_group([(psb[pd][:, 0:256], hT[:, dc, tb * 128:(tb + 1) * 128], wgu[iw][:, dc, :]) for dc in range(NDC)],
                                           [t_wgu[iw]] + t_hT, [t_ps[pd]])
                                P.op("scalar", lambda e, vi=vi, tb=tb, pd=pd: e.copy(out=vst[vi][:, tb, :], in_=psb[pd][:, 0:256]),
                                     (t_ps[pd],), (t_vst[vi],))
                            dst = zv_d[t0:t0 + TT, vb * 256:(vb + 1) * 256].rearrange("(b p) f -> p b f", p=128)
                            P.op("sync", lambda e, dst=dst, vi=vi: e.dma_start(out=dst, in_=vst[vi][:]), (t_vst[vi],), (), dma=True)
                if do_c:
                    src = x1T_d.rearrange("c p t -> p c t")[:, :, t0:t0 + TT]
                    P.op("sync", lambda e, src=src: e.dma_start(out=xT[:], in_=src), (), t_xT, dma=True)
                    src = oT_d.rearrange("c p t -> p c t")[:, :, t0:t0 + TT]
                    P.op("sync", lambda e, src=src: e.dma_start(out=hT[:], in_=src), (), t_hT, dma=True)
                    for db in range(8):
                        iw = load_w_cols(wout, db * 256, 256)
                        for j in range(2):
                            dc = db * 2 + j
                            pd = 4 + rot("ps_d", 2)
                            P.mm_group([(psb[pd][:], wgu[iw][:, mc, j * 128:(j + 1) * 128], hT[:, mc, :]) for mc in range(NDC)],
                                       [t_wgu[iw]] + t_hT, [t_ps[pd]])
                            P.op(V, lambda e, dc=dc, pd=pd: e.tensor_copy(out=fT(dc), in_=psb[pd][:]), (t_ps[pd],), (t_big[dc],))
                            sumsq_chunk(fT(dc), t_big[dc], dc)
                    make_rstd()
                    residual_update(lambda dc: pcol("mix_post", dc))
                    ffn("ffn2_pre", "ffn2_posth", w2g, w2u, w2d)
                    for tb in range(4):
                        for d4 in range(4):
                            for j in range(4):
                                dc = d4 * 4 + j
                                P.op("tensor", lambda e, dc=dc, tb=tb, j=j: e.transpose(
                                    psb[7][:, j * 128:(j + 1) * 128], xT[:, dc, tb * 128:(tb + 1) * 128], identf),
                                     [t_xT[dc], t_con], (t_ps[7],), sig=(j == 3), nowait=False)
                            P.op("scalar", lambda e, tb=tb, d4=d4: e.copy(
                                out=big[:, tb * 2048 + d4 * 512: tb * 2048 + (d4 + 1) * 512], in_=psb[7][:]),
                                 (t_ps[7],), t_big)
                    dst = y_d[t0:t0 + TT, :].rearrange("(b p) d -> p b d", p=128)
                    P.op("sync", lambda e, dst=dst: e.dma_start(out=dst, in_=big[:, :].rearrange("p (b d) -> p b d", b=4)),
                         t_big, (), dma=True)

    if do_a:
        dense_phases(True, False, list(range(ntiles)))
        P.barrier()
    if do_b:
        mixer_phase(nc, P, dbg, locals())
        P.barrier()
    if do_c:
        dense_phases(False, True, list(range(ntiles)))
    P.finish()
    with nc.Block() as block:
        @block.sync
        def _(e):
            P.emit("sync", e)

        @block.gpsimd
        def _(e):
            P.emit("gpsimd", e)

        @block.scalar
        def _(e):
            P.emit("scalar", e)

        @block.vector
        def _(e):
            P.emit("vector", e)

        @block.tensor
        def _(e):
            P.emit("tensor", e)
    es.close()
    return nc


def mixer_phase(nc, P, dbg, env):
    units = dbg.get("units", list(range(NUNIT)))
    if dbg.get("nat", True):
        nat_phase(nc, P, dbg, env, units)
        P.barrier()
    if dbg.get("rwkv", True):
        rwkv_phase(nc, P, dbg, env, units)


NEG = -30000.0
SPECIAL = [(0, 29), (0, 30), (0, 31), (1, 0), (1, 1), (1, 2), (1, 3)]


def nat_phase(nc, P, dbg, env, units):
    zqk_d, zv_d, oT_d, G_d, S_d, con, t_con = (env[k] for k in ("zqk_d", "zv_d", "oT_d", "G_d", "S_d", "con", "t_con"))
    V = "vector"
    with ExitStack() as ds:
        sbd = lambda n, s, dt=F32: ds.enter_context(nc.sbuf_tensor("n_" + n, s, dt))
        psd = lambda n, s, dt=F32: ds.enter_context(nc.psum_tensor("n_" + n, s, dt))
        qT = [sbd("qT%d" % i, [128, UNIT], BF16) for i in range(2)]
        kT = [sbd("kT%d" % i, [128, UNIT + 512], BF16) for i in range(2)]
        vx = [sbd("vx%d" % i, [128, 20, 2, 80], BF16) for i in range(2)]
        vxo = [sbd("vxo%d" % i, [128, 20, 2, 80], BF16) for i in range(2)]
        G = [sbd("G%d" % i, [128, 2, 2, 7, 64]) for i in range(2)]
        Ssp = [sbd("S%d" % i, [128, 6, 64]) for i in range(2)]
        sbuf = [sbd("sb%d" % i, [128, 384]) for i in range(2)]
        pT = [sbd("pT%d" % i, [128, 384], BF16) for i in range(2)]
        onesb = sbd("onesb", [128, 64], BF16)
        rs = sbd("rs", [128, 512])
        srow = sbd("srow", [128, 512])
        ob = [sbd("ob%d" % i, [128, 512], BF16) for i in range(2)]
        ps_s = [psd("s%d" % i, [128, 512]) for i in range(2)]
        ps_o = [psd("o%d" % i, [128, 512]) for i in range(4)]
        ps_bc = psd("bc", [128, 512])
        t_qT = [Tk(), Tk()]; t_kT = [Tk(), Tk()]; t_vx = [Tk(), Tk()]; t_G = [Tk(), Tk()]; t_S = [Tk(), Tk()]
        t_sb = [Tk(), Tk()]; t_pT = [Tk(), Tk()]; t_ones = Tk(); t_rs = Tk(); t_ob = [Tk(), Tk()]
        t_ps_s = [Tk(), Tk()]; t_ps_o = [Tk() for _ in range(4)]; t_bc = Tk(); t_srow = Tk()
        P.op(V, lambda e: e.tensor_copy(out=onesb[:], in_=con[:, CC["ones"]:CC["ones"] + 64]), (t_con,), (t_ones,))
        P.op(V, lambda e: e.memset(kT[0][:], 0.0), (), (t_kT[0],))
        P.op(V, lambda e: e.memset(kT[1][:], 0.0), (), (t_kT[1],))
        for vt, tt_ in ((vx[0], t_vx[0]), (vx[1], t_vx[1]), (vxo[0], t_vx[0]), (vxo[1], t_vx[1])):
            P.op(V, lambda e, vt=vt: e.memset(vt[:], 0.0), (), (tt_,))
            P.op(V, lambda e, vt=vt: e.memset(vt[:, :, :, 64:65], 1.0), (), (tt_,))
        cnt = {"b": 0, "s": 0, "sp": 0, "g": 0, "ob": 0}
        for u in units:
            U0 = u * UNIT
            for hp in range(8):
                b = cnt["b"] % 2
                cnt["b"] += 1
                P.op("sync", lambda e, b=b, hp=hp, U0=U0: e.dma_start(out=qT[b][:], in_=zqk_d[hp, :, U0:U0 + UNIT]),
                     (), (t_qT[b],), dma=True)
                lo = U0 - 256 if u == 1 else U0
                hi = U0 + UNIT + 256 if u == 0 else U0 + UNIT
                P.op("sync", lambda e, b=b, hp=hp, lo=lo, hi=hi, U0=U0: e.dma_start(
                    out=kT[b][:, lo - (U0 - 256):hi - (U0 - 256)], in_=zqk_d[8 + hp, :, lo:hi]), (), (t_kT[b],), dma=True)
                blo = (lo - (U0 - 256)) // 128
                nb = (hi - lo) // 128
                for jh in range(2):
                    P.op("sync", lambda e, b=b, hp=hp, lo=lo, hi=hi, blo=blo, nb=nb, jh=jh: e.dma_start(
                        out=vx[b][:, blo:blo + nb, jh, 0:64],
                        in_=zv_d[lo:hi, hp * 128 + jh * 64:hp * 128 + (jh + 1) * 64].rearrange("(k p) f -> p k f", p=128)),
                         (), (t_vx[b],), dma=True)
                lob = lo - (U0 - 256)
                hib = hi - (U0 - 256)
                kmin = -((64 - lob) // 128)
                kmax = (hib - 64) // 128 - 1
                g0 = (U0 - 256) + 64 + 128 * kmin
                g1 = (U0 - 256) + 64 + 128 * (kmax + 1)
                for jh in range(2):
                    P.op("sync", lambda e, b=b, hp=hp, g0=g0, g1=g1, kmin=kmin, kmax=kmax, jh=jh: e.dma_start(
                        out=vxo[b][:, kmin:kmax + 1, jh, 0:64],
                        in_=zv_d[g0:g1, hp * 128 + jh * 64:hp * 128 + (jh + 1) * 64].rearrange("(k p) f -> p k f", p=128)),
                         (), (t_vx[b],), dma=True)
                P.op("sync", lambda e, b=b, hp=hp: e.dma_start(out=G[b][:], in_=G_d[hp]), (), (t_G[b],), dma=True)
                for i8 in range(4):
                    po = cnt["ob"] % 2
                    cnt["ob"] += 1
                    pending = [None]
                    for ii in range(8):
                        i = i8 * 8 + ii
                        for j in range(2):
                            hb = 64 * j
                            h = hp * 2 + j
                            if (u, i) in SPECIAL:
                                spi = SPECIAL.index((u, i))
                                nch = 6
                                er0 = 24 if u == 0 else -4
                                si = cnt["sp"] % 2
                                cnt["sp"] += 1
                                P.op("sync", lambda e, si=si, spi=spi, h=h: e.dma_start(out=Ssp[si][:], in_=S_d[spi, h]),
                                     (), (t_S[si],), dma=True)
                                tab = Ssp[si][:, :, :]
                                t_tab = t_S[si]
                            else:
                                ws = min(max(i - 4, 0), 24)
                                off = i - ws
                                nch = 4
                                er0 = ws
                                m0 = 7 - off
                                tab = G[b][:, j, m0 % 2, m0 // 2:m0 // 2 + 4, :]
                                t_tab = t_G[b]
                            ks = (er0 + 4) * 64
                            s_i = cnt["s"] % 2
                            cnt["s"] += 1
                            for c in range(nch):
                                P.op("tensor", lambda e, s_i=s_i, c=c, b=b, hb=hb, ks=ks, i=i: e.matmul(
                                    ps_s[s_i][:, c * 64:(c + 1) * 64], kT[b][hb:hb + 64, ks + c * 128:ks + (c + 1) * 128],
                                    qT[b][hb:hb + 64, i * 64:(i + 1) * 64], start=True, stop=True),
                                     (t_kT[b], t_qT[b]), (t_ps_s[s_i],), sig=(c == nch - 1))
                            n = nch * 64
                            P.op(V, lambda e, s_i=s_i, n=n, nch=nch, tab=tab: e.scalar_tensor_tensor(
                                out=sbuf[s_i][:, 0:n].rearrange("p (c q) -> p c q", c=nch),
                                in0=ps_s[s_i][:, 0:n].rearrange("p (c q) -> p c q", c=nch), scalar=0.125, in1=tab,
                                op0=ALU.mult, op1=ALU.add), (t_ps_s[s_i], t_tab), (t_sb[s_i],))
                            P.op("scalar", lambda e, s_i=s_i, n=n: e.activation(out=pT[s_i][:, 0:n], in_=sbuf[s_i][:, 0:n], func=AF.Exp),
                                 (t_sb[s_i],), (t_pT[s_i],))
                            def pv_stage(nch=nch, er0=er0, po=po, j=j, ii=ii, b=b, s_i=s_i):
                                pso = ps_o[po * 2 + j]
                                tpso = t_ps_o[po * 2 + j]
                                for c in range(nch):
                                    R = er0 + 4 + 2 * c
                                    vsrc = vx[b] if R % 2 == 0 else vxo[b]
                                    blk = R // 2
                                    P.op("tensor", lambda e, vsrc=vsrc, blk=blk, c=c: e.matmul(
                                        pso[0:65, ii * 64:(ii + 1) * 64], vsrc[:, blk, j, 0:65],
                                        pT[s_i][:, c * 64:(c + 1) * 64], start=(c == 0), stop=(c == nch - 1)),
                                         (t_vx[b], t_pT[s_i]), (tpso,), sig=(c == nch - 1))
                            if pending[0] is not None:
                                pending[0]()
                            pending[0] = pv_stage
                    if pending[0] is not None:
                        pending[0]()
                        pending[0] = None
                    for j in range(2):
                        pso = ps_o[po * 2 + j]
                        tpso = t_ps_o[po * 2 + j]
                        P.op("scalar", lambda e, pso=pso: e.copy(out=srow[64:65, :], in_=pso[64:65, :]), (tpso,), (t_srow,))
                        P.mm_group([(ps_bc[0:64, :], con[64:65, CC["ones"]:CC["ones"] + 64], srow[64:65, :])], (t_con, t_srow), (t_bc,))
                        P.op("scalar", lambda e: e.activation(out=rs[0:64, :], in_=ps_bc[0:64, :], func=AF.Ln), (t_bc,), (t_rs,))
                        P.op("scalar", lambda e: e.activation(out=rs[0:64, :], in_=rs[0:64, :], func=AF.Exp, scale=-1.0), (t_rs,), (t_rs,))
                        ob_ = ob[j]
                        P.op(V, lambda e, pso=pso, ob_=ob_: e.tensor_tensor(out=ob_[0:64, :], in0=pso[0:64, :], in1=rs[0:64, :], op=ALU.mult),
                             (tpso, t_rs), (t_ob[j],))
                        P.op("sync", lambda e, ob_=ob_, hp=hp, U0=U0, i8=i8, j=j: e.dma_start(
                            out=oT_d[8 + hp, 64 * j:64 * j + 64, U0 + i8 * 512:U0 + (i8 + 1) * 512], in_=ob_[0:64, :]), (t_ob[j],), (), dma=True)


def rwkv_phase(nc, P, dbg, env, units):
    zr_d, oT_d, yf_d, con, t_con, t_par, t_der, identb, t_identb = (env[k] for k in (
        "zr_d", "oT_d", "yf_d", "con", "t_con", "t_par", "t_der", "identb", "t_identb"))
    lup_dd = {"f": env["lupf_d"], "b": env["lupb_d"]}
    gup_d = env["gup_d"]
    pcol, dcol = env["pcol"], env["dcol"]
    T = UNIT
    V = "vector"
    order = [(0, "f"), (1, "f"), (2, "f"), (2, "b"), (1, "b"), (0, "b")]
    order = [(u, d) for (u, d) in order if u in units]
    if dbg.get("dump"):
        order = order if dbg.get("dump2") else order[dbg.get("dump_pass", 0):][:1]
    bonesf = con[:, CC["bones"]:CC["bones"] + 128]
    onesf = con[:, CC["ones"]:CC["ones"] + 128]
    with ExitStack() as ds:
        sbd = lambda n, s, dt=F32: ds.enter_context(nc.sbuf_tensor("r_" + n, s, dt))
        psd = lambda n, s, dt=F32: ds.enter_context(nc.psum_tensor("r_" + n, s, dt))
        F = [sbd("F%d" % i, [128, T + 2]) for i in range(10)]
        tF = [Tk() for _ in range(10)]
        TMP0, TMP1, KK, R, K, VV, E, A, L, YA = range(10)
        B = TMP1
        AR = sbd("AR", [128, 16, 256], BF16); tAR = Tk()
        Bf = sbd("Bf", [128, T], BF16); tBf = Tk()
        Kf = sbd("Kf", [128, T], BF16); tKf = Tk()
        vb = sbd("vb", [128, T], BF16); tvb = Tk()
        Btm = sbd("Btm", [128, 16, 128], BF16); tBtm = Tk()
        Ktm = sbd("Ktm", [128, 16, 128], BF16); tKtm = Tk()
        Vtm = sbd("Vtm", [128, 16, 128], BF16); tVtm = Tk()
        Am = sbd("Am", [128, 32, 512], BF16); tAm = [Tk() for _ in range(32)]
        TTs = sbd("TTs", [128, 32, 128], BF16); tTTs = [Tk() for _ in range(32)]
        Xb = sbd("Xb", [128, 8, 2, 256], BF16); tXb = [[Tk(), Tk()] for _ in range(8)]
        TTb = sbd("TTb", [128, 8, 2, 128], BF16); tTTb = [[Tk(), Tk()] for _ in range(8)]
        zs24b = sbd("zs24b", [128, T], BF16); tz24 = Tk()
        glb = sbd("glb", [128, T], BF16); tglb = Tk()
        lup = {"f": sbd("lupf", [128, 1024], BF16), "b": sbd("lupb", [128, 1024], BF16)}
        gup = sbd("gup", [128, 1024], BF16); tlw = Tk()
        mids = sbd("mids", [128, 16]); tots = sbd("tots", [128, 16]); biasm = sbd("biasm", [128, 16])
        nbiasm = sbd("nbiasm", [128, 16]); epsk = sbd("epsk", [128, 2]); tsm0 = Tk()
        scj = sbd("scj", [128, 16]); sci = sbd("sci", [128, 1]); sct = sbd("sct", [128, 16]); tsm = Tk()
        Hf = sbd("Hf", [128, 64]); Hb = sbd("Hb", [128, 64], BF16); Ht = sbd("Ht", [128, 64]); tH = Tk(); tHb = Tk(); tHt = Tk()
        Hc = sbd("Hc", [128, 2, 8, 64]); tHc = Tk()
        Xs = sbd("Xs", [128, 128], BF16); tXs = Tk()
        Ub = sbd("Ub", [128, 128], BF16); tUb = Tk()
        fin = [sbd("fin%d" % i, [128, 512]) for i in range(4)]; tfin = [Tk() for _ in range(4)]
        finb = sbd("finb", [128, 512], BF16); tfinb = Tk()
        pb = [psd("pb%d" % i, [128, 512]) for i in range(2)]; tpb = [Tk(), Tk()]
        pA = [psd("pA%d" % i, [128, 512]) for i in range(2)]; tpA = [Tk(), Tk()]
        pX = [psd("pX%d" % i, [128, 512]) for i in range(2)]; tpX = [Tk(), Tk()]
        pS = psd("pS", [128, 512]); tpS = {k: Tk() for k in "SUHY"}
        pT = psd("pT", [128, 1024], BF16); tpT = Tk()
        cnt = {"pb": 0, "pA": 0, "pX": 0}

        def rot(n):
            i = cnt[n] % 2
            cnt[n] += 1
            return i

        vop = lambda fn, r, w: P.op(V, fn, r, w)
        aop = lambda fn, r, w: P.op("scalar", fn, r, w)
        fa = lambda i: F[i][:, 0:T]
        blkc = lambda ap, k: ap[:, k * 512:(k + 1) * 512]
        for d in ("f", "b"):
            P.op("gpsimd", lambda e, d=d: e.dma_start(out=lup[d][:], in_=lup_dd[d][:, :]), (), (tlw,), dma=True)
        P.op("gpsimd", lambda e: e.dma_start(out=gup[:], in_=gup_d[:, :]), (), (tlw,), dma=True)
        vop(lambda e: e.memset(Hc[:], 0.0), (), (tHc,))
        vop(lambda e: e.memset(epsk[:, 0:1], 1e-24), (), (tsm0,))
        vop(lambda e: e.memset(epsk[:, 1:2], 64e-5), (), (tsm0,))

        def load_shift(ch, dst, u):
            U0 = u * T
            raw = F[TMP0]
            P.op("sync", lambda e: e.dma_start(out=raw[:, 1:T + 1], in_=zr_d[ch, :, U0:U0 + T]), (), (tF[TMP0],), dma=True)
            if u == 1:
                P.op("sync", lambda e: e.dma_start(out=raw[:, 0:1], in_=zr_d[ch, :, U0 - 1:U0], allow_slow_non_contiguous=True), (), (tF[TMP0],), dma=True)
                vop(lambda e: e.tensor_scalar(out=raw[:, 0:1], in0=raw[:, 0:1], scalar1=pcol("flag"), scalar2=None, op0=ALU.mult),
                    (tF[TMP0], t_par), (tF[TMP0],))
            else:
                vop(lambda e: e.memset(raw[:, 0:1], 0.0), (), (tF[TMP0],))
            if u == 0:
                P.op("sync", lambda e: e.dma_start(out=raw[:, T + 1:T + 2], in_=zr_d[ch, :, U0 + T:U0 + T + 1], allow_slow_non_contiguous=True), (), (tF[TMP0],), dma=True)
                vop(lambda e: e.tensor_scalar(out=raw[:, T + 1:T + 2], in0=raw[:, T + 1:T + 2], scalar1=pcol("flag"), scalar2=None,
                                              op0=ALU.mult), (tF[TMP0], t_par), (tF[TMP0],))
            else:
                vop(lambda e: e.memset(raw[:, T + 1:T + 2], 0.0), (), (tF[TMP0],))
            vop(lambda e: e.tensor_tensor(out=fa(TMP1), in0=raw[:, 0:T], in1=raw[:, 2:T + 2], op=ALU.add), (tF[TMP0],), (tF[TMP1],))
            vop(lambda e: e.tensor_scalar(out=fa(TMP1), in0=fa(TMP1), scalar1=dcol("hmu", ch), scalar2=None, op0=ALU.mult),
                (tF[TMP1], t_der), (tF[TMP1],))
            vop(lambda e: e.scalar_tensor_tensor(out=fa(dst), in0=raw[:, 1:T + 1], scalar=dcol("omu", ch), in1=fa(TMP1),
                                                 op0=ALU.mult, op1=ALU.add), (tF[TMP0], tF[TMP1], t_der), (tF[dst],))

        def raw_load(ch, ri, u):
            U0 = u * T
            raw = F[ri]
            P.op("sync", lambda e: e.dma_start(out=raw[:, 1:T + 1], in_=zr_d[ch, :, U0:U0 + T]), (), (tF[ri],), dma=True)
            if u == 1:
                P.op("sync", lambda e: e.dma_start(out=raw[:, 0:1], in_=zr_d[ch, :, U0 - 1:U0], allow_slow_non_contiguous=True), (), (tF[ri],), dma=True)
            if u == 0:
                P.op("sync", lambda e: e.dma_start(out=raw[:, T + 1:T + 2], in_=zr_d[ch, :, U0 + T:U0 + T + 1], allow_slow_non_contiguous=True), (), (tF[ri],), dma=True)

        def shift_from(ch, ri, dst, u, eng):
            raw = F[ri]
            xop = lambda fn, r, w: P.op(eng, fn, r, w)
            if u == 1:
                xop(lambda e: e.tensor_scalar(out=raw[:, 0:1], in0=raw[:, 0:1], scalar1=pcol("flag"), scalar2=None, op0=ALU.mult),
                    (tF[ri], t_par), (tF[ri],))
            else:
                xop(lambda e: e.memset(raw[:, 0:1], 0.0), (), (tF[ri],))
            if u == 0:
                xop(lambda e: e.tensor_scalar(out=raw[:, T + 1:T + 2], in0=raw[:, T + 1:T + 2], scalar1=pcol("flag"), scalar2=None,
                                              op0=ALU.mult), (tF[ri], t_par), (tF[ri],))
            else:
                xop(lambda e: e.memset(raw[:, T + 1:T + 2], 0.0), (), (tF[ri],))
            xop(lambda e: e.tensor_tensor(out=fa(dst), in0=raw[:, 0:T], in1=raw[:, 2:T + 2], op=ALU.add), (tF[ri],), (tF[dst],))
            xop(lambda e: e.tensor_scalar(out=fa(dst), in0=fa(dst), scalar1=dcol("hmu", ch), scalar2=None, op0=ALU.mult),
                (tF[dst], t_der), (tF[dst],))
            if eng == "gpsimd":
                xop(lambda e: e.tensor_scalar(out=raw[:, 1:T + 1], in0=raw[:, 1:T + 1], scalar1=dcol("omu", ch), scalar2=None, op0=ALU.mult),
                    (tF[ri], t_der), (tF[ri],))
                xop(lambda e: e.tensor_tensor(out=fa(dst), in0=fa(dst), in1=raw[:, 1:T + 1], op=ALU.add), (tF[ri], tF[dst]), (tF[dst],))
            else:
                xop(lambda e: e.scalar_tensor_tensor(out=fa(dst), in0=raw[:, 1:T + 1], scalar=dcol("omu", ch), in1=fa(dst),
                                                     op0=ALU.mult, op1=ALU.add), (tF[ri], tF[dst], t_der), (tF[dst],))

        def raw_loads(hp, u):
            raw_load(hp, E, u)
            raw_load(8 + hp, A, u)
            raw_load(16 + hp, L, u)

        def lora_sig(d, prow, bname, hp, dst):
            for k in range(4):
                i = rot("pb")
                P.mm_group([(pb[i][:], lup[d][prow:prow + 64, hp * 128:(hp + 1) * 128], blkc(zs24b[prow:prow + 64, :], k))],
                           (tlw, tz24), (tpb[i],))
                aop(lambda e, i=i, k=k: e.activation(out=blkc(fa(dst), k), in_=pb[i][:], func=AF.Sigmoid, bias=pcol(bname, hp)),
                    (tpb[i], t_par), (tF[dst],))

        def kd_from_a(hp):
            vop(lambda e: e.tensor_scalar(out=fa(A), in0=fa(A), scalar1=pcol("kim", hp), scalar2=dcol("omk", hp), op0=ALU.mult,
                                          op1=ALU.add), (tF[A], t_par, t_der), (tF[A],))
            vop(lambda e: e.tensor_tensor(out=fa(A), in0=fa(A), in1=fa(K), op=ALU.mult), (tF[A], tF[K]), (tF[A],))

        for (u, d) in order:
            U0 = u * T
            fwd = d == "f"
            m4 = con[:, CC["m4f"]:CC["m4f"] + 512] if fwd else con[:, CC["m4b"]:CC["m4b"] + 512]
            ml = con[:, CC["mlf"]:CC["mlf"] + 128] if fwd else con[:, CC["mlb"]:CC["mlb"] + 128]
            load_shift(24, E, u)
            aop(lambda e: e.activation(out=zs24b[0:64, :], in_=F[E][0:64, 0:T], func=AF.Tanh), (tF[E],), (tz24,))
            aop(lambda e: e.copy(out=zs24b[64:128, :], in_=F[E][64:128, 0:T]), (tF[E],), (tz24,))
            load_shift(25, E, u)
            aop(lambda e: e.activation(out=glb[:], in_=fa(E), func=AF.Sigmoid), (tF[E],), (tglb,))
            for hp in range(1 if dbg.get("dump2") else 8):
                if hp == 0:
                    raw_loads(0, u)
                shift_from(hp, E, R, u, V)
                shift_from(16 + hp, L, VV, u, "gpsimd")
                shift_from(8 + hp, A, K, u, V)
                aop(lambda e, hp=hp: e.activation(out=fa(TMP0), in_=fa(K), func=AF.Square, scale=pcol("kns", hp)),
                    (tF[K], t_par), (tF[TMP0],))
                for k in range(4):
                    i = rot("pb")
                    P.mm_group([(pb[i][:], bonesf, blkc(fa(TMP0), k))], (t_con, tF[TMP0]), (tpb[i],))
                    aop(lambda e, i=i, k=k: e.activation(out=blkc(fa(TMP1), k), in_=pb[i][:], func=AF.Ln, bias=epsk[:, 0:1]), (tpb[i], tsm0), (tF[TMP1],))
                aop(lambda e: e.activation(out=fa(TMP1), in_=fa(TMP1), func=AF.Exp, scale=-0.5), (tF[TMP1],), (tF[TMP1],))
                vop(lambda e, hp=hp: e.scalar_tensor_tensor(out=fa(KK), in0=fa(K), scalar=pcol("kns", hp), in1=fa(TMP1), op0=ALU.mult,
                                                            op1=ALU.mult), (tF[K], tF[TMP1], t_par), (tF[KK],))
                if not fwd:
                    lora_sig("f", 64, "ibf", hp, A)
                    kd_from_a(hp)
                    vop(lambda e: e.tensor_copy(out=fa(TMP0), in_=fa(A)), (tF[A],), (tF[TMP0],))
                lora_sig(d, 0, "dbf" if fwd else "dbb", hp, E)
                lora_sig(d, 64, "ibf" if fwd else "ibb", hp, A)
                vop(lambda e: e.tensor_tensor(out=fa(B), in0=fa(KK), in1=fa(A), op=ALU.mult), (tF[KK], tF[A]), (tF[B],))
                kd_from_a(hp)
                if not fwd:
                    vop(lambda e: e.tensor_tensor(out=fa(TMP0), in0=fa(TMP0), in1=fa(A), op=ALU.add), (tF[TMP0], tF[A]), (tF[TMP0],))
                    vop(lambda e, hp=hp: e.scalar_tensor_tensor(out=fa(TMP0), in0=fa(TMP0), scalar=dcol("hbs", hp), in1=fa(R),
                                                                op0=ALU.mult, op1=ALU.mult), (tF[TMP0], tF[R], t_der), (tF[TMP0],))
                for j in range(16):
                    vop(lambda e, j=j: e.tensor_tensor_scan(out=F[L][:, j * 128:(j + 1) * 128], data0=F[E][:, j * 128:(j + 1) * 128],
                                                            data1=onesf, initial=0.0, op0=ALU.add, op1=ALU.mult),
                        (tF[E], t_con), (tF[L],))
                vop(lambda e: e.tensor_tensor(out=fa(E), in0=fa(L), in1=fa(E), op=ALU.subtract), (tF[L], tF[E]), (tF[E],))
                L3 = fa(L).rearrange("p (j t) -> p j t", t=128)
                vop(lambda e: e.tensor_copy(out=mids[:], in_=L3[:, :, 63]), (tF[L],), (tsm,))
                vop(lambda e: e.tensor_copy(out=tots[:], in_=L3[:, :, 127]), (tF[L],), (tsm,))
                vop(lambda e: e.tensor_scalar(out=biasm[:], in0=mids[:], scalar1=C0, scalar2=None, op0=ALU.mult), (tsm,), (tsm,))
                vop(lambda e: e.tensor_tensor(out=sct[:], in0=tots[:], in1=mids[:], op=ALU.subtract), (tsm,), (tsm,))
                if fwd:
                    vop(lambda e: e.tensor_copy(out=scj[:], in_=sct[:]), (tsm,), (tsm,))
                    vop(lambda e: e.tensor_tensor(out=scj[:, 0:15], in0=sct[:, 0:15], in1=mids[:, 1:16], op=ALU.add), (tsm,), (tsm,))
                    aop(lambda e: e.activation(out=sci[:], in_=mids[:, 0:1], func=AF.Exp, scale=-C0), (tsm,), (tsm,))
                else:
                    vop(lambda e: e.tensor_copy(out=scj[:], in_=mids[:]), (tsm,), (tsm,))
                    vop(lambda e: e.tensor_tensor(out=scj[:, 1:16], in0=mids[:, 1:16], in1=sct[:, 0:15], op=ALU.add), (tsm,), (tsm,))
                    aop(lambda e: e.activation(out=sci[:], in_=sct[:, 15:16], func=AF.Exp, scale=-C0), (tsm,), (tsm,))
                aop(lambda e: e.activation(out=scj[:], in_=scj[:], func=AF.Exp, scale=-C0), (tsm,), (tsm,))
                vop(lambda e: e.tensor_scalar(out=nbiasm[:], in0=mids[:], scalar1=-C0, scalar2=None, op0=ALU.mult), (tsm,), (tsm,))
                ARa = AR[:, :, 0:128]
                ARr = AR[:, :, 128:256]
                v3 = lambda i: fa(i).rearrange("p (j t) -> p j t", t=128)

                def exps(dst, src, sign):
                    bb = biasm if sign < 0 else nbiasm
                    for j in range(16):
                        aop(lambda e, j=j: e.activation(out=F[dst][:, j * 128:(j + 1) * 128], in_=F[src][:, j * 128:(j + 1) * 128], func=AF.Exp,
                                                        scale=sign * C0, bias=bb[:, j:j + 1]), (tF[src], tsm), (tF[dst],))
                if fwd:
                    exps(YA, L, -1.0)
                    vop(lambda e: e.tensor_tensor(out=ARr, in0=v3(R), in1=v3(YA), op=ALU.mult), (tF[R], tF[YA]), (tAR,))
                    exps(E, E, -1.0)
                    vop(lambda e: e.scalar_tensor_tensor(out=ARa, in0=v3(KK), scalar=-1.0, in1=v3(E), op0=ALU.mult, op1=ALU.mult),
                        (tF[KK], tF[E]), (tAR,))
                    exps(L, L, 1.0)
                    vop(lambda e: e.tensor_tensor(out=Bf[:], in0=fa(B), in1=fa(L), op=ALU.mult), (tF[B], tF[L]), (tBf,))
                    vop(lambda e: e.tensor_tensor(out=Kf[:], in0=fa(A), in1=fa(L), op=ALU.mult), (tF[A], tF[L]), (tKf,))
                else:
                    exps(YA, E, -1.0)
                    vop(lambda e: e.tensor_tensor(out=Bf[:], in0=fa(B), in1=fa(YA), op=ALU.mult), (tF[B], tF[YA]), (tBf,))
                    vop(lambda e: e.tensor_tensor(out=Kf[:], in0=fa(A), in1=fa(YA), op=ALU.mult), (tF[A], tF[YA]), (tKf,))
                    exps(E, E, 1.0)
                    vop(lambda e: e.tensor_tensor(out=ARr, in0=v3(R), in1=v3(E), op=ALU.mult), (tF[R], tF[E]), (tAR,))
                    exps(L, L, 1.0)
                    vop(lambda e: e.scalar_tensor_tensor(out=ARa, in0=v3(KK), scalar=-1.0, in1=v3(L), op0=ALU.mult, op1=ALU.mult),
                        (tF[KK], tF[L]), (tAR,))
                aop(lambda e: e.copy(out=vb[:], in_=fa(VV)), (tF[VV],), (tvb,))
                if hp < 7 and not dbg.get("dump2"):
                    raw_loads(hp + 1, u)
                for (src, tsrc, dst, tdst) in ((Bf, tBf, Btm, tBtm), (Kf, tKf, Ktm, tKtm), (vb, tvb, Vtm, tVtm)):
                    for half in range(2):
                        for jj in range(8):
                            j = half * 8 + jj
                            P.op("tensor", lambda e, src=src, j=j, jj=jj: e.transpose(pT[:, jj * 128:(jj + 1) * 128],
                                                                                      src[:, j * 128:(j + 1) * 128], identb[:]),
                                 (tsrc, t_identb), (tpT,), sig=(jj == 7))
                        aop(lambda e, dst=dst, half=half: e.copy(out=dst[:, half * 8:(half + 1) * 8, :],
                                                                 in_=pT[:, :].rearrange("p (j c) -> p j c", c=128)),
                            (tpT,), (tdst,))
                for hd in range(2):
                    hb = 64 * hd
                    for j in range(16):
                        pi = hd * 16 + j
                        cs = slice(j * 128, (j + 1) * 128)
                        i = rot("pA")
                        P.op("tensor", lambda e, i=i, hb=hb, cs=cs, j=j: e.matmul(pA[i][:, 0:256], Bf[hb:hb + 64, cs], AR[hb:hb + 64, j, :],
                                                                                 start=True, stop=True), (tBf, tAR), (tpA[i],), sig=False)
                        P.op("tensor", lambda e, i=i, hb=hb, cs=cs, j=j: e.matmul(pA[i][:, 256:512], Kf[hb:hb + 64, cs], AR[hb:hb + 64, j, :],
                                                                                 start=True, stop=True), (tKf, tAR), (tpA[i],))
                        vop(lambda e, i=i, pi=pi, m4=m4: e.tensor_tensor(out=Am[:, pi, :], in0=pA[i][:], in1=m4, op=ALU.mult),
                            (tpA[i], t_con), (tAm[pi],))
                SQB = [(pb[0], tpb[0]), (pb[1], tpb[1]), (pA[0], tpA[0]), (pA[1], tpA[1])]
                TTB = [(pX[0], tpX[0]), (pX[1], tpX[1])]
                for g in range(4):
                    prs = [g * 8 + q for q in range(8)]
                    for bk in range(4):
                        bank, tbank = SQB[bk]
                        mms = []
                        for q in (2 * bk, 2 * bk + 1):
                            pi = prs[q]
                            hd, j = pi // 16, pi % 16
                            hb = 64 * hd
                            c0 = (q % 2) * 256
                            P.op("tensor", lambda e, bank=bank, c0=c0, hb=hb, j=j: e.matmul(
                                bank[:, c0:c0 + 128], AR[hb:hb + 64, j, 0:128], Bf[hb:hb + 64, j * 128:(j + 1) * 128], start=True, stop=True),
                                 (tAR, tBf), (tbank,), sig=(q % 2 == 1))
                        for q in (2 * bk, 2 * bk + 1):
                            pi = prs[q]
                            c0 = (q % 2) * 256
                            vop(lambda e, q=q, ml=ml, bank=bank, c0=c0: e.tensor_tensor(out=Xb[:, q, 0, 0:128], in0=bank[:, c0:c0 + 128], in1=ml,
                                                                                        op=ALU.mult), (tbank, t_con), (tXb[q][0],))
                            aop(lambda e, q=q, pi=pi: e.copy(out=Xb[:, q, 0, 128:256], in_=Am[:, pi, 0:128]), (tAm[pi],), (tXb[q][0],))
                            vop(lambda e, q=q, pi=pi: e.tensor_tensor(out=TTb[:, q, 0, :], in0=Am[:, pi, 0:128], in1=identb[:], op=ALU.add),
                                (tAm[pi], t_identb), (tTTb[q][0],))
                    for lv in range(6):
                        cur, nxt = lv % 2, (lv + 1) % 2
                        last = lv == 5
                        n = 128 if last else 256
                        for bk in range(4):
                            bank, tbank = SQB[bk]
                            qs = (2 * bk, 2 * bk + 1)
                            nmm = 0
                            for q in qs:
                                c0 = (q % 2) * 256
                                X_ = Xb[:, q, cur, 0:128]
                                XT_ = Xb[:, q, cur, 128:256]
                                fin_ = (q % 2 == 1)
                                P.op("tensor", lambda e, bank=bank, c0=c0, X_=X_, XT_=XT_: e.matmul(bank[:, c0:c0 + 128], XT_, X_, start=True, stop=True),
                                     (tXb[q][cur],), (tbank,), sig=(last and fin_))
                                if not last:
                                    P.op("tensor", lambda e, bank=bank, c0=c0, X_=X_, XT_=XT_: e.matmul(bank[:, c0 + 128:c0 + 256], X_, XT_, start=True,
                                                                                                       stop=True), (tXb[q][cur],), (tbank,), sig=fin_)
                            q0 = qs[0]
                            aop(lambda e, bank=bank, q0=q0, nxt=nxt, n=n: e.copy(
                                out=Xb[:, q0:q0 + 2, nxt, 0:n], in_=bank[:, :].rearrange("p (a c) -> p a c", a=2)[:, :, 0:n]),
                                (tbank,), (tXb[qs[0]][nxt], tXb[qs[1]][nxt]))
                        for tb in range(2):
                            bank, tbank = TTB[tb]
                            qs = list(range(4 * tb, 4 * tb + 4))
                            for q in qs:
                                P.op("tensor", lambda e, bank=bank, q=q, nxt=nxt, cur=cur: e.matmul(
                                    bank[:, (q % 4) * 128:(q % 4 + 1) * 128], Xb[:, q, nxt, 0:128], TTb[:, q, cur, :], start=True, stop=True),
                                     (tTTb[q][cur], tXb[q][nxt]), (tbank,), sig=(q % 4 == 3))
                            q0 = qs[0]
                            bview = bank[:, :].rearrange("p (a c) -> p a c", a=4)
                            if last:
                                pi0 = prs[q0]
                                vop(lambda e, bview=bview, q0=q0, pi0=pi0, cur=cur: e.tensor_tensor(
                                    out=TTs[:, pi0:pi0 + 4, :], in0=bview, in1=TTb[:, q0:q0 + 4, cur, :], op=ALU.add),
                                    [tbank] + [tTTb[q][cur] for q in qs], [tTTs[prs[q]] for q in qs])
                            else:
                                vop(lambda e, bview=bview, q0=q0, nxt=nxt, cur=cur: e.tensor_tensor(
                                    out=TTb[:, q0:q0 + 4, nxt, :], in0=bview, in1=TTb[:, q0:q0 + 4, cur, :], op=ALU.add),
                                    [tbank] + [tTTb[q][cur] for q in qs], [tTTb[q][nxt] for q in qs])
                di = 0 if fwd else 1
                hcar = Hc[:, di, hp, :]
                linked_in = (fwd and u == 1) or ((not fwd) and u == 0)
                if linked_in:
                    vop(lambda e, hcar=hcar: e.tensor_scalar(out=Hf[:], in0=hcar, scalar1=sci[:, 0:1], scalar2=pcol("flag"), op0=ALU.mult,
                                                             op1=ALU.mult), (tHc, tsm, t_par), (tH,))
                else:
                    vop(lambda e: e.memset(Hf[:], 0.0), (), (tH,))
                aop(lambda e: e.copy(out=Hb[:], in_=Hf[:]), (tH,), (tHb,))
                jorder = list(range(16)) if fwd else list(range(15, -1, -1))
                for j in jorder:
                    for hd in range(2):
                        hb = 64 * hd
                        pi = hd * 16 + j
                        P.mm_group([(pS[:, hd * 64:(hd + 1) * 64], AR[hb:hb + 64, j, 0:128], Hb[hb:hb + 64, :]),
                                    (pS[:, hd * 64:(hd + 1) * 64], Am[:, pi, 256:384], Vtm[:, j, hb:hb + 64])],
                                   (tAR, tHb, tAm[pi], tVtm), (tpS["S"],))
                    aop(lambda e: e.copy(out=Xs[:], in_=pS[:, 0:128]), (tpS["S"],), (tXs,))
                    for hd in range(2):
                        pi = hd * 16 + j
                        P.mm_group([(pS[:, 128 + hd * 64:128 + (hd + 1) * 64], TTs[:, pi, :], Xs[:, hd * 64:(hd + 1) * 64])],
                                   (tTTs[pi], tXs), (tpS["U"],))
                    aop(lambda e: e.copy(out=Ub[:], in_=pS[:, 128:256]), (tpS["U"],), (tUb,))
                    for hd in range(2):
                        hb = 64 * hd
                        pi = hd * 16 + j
                        P.mm_group([(pS[hb:hb + 64, 320:448], Hb[hb:hb + 64, :], AR[hb:hb + 64, j, 128:256]),
                                    (pS[hb:hb + 64, 320:448], Ub[:, hd * 64:(hd + 1) * 64], Am[:, pi, 128:256]),
                                    (pS[hb:hb + 64, 320:448], Vtm[:, j, hb:hb + 64], Am[:, pi, 384:512])],
                                   (tHb, tAR, tUb, tAm[pi], tVtm), (tpS["Y"],))
                        P.mm_group([(pS[hb:hb + 64, 256:320], Btm[:, j, hb:hb + 64], Ub[:, hd * 64:(hd + 1) * 64]),
                                    (pS[hb:hb + 64, 256:320], Ktm[:, j, hb:hb + 64], Vtm[:, j, hb:hb + 64])],
                                   (tBtm, tUb, tKtm, tVtm), (tpS["H"],))
                    aop(lambda e, j=j: e.copy(out=F[YA][:, j * 128:(j + 1) * 128], in_=pS[:, 320:448]), (tpS["Y"],), (tF[YA],))
                    vop(lambda e: e.tensor_tensor(out=Ht[:], in0=pS[:, 256:320], in1=Hf[:], op=ALU.add), (tpS["H"], tH), (tHt,))
                    vop(lambda e, j=j: e.tensor_scalar(out=Hf[:], in0=Ht[:], scalar1=scj[:, j:j + 1], scalar2=None, op0=ALU.mult),
                        (tHt, tsm), (tH,))
                    aop(lambda e: e.copy(out=Hb[:], in_=Hf[:]), (tH,), (tHb,))
                vop(lambda e, hcar=hcar: e.tensor_copy(out=hcar, in_=Hf[:]), (tH,), (tHc,))
                if dbg.get("dump") and hp == 0 and (u, d) == order[0]:
                    def dump(name, ap, shape, dt, toks):
                        dd = nc.dram_tensor("dbg_" + name, shape, dt, kind="ExternalOutput").ap()
                        P.op("sync", lambda e: e.dma_start(out=dd, in_=ap), toks, (), dma=True)
                    dump("L", fa(L), [128, T], F32, (tF[L],))
                    dump("E", fa(E), [128, T], F32, (tF[E],))
                    dump("KK", fa(KK), [128, T], F32, (tF[KK],))
                    dump("R", fa(R), [128, T], F32, (tF[R],))
                    dump("Akd", fa(A), [128, T], F32, (tF[A],))
                    dump("AR", AR[:], [128, 16, 256], BF16, (tAR,))
                    dump("Bf", Bf[:], [128, T], BF16, (tBf,))
                    dump("Kf", Kf[:], [128, T], BF16, (tKf,))
                    dump("Btm", Btm[:], [128, 16, 128], BF16, (tBtm,))
                    dump("Vtm", Vtm[:], [128, 16, 128], BF16, (tVtm,))
                    dump("Am", Am[:], [128, 32, 512], BF16, tAm)
                    dump("TTs", TTs[:], [128, 32, 128], BF16, tTTs)
                    dump("YA", fa(YA), [128, T], F32, (tF[YA],))
                    dump("scj", scj[:], [128, 16], F32, (tsm,))
                    dump("mids", mids[:], [128, 16], F32, (tsm,))
                    dump("tots", tots[:], [128, 16], F32, (tsm,))
                if dbg.get("dump") and hp == 0 and not dbg.get("dump2"):
                    break
                if fwd:
                    P.op("sync", lambda e, hp=hp, U0=U0: e.dma_start(out=yf_d[hp, :, U0:U0 + T], in_=fa(YA)), (tF[YA],), (), dma=True)
                else:
                    P.op("sync", lambda e, hp=hp, U0=U0: e.dma_start(out=fa(TMP1), in_=yf_d[hp, :, U0:U0 + T]), (), (tF[TMP1],), dma=True)
                    vop(lambda e: e.tensor_tensor(out=fa(YA), in0=fa(YA), in1=fa(TMP1), op=ALU.add), (tF[YA], tF[TMP1]), (tF[YA],))
                    aop(lambda e: e.activation(out=fa(TMP1), in_=fa(YA), func=AF.Square), (tF[YA],), (tF[TMP1],))
                    for k in range(4):
                        i1 = rot("pb")
                        P.mm_group([(pb[i1][:], bonesf, blkc(fa(YA), k))], (t_con, tF[YA]), (tpb[i1],))
                        vop(lambda e, i1=i1: e.tensor_scalar(out=fin[0][:], in0=pb[i1][:], scalar1=1.0 / 64, scalar2=None, op0=ALU.mult),
                            (tpb[i1],), (tfin[0],))
                        i2 = rot("pb")
                        P.mm_group([(pb[i2][:], bonesf, blkc(fa(TMP1), k))], (t_con, tF[TMP1]), (tpb[i2],))
                        vop(lambda e: e.tensor_tensor(out=fin[1][:], in0=fin[0][:], in1=fin[0][:], op=ALU.mult), (tfin[0],), (tfin[1],))
                        vop(lambda e, i2=i2: e.scalar_tensor_tensor(out=fin[1][:], in0=pb[i2][:], scalar=1.0 / 64, in1=fin[1][:],
                                                                    op0=ALU.mult, op1=ALU.subtract), (tpb[i2], tfin[1]), (tfin[1],))
                        aop(lambda e: e.activation(out=fin[1][:], in_=fin[1][:], func=AF.Ln, bias=epsk[:, 1:2]), (tfin[1], tsm0), (tfin[1],))
                        aop(lambda e: e.activation(out=fin[1][:], in_=fin[1][:], func=AF.Exp, scale=-0.5), (tfin[1],), (tfin[1],))
                        vop(lambda e, k=k: e.tensor_tensor(out=fin[2][:], in0=blkc(fa(YA), k), in1=fin[0][:], op=ALU.subtract),
                            (tF[YA], tfin[0]), (tfin[2],))
                        vop(lambda e: e.tensor_tensor(out=fin[2][:], in0=fin[2][:], in1=fin[1][:], op=ALU.mult), (tfin[2], tfin[1]), (tfin[2],))
                        vop(lambda e, hp=hp: e.tensor_scalar(out=fin[2][:], in0=fin[2][:], scalar1=pcol("gnw", hp), scalar2=pcol("gnb", hp),
                                                             op0=ALU.mult, op1=ALU.add), (tfin[2], t_par), (tfin[2],))
                        i3 = rot("pb")
                        P.mm_group([(pb[i3][:], bonesf, blkc(fa(TMP0), k))], (t_con, tF[TMP0]), (tpb[i3],))
                        vop(lambda e, i3=i3, k=k: e.tensor_tensor(out=fin[3][:], in0=pb[i3][:], in1=blkc(fa(VV), k), op=ALU.mult),
                            (tpb[i3], tF[VV]), (tfin[3],))
                        vop(lambda e: e.tensor_tensor(out=fin[2][:], in0=fin[2][:], in1=fin[3][:], op=ALU.add), (tfin[2], tfin[3]), (tfin[2],))
                        i4 = rot("pb")
                        P.mm_group([(pb[i4][:], gup[:, hp * 128:(hp + 1) * 128], blkc(glb, k))], (tlw, tglb), (tpb[i4],))
                        vop(lambda e, i4=i4: e.tensor_tensor(out=finb[:], in0=fin[2][:], in1=pb[i4][:], op=ALU.mult), (tfin[2], tpb[i4]), (tfinb,))
                        P.op("sync", lambda e, hp=hp, U0=U0, k=k: e.dma_start(out=oT_d[hp, :, U0 + k * 512:U0 + (k + 1) * 512], in_=finb[:]),
                             (tfinb,), (), dma=True)
                    if dbg.get("dump2"):
                        def dump2(name, ap, shape, dt, toks):
                            dd = nc.dram_tensor("dbg2_" + name, shape, dt, kind="ExternalOutput").ap()
                            P.op("sync", lambda e: e.dma_start(out=dd, in_=ap), toks, (), dma=True)
                        dump2("YA", fa(YA), [128, T], F32, (tF[YA],))
                        dump2("TMP0", fa(TMP0), [128, T], F32, (tF[TMP0],))
                        dump2("TMP1", fa(TMP1), [128, T], F32, (tF[TMP1],))
                        dump2("VV", fa(VV), [128, T], F32, (tF[VV],))
                        dump2("glb", glb[:], [128, T], BF16, (tglb,))
                        for q in range(4):
                            dump2("fin%d" % q, fin[q][:], [128, 512], F32, (tfin[q],))


def _cols(v, n):
    return np.ascontiguousarray(v.reshape(n, 128).T.astype(np.float32))


def make_params(inp, flag):
    p = np.zeros((128, NPCOL), np.float32)

    def put(name, arr, n):
        p[:, PC[name]:PC[name] + n] = _cols(np.asarray(arr).reshape(-1), n)

    put("ffn1_pre", inp["ffn1_pre_g"], 16); put("ffn1_post", inp["ffn1_post_g"], 16)
    put("mix_pre", inp["mix_pre_g"], 16); put("mix_post", inp["mix_post_g"], 16)
    put("ffn2_pre", inp["ffn2_pre_g"], 16); put("ffn2_post", inp["ffn2_post_g"], 16)
    put("mu", inp["rwkv_shift_mix"], 26)
    put("dbf", inp["decay_bias_fwd"], 8); put("dbb", inp["decay_bias_bwd"], 8)
    put("ibf", inp["iclr_bias_fwd"], 8); put("ibb", inp["iclr_bias_bwd"], 8)
    put("kns", inp["key_norm_scale"], 8); put("kim", inp["key_iclr_mix"], 8)
    put("bsc", inp["bonus_scale"], 8); put("gnw", inp["gn_w"], 8); put("gnb", inp["gn_b"], 8)
    p[:, PC["flag"]] = flag
    return p


def make_consts():
    c = np.zeros((128, NCCOL), np.float32)
    c[:, 0:128] = np.eye(128)
    c[:, 128:256] = 1.0
    c[0:64, 256:320] = 1.0
    c[64:128, 320:384] = 1.0
    s = np.arange(128)[:, None]
    t = np.arange(128)[None, :]
    strict_f = (t > s).astype(np.float32)
    incl_f = (t >= s).astype(np.float32)
    strict_b = (t < s).astype(np.float32)
    incl_b = (t <= s).astype(np.float32)
    c[:, CC["m4f"]:CC["m4f"] + 512] = np.concatenate([strict_f, incl_f, strict_f, incl_f], 1)
    c[:, CC["m4b"]:CC["m4b"] + 512] = np.concatenate([strict_b, incl_b, strict_b, incl_b], 1)
    c[:, CC["mlf"]:CC["mlf"] + 128] = strict_f.T
    c[:, CC["mlb"]:CC["mlb"] + 128] = strict_b.T
    return c


def make_nat_tables(rpb, linked):
    rpb = np.asarray(rpb, np.float32)
    kc = np.arange(64)[:, None]
    qc = np.arange(64)[None, :]
    cs = np.clip(qc - 8, 0, 48)
    colok = (kc >= cs) & (kc < cs + 16)
    cidx = np.clip(kc - qc + 15, 0, 30)
    G = np.full((16, 2, 64, 2, 7, 64), NEG, np.float32)
    for par in range(2):
        for m in range(14):
            dr = m - 7 + par
            if dr < -7 or dr > 7:
                continue
            val = np.where(colok[None], rpb[:, dr + 7][:, cidx], NEG)
            G[:, par, :, m % 2, m // 2, :] = val
    G = G.reshape(8, 2, 128, 2, 7, 64).transpose(0, 2, 1, 3, 4, 5)
    S = np.full((7, 16, 2, 64, 6, 64), NEG, np.float32)
    for spi, (u, i) in enumerate(SPECIAL):
        er0 = 24 if u == 0 else -4
        for c in range(6):
            for par in range(2):
                er = er0 + 2 * c + par
                if linked:
                    gi = u * 32 + i
                    ws = min(max(gi - 4, 0), 56)
                    gr = u * 32 + er
                    ok = ws <= gr < ws + 8
                    dr = gr - gi
                else:
                    ws = min(max(i - 4, 0), 24)
                    ok = (ws <= er < ws + 8) and (0 <= er < 32)
                    dr = er - i
                if not ok:
                    continue
                S[spi, :, par, :, c, :] = np.where(colok[None], rpb[:, dr + 7][:, cidx], NEG)
    S = S.reshape(7, 16, 128, 6, 64)
    return np.ascontiguousarray(G), np.ascontiguousarray(S)


def core_units(xp, xs, c):
    if c < 4:
        return np.concatenate([xs[c], xp[c]], 0)
    b = 4 + 3 * (c - 4)
    return np.concatenate([xp[b], xp[b + 1], xp[b + 2]], 0)


WNAMES = ("ffn1_w_gate", "ffn1_w_up", "ffn1_w_down", "ffn2_w_gate", "ffn2_w_up", "ffn2_w_down", "w_in", "w_out")


def make_inputs_core(inp, x, linked, shared=None):
    m = {"x": np.ascontiguousarray(x, dtype=np.float32), "params": make_params(inp, 1.0 if linked else 0.0)}
    if shared is None:
        shared = make_shared(inp)
    m.update(shared["common"])
    G, S = shared["nat"][bool(linked)]
    m["natG"] = G
    m["natS"] = S
    return m


def make_shared(inp):
    common = {"consts": make_consts()}
    for k in WNAMES:
        common[k] = np.ascontiguousarray(np.asarray(inp[k])[0], dtype=np.float32)
    g = lambda k: np.asarray(inp[k])[0].astype(np.float32)
    common["lupf"] = np.ascontiguousarray(np.concatenate([g("decay_up_fwd"), g("iclr_up_fwd")], 0))
    common["lupb"] = np.ascontiguousarray(np.concatenate([g("decay_up_bwd"), g("iclr_up_bwd")], 0))
    common["gup"] = np.ascontiguousarray(g("gate_up"))
    nat = {True: make_nat_tables(np.asarray(inp["nat_rpb"])[0], True),
           False: make_nat_tables(np.asarray(inp["nat_rpb"])[0], False)}
    return {"common": common, "nat": nat}


_CACHE = {}


def kernel(**inp):
    inp = {k: np.asarray(v) for k, v in inp.items()}
    xp, xs = inp["x_prompt"], inp["x_sample"]
    if "nc" not in _CACHE:
        _CACHE["nc"] = build_program()
    nc = _CACHE["nc"]
    shared = make_shared(inp)
    in_maps = []
    for c in range(8):
        in_maps.append(make_inputs_core(inp, core_units(xp, xs, c), c < 4, shared))
    res = run_bass_kernel_spmd(nc, in_maps, core_ids=list(range(8)))
    yp = np.empty(xp.shape, np.float32)
    ys = np.empty(xs.shape, np.float32)
    for c in range(8):
        y = np.asarray(res.results[c]["y"], dtype=np.float32)
        if c < 4:
            ys[c] = y[:2 * UNIT]
            yp[c] = y[2 * UNIT:]
        else:
            b = 4 + 3 * (c - 4)
            yp[b] = y[:UNIT]
            yp[b + 1] = y[UNIT:2 * UNIT]
            yp[b + 2] = y[2 * UNIT:]
    return (yp, ys)
```

```python
import numpy as np
from contextlib import ExitStack
import concourse.bass as bass
import concourse.mybir as mybir
from concourse.bass_utils import run_bass_kernel_spmd

F32 = mybir.dt.float32
BF16 = mybir.dt.bfloat16
AF = mybir.ActivationFunctionType
ALU = mybir.AluOpType

D = 2048
DFF = 5632
NDC = 16
NFC = 44
TT = 512
UNIT = 2048
NUNIT = 3
NTOK = UNIT * NUNIT
RW_CH = 26
C0 = float(np.exp(-0.5))
NDS = 8


class Tk:
    __slots__ = ("w", "r")

    def __init__(self):
        self.w = None
        self.r = {}


class Prog:
    ENGS = ("sync", "gpsimd", "scalar", "vector", "tensor")

    def __init__(self, nc, es):
        self.nc = nc
        self.q = {k: [] for k in self.ENGS}
        self.csem = {}
        for k in ("scalar", "vector", "tensor", "gpsimd"):
            self.csem[k] = es.enter_context(nc.semaphore("c_" + k))
        self.ccnt = {k: 0 for k in self.csem}
        self.dpool = {}
        self.dnext = {}
        for k in ("sync", "gpsimd"):
            self.dpool[k] = [[es.enter_context(nc.semaphore("d_%s%d" % (k, i))), 0] for i in range(NDS)]
            self.dnext[k] = 0
        self.waited = {}

    def op(self, eng, fn, reads=(), writes=(), dma=False, sig=True, nowait=False, reg=True):
        deps = {}

        def need(tok):
            if tok is None:
                return
            s, v = tok
            if deps.get(s, 0) < v:
                deps[s] = v

        if not nowait:
            for t in reads:
                need(t.w)
            for t in writes:
                need(t.w)
                for s, v in t.r.items():
                    need((s, v))
        tok = None
        inc = 0
        if dma:
            slot = self.dpool[eng][self.dnext[eng]]
            self.dnext[eng] = (self.dnext[eng] + 1) % NDS
            if slot[1] > 0:
                need((slot[0], slot[1]))
            slot[1] += 16
            tok = (slot[0], slot[1])
            inc = 16
        elif sig:
            self.ccnt[eng] += 1
            tok = (self.csem[eng], self.ccnt[eng])
            inc = 1
        waits = []
        for s, v in deps.items():
            key = (eng, s)
            if self.waited.get(key, 0) >= v:
                continue
            if eng == "tensor" and s is self.csem["tensor"]:
                continue
            self.waited[key] = v
            waits.append((s, v))
        if tok is not None and reg:
            for t in reads:
                if t.r.get(tok[0], 0) < tok[1]:
                    t.r[tok[0]] = tok[1]
            for t in writes:
                t.w = tok
                t.r = {}
        self.q[eng].append((waits, fn, tok, inc))
        return tok

    def mm_group(self, mms, reads, writes):
        n = len(mms)
        for i, (o, l, r) in enumerate(mms):
            fn = (lambda e, o=o, l=l, r=r, st=(i == 0), sp=(i == n - 1): e.matmul(o, l, r, start=st, stop=sp))
            if n == 1:
                self.op("tensor", fn, reads, writes)
            elif i == 0:
                self.op("tensor", fn, reads, writes, sig=False)
            elif i == n - 1:
                self.op("tensor", fn, reads, writes, nowait=True)
            else:
                self.op("tensor", fn, (), (), sig=False, nowait=True)

    def barrier(self):
        toks = []
        for k, s in self.csem.items():
            if self.ccnt[k] > 0:
                toks.append((s, self.ccnt[k]))
        for k in self.dpool:
            for s, v in self.dpool[k]:
                if v > 0:
                    toks.append((s, v))
        for eng in self.ENGS:
            waits = []
            for s, v in toks:
                if self.waited.get((eng, s), 0) >= v:
                    continue
                self.waited[(eng, s)] = v
                waits.append((s, v))
            self.q[eng].append((waits, None, None, 0))

    def finish(self):
        waits = []
        for k in self.dpool:
            for s, v in self.dpool[k]:
                if v > 0:
                    waits.append((s, v))
        self.q["sync"].append((waits, None, None, 0))

    def emit(self, eng, e):
        for waits, fn, tok, inc in self.q[eng]:
            for s, v in waits:
                e.wait_ge(s, v)
            if fn is None:
                continue
            inst = fn(e)
            if tok is not None:
                inst.then_inc(tok[0], inc)


PC = {}
_o = 0
for _n, _w in (("ffn1_pre", 16), ("ffn1_post", 16), ("mix_pre", 16), ("mix_post", 16), ("ffn2_pre", 16),
               ("ffn2_post", 16), ("mu", 26), ("dbf", 8), ("dbb", 8), ("ibf", 8), ("ibb", 8), ("kns", 8),
               ("kim", 8), ("bsc", 8), ("gnw", 8), ("gnb", 8), ("flag", 1)):
    PC[_n] = _o
    _o += _w
NPCOL = _o
DC = {}
_o = 0
for _n, _w in (("ffn1_posth", 16), ("ffn2_posth", 16), ("hmu", 26), ("omu", 26), ("omk", 8), ("hbs", 8)):
    DC[_n] = _o
    _o += _w
NDCOL = _o

CC = {"ident": 0, "ones": 128, "bones": 256, "m4f": 384, "m4b": 896, "mlf": 1408, "mlb": 1536}
NCCOL = 1664


def build_program(dbg=None):
    dbg = dbg or {}
    ntiles = dbg.get("ntiles", NTOK // TT)
    do_a = dbg.get("A", True)
    do_b = dbg.get("B", True)
    do_c = dbg.get("C", True)
    nc = bass.Bass("TRN2", target_bir_lowering=False)
    ext = lambda n, s, dt=F32: nc.dram_tensor(n, s, dt, kind="ExternalInput").ap()
    x_d = ext("x", [NTOK, D])
    par_d = ext("params", [128, NPCOL])
    con_d = ext("consts", [128, NCCOL])
    w1g = ext("ffn1_w_gate", [D, DFF]); w1u = ext("ffn1_w_up", [D, DFF]); w1d = ext("ffn1_w_down", [DFF, D])
    w2g = ext("ffn2_w_gate", [D, DFF]); w2u = ext("ffn2_w_up", [D, DFF]); w2d = ext("ffn2_w_down", [DFF, D])
    win = ext("w_in", [D, 6400]); wout = ext("w_out", [D, D])
    lupf_d = ext("lupf", [128, 1024]); lupb_d = ext("lupb", [128, 1024]); gup_d = ext("gup", [128, 1024])
    yf_d = nc.dram_tensor("yf", [8, 128, NTOK], F32, kind="ExternalOutput").ap()
    G_d = ext("natG", [8, 128, 2, 2, 7, 64])
    S_d = ext("natS", [7, 16, 128, 6, 64])
    y_d = nc.dram_tensor("y", [NTOK, D], F32, kind="ExternalOutput").ap()
    skind = "ExternalOutput"
    x1T_d = nc.dram_tensor("x1T", [NDC, 128, NTOK], F32, kind=skind).ap()
    zr_d = nc.dram_tensor("zr", [RW_CH, 128, NTOK], F32, kind=skind).ap()
    zqk_d = nc.dram_tensor("zqk", [16, 128, NTOK], BF16, kind=skind).ap()
    zv_d = nc.dram_tensor("zv", [NTOK, 1024], BF16, kind=skind).ap()
    if dbg.get("oT_in"):
        oT_d = ext("oT", [16, 128, NTOK], BF16)
    else:
        oT_d = nc.dram_tensor("oT", [16, 128, NTOK], BF16, kind=skind).ap()

    es = ExitStack()
    P = Prog(nc, es)
    sb = lambda n, s, dt=F32: es.enter_context(nc.sbuf_tensor(n, s, dt))
    par = sb("par", [128, NPCOL]); der = sb("der", [128, NDCOL]); con = sb("con", [128, NCCOL])
    identb = sb("identb", [128, 128], BF16)
    onesb128 = sb("onesb128", [128, 128], BF16)
    epsr = sb("epsr", [128, 1]); t_epsr = Tk()
    P.op("vector", lambda e: e.memset(epsr[:], 1e-6), (), (t_epsr,))
    t_par, t_der, t_con, t_identb = Tk(), Tk(), Tk(), Tk()
    P.op("sync", lambda e: e.dma_start(out=par[:], in_=par_d[:, :]), (), (t_par,), dma=True)
    P.op("sync", lambda e: e.dma_start(out=con[:], in_=con_d[:, :]), (), (t_con,), dma=True)

    def pcol(name, i=0, n=1):
        return par[:, PC[name] + i:PC[name] + i + n]

    def dcol(name, i=0, n=1):
        return der[:, DC[name] + i:DC[name] + i + n]

    def dslice(name, n):
        return der[:, DC[name]:DC[name] + n]

    def pslice(name, n):
        return par[:, PC[name]:PC[name] + n]

    V = "vector"
    P.op(V, lambda e: e.tensor_scalar(out=dslice("ffn1_posth", 16), in0=pslice("ffn1_post", 16), scalar1=0.5,
                                      scalar2=None, op0=ALU.mult), (t_par,), (t_der,))
    P.op(V, lambda e: e.tensor_scalar(out=dslice("ffn2_posth", 16), in0=pslice("ffn2_post", 16), scalar1=0.5,
                                      scalar2=None, op0=ALU.mult), (t_par,), (t_der,))
    P.op(V, lambda e: e.tensor_scalar(out=dslice("hmu", 26), in0=pslice("mu", 26), scalar1=0.5,
                                      scalar2=None, op0=ALU.mult), (t_par,), (t_der,))
    P.op(V, lambda e: e.tensor_scalar(out=dslice("omu", 26), in0=pslice("mu", 26), scalar1=-1.0,
                                      scalar2=1.0, op0=ALU.mult, op1=ALU.add), (t_par,), (t_der,))
    P.op(V, lambda e: e.tensor_scalar(out=dslice("omk", 8), in0=pslice("kim", 8), scalar1=-1.0,
                                      scalar2=1.0, op0=ALU.mult, op1=ALU.add), (t_par,), (t_der,))
    P.op(V, lambda e: e.tensor_scalar(out=dslice("hbs", 8), in0=pslice("bsc", 8), scalar1=0.5,
                                      scalar2=None, op0=ALU.mult), (t_par,), (t_der,))
    P.op(V, lambda e: e.tensor_copy(out=identb[:], in_=con[:, 0:128]), (t_con,), (t_identb,))
    P.op(V, lambda e: e.tensor_copy(out=onesb128[:], in_=con[:, 128:256]), (t_con,), (t_identb,))
    identf = con[:, CC["ident"]:CC["ident"] + 128]
    onesf = con[:, CC["ones"]:CC["ones"] + 128]
    bonesf = con[:, CC["bones"]:CC["bones"] + 128]

    def dense_phases(do_a, do_c, tiles):
        sfx = "A" if do_a else "C"
        with ExitStack() as ds:
            sbd = lambda n, s, dt=F32: ds.enter_context(nc.sbuf_tensor(n + sfx, s, dt))
            psd = lambda n, s, dt=F32: ds.enter_context(nc.psum_tensor(n + sfx, s, dt))
            big = sbd("big", [128, 8192])
            xT = sbd("xT", [128, NDC, TT])
            hT = sbd("hT", [128, NDC, TT], BF16)
            aT = sbd("aT", [128, NFC, TT], BF16)
            wgu = [sbd("wgu%d" % i, [128, NDC, 256], BF16) for i in range(3)]
            wdb = [sbd("wdb%d" % i, [128, NFC, 128], BF16) for i in range(2)]
            sq = [sbd("sq%d" % i, [128, TT], BF16) for i in range(2)]
            sl = [sbd("sl%d" % i, [128, TT]) for i in range(2)]
            rstd = sbd("rstd", [128, TT]); rtmp = sbd("rtmp", [128, TT])
            zst = [sbd("zst%d" % i, [128, 2, TT]) for i in range(1)]
            zsb = [sbd("zsb%d" % i, [128, 2, TT], BF16) for i in range(1)]
            vst = [sbd("vst%d" % i, [128, 4, 256], BF16) for i in range(1)]
            psb = [psd("psb%d" % i, [128, TT]) for i in range(8)]
            t_big = [Tk() for _ in range(16)]
            t_xT = [Tk() for _ in range(NDC)]
            t_hT = [Tk() for _ in range(NDC)]
            t_aT = [Tk() for _ in range(NFC)]
            t_wgu = [Tk() for _ in range(3)]
            t_wdb = [Tk() for _ in range(2)]
            t_sq = [Tk(), Tk()]; t_sl = [Tk(), Tk()]
            t_rstd, t_rtmp = Tk(), Tk()
            t_zst = [Tk(), Tk()]; t_zsb = [Tk(), Tk()]; t_vst = [Tk(), Tk()]
            t_ps = [Tk() for _ in range(8)]
            cnt = {"wgu": 0, "wdb": 0, "sq": 0, "sl": 0, "zst": 0, "ps_g": 0, "ps_u": 0, "ps_d": 0, "vst": 0}

            def rot(name, n):
                i = cnt[name] % n
                cnt[name] += 1
                return i

            def fT(dc):
                return big[:, dc * TT:(dc + 1) * TT]

            def stat_begin():
                pass

            def sumsq_chunk(src_ap, src_tk, dc, from_psum_eng="scalar"):
                i = rot("sq", 2)
                P.op("scalar", lambda e: e.activation(out=sq[i][:], in_=src_ap, func=AF.Square), (src_tk,), (t_sq[i],))
                P.op("tensor", lambda e: e.matmul(psb[6][:], onesb128[:], sq[i][:], start=(dc == 0), stop=(dc == NDC - 1)),
                     (t_sq[i], t_identb), (t_ps[6],))

            def make_rstd():
                P.op("scalar", lambda e: e.activation(out=rtmp[:], in_=psb[6][:], func=AF.Ln, bias=epsr[:, 0:1],
                                                      scale=1.0 / D), (t_ps[6], t_epsr), (t_rtmp,))
                P.op("scalar", lambda e: e.activation(out=rstd[:], in_=rtmp[:], func=AF.Exp, scale=-0.5), (t_rtmp,), (t_rstd,))

            def make_h(gname):
                for dc in range(NDC):
                    P.op(V, lambda e, dc=dc: e.scalar_tensor_tensor(out=hT[:, dc, :], in0=xT[:, dc, :],
                                                                    scalar=pcol(gname, dc), in1=rstd[:],
                                                                    op0=ALU.mult, op1=ALU.mult),
                         (t_xT[dc], t_rstd, t_par), (t_hT[dc],))

            def load_w_cols(wd, c0, ncols):
                i = rot("wgu", 3)
                src = wd.rearrange("(c p) f -> p c f", p=128)[:, :, c0:c0 + ncols]
                P.op("gpsimd", lambda e: e.dma_start(out=wgu[i][:, :, 0:ncols], in_=src), (), (t_wgu[i],), dma=True)
                return i

            def residual_update(gcols_ap_fn, src_fT=True):
                for dc in range(NDC):
                    P.op(V, lambda e, dc=dc: e.scalar_tensor_tensor(out=fT(dc), in0=fT(dc), scalar=gcols_ap_fn(dc),
                                                                    in1=rstd[:], op0=ALU.mult, op1=ALU.mult),
                         (t_big[dc], t_rstd, t_par, t_der), (t_big[dc],))
                    P.op(V, lambda e, dc=dc: e.tensor_tensor(out=xT[:, dc, :], in0=xT[:, dc, :], in1=fT(dc), op=ALU.add),
                         (t_big[dc], t_xT[dc]), (t_xT[dc],))

            def ffn(pre_name, posth_name, wg, wu, wd):
                for dc in range(NDC):
                    sumsq_chunk(xT[:, dc, :], t_xT[dc], dc)
                make_rstd()
                make_h(pre_name)
                for fb in range(NFC // 2):
                    ig = load_w_cols(wg, fb * 256, 256)
                    iu = load_w_cols(wu, fb * 256, 256)
                    for j in range(2):
                        fc = fb * 2 + j
                        pg = rot("ps_g", 2)
                        pu = 2 + rot("ps_u", 2)
                        P.mm_group([(psb[pg][:], wgu[ig][:, dc, j * 128:(j + 1) * 128], hT[:, dc, :]) for dc in range(NDC)],
                                   [t_wgu[ig]] + t_hT, [t_ps[pg]])
                        P.mm_group([(psb[pu][:], wgu[iu][:, dc, j * 128:(j + 1) * 128], hT[:, dc, :]) for dc in range(NDC)],
                                   [t_wgu[iu]] + t_hT, [t_ps[pu]])
                        si = rot("sl", 2)
                        P.op("scalar", lambda e, pg=pg, si=si: e.activation(out=sl[si][:], in_=psb[pg][:], func=AF.Silu),
                             (t_ps[pg],), (t_sl[si],))
                        P.op(V, lambda e, pu=pu, si=si, fc=fc: e.tensor_tensor(out=aT[:, fc, :], in0=sl[si][:], in1=psb[pu][:],
                                                                               op=ALU.mult),
                             (t_sl[si], t_ps[pu]), (t_aT[fc],))
                for dc in range(NDC):
                    i = rot("wdb", 2)
                    src = wd.rearrange("(c p) d -> p c d", p=128)[:, :, dc * 128:(dc + 1) * 128]
                    P.op("gpsimd", lambda e, i=i, src=src: e.dma_start(out=wdb[i][:], in_=src), (), (t_wdb[i],), dma=True)
                    pd = 4 + rot("ps_d", 2)
                    P.mm_group([(psb[pd][:], wdb[i][:, fc, :], aT[:, fc, :]) for fc in range(NFC)],
                               [t_wdb[i]] + t_aT, [t_ps[pd]])
                    P.op(V, lambda e, dc=dc, pd=pd: e.tensor_copy(out=fT(dc), in_=psb[pd][:]), (t_ps[pd],), (t_big[dc],))
                    sumsq_chunk(fT(dc), t_big[dc], dc)
                make_rstd()
                residual_update(lambda dc: dcol(posth_name, dc))

            for tt in tiles:
                t0 = tt * TT
                if do_a:
                    def x_load(tq):
                        src = x_d[tq * TT:(tq + 1) * TT, :].rearrange("(b p) d -> p b d", p=128)
                        P.op("sync", lambda e, src=src: e.dma_start(out=big[:, :].rearrange("p (b d) -> p b d", b=4), in_=src),
                             (), t_big, dma=True)

                    def x_transpose():
                        for dc in range(NDC):
                            for tb in range(4):
                                P.op("tensor", lambda e, dc=dc, tb=tb: e.transpose(
                                    psb[7][:, tb * 128:(tb + 1) * 128],
                                    big[:, tb * 2048 + dc * 128: tb * 2048 + (dc + 1) * 128], identf),
                                     t_big + [t_con], (t_ps[7],), sig=(tb == 3), nowait=(tb != 0))
                            P.op("scalar", lambda e, dc=dc: e.copy(out=xT[:, dc, :], in_=psb[7][:]), (t_ps[7],), (t_xT[dc],))
                    if tt == tiles[0]:
                        x_load(tt)
                        x_transpose()
                    ffn("ffn1_pre", "ffn1_posth", w1g, w1u, w1d)
                    dst = x1T_d.rearrange("c p t -> p c t")[:, :, t0:t0 + TT]
                    P.op("sync", lambda e, dst=dst: e.dma_start(out=dst, in_=xT[:]), t_xT, (), dma=True)
                    for dc in range(NDC):
                        sumsq_chunk(xT[:, dc, :], t_xT[dc], dc)
                    make_rstd()
                    make_h("mix_pre")
                    has_next = (tt != tiles[-1])
                    if has_next:
                        x_load(tt + 1)
                    for cb in range(25):
                        if has_next and cb == 10:
                            x_transpose()
                        iw = load_w_cols(win, cb * 256, 256)
                        if cb < 21:
                            zi = 0
                            for j in range(2):
                                pd = 4 + rot("ps_d", 2)
                                P.mm_group([(psb[pd][:], wgu[iw][:, dc, j * 128:(j + 1) * 128], hT[:, dc, :]) for dc in range(NDC)],
                                           [t_wgu[iw]] + t_hT, [t_ps[pd]])
                                if cb < 13:
                                    P.op("scalar", lambda e, zi=zi, j=j, pd=pd: e.copy(out=zst[zi][:, j, :], in_=psb[pd][:]),
                                         (t_ps[pd],), (t_zst[zi],))
                                else:
                                    P.op("scalar", lambda e, zi=zi, j=j, pd=pd: e.copy(out=zsb[zi][:, j, :], in_=psb[pd][:]),
                                         (t_ps[pd],), (t_zsb[zi],))
                            if cb < 13:
                                dst = zr_d.rearrange("c p t -> p c t")[:, cb * 2:cb * 2 + 2, t0:t0 + TT]
                                P.op("sync", lambda e, dst=dst, zi=zi: e.dma_start(out=dst, in_=zst[zi][:]), (t_zst[zi],), (), dma=True)
                            else:
                                c2 = (cb - 13) * 2
                                dst = zqk_d.rearrange("c p t -> p c t")[:, c2:c2 + 2, t0:t0 + TT]
                                P.op("sync", lambda e, dst=dst, zi=zi: e.dma_start(out=dst, in_=zsb[zi][:]), (t_zsb[zi],), (), dma=True)
                        else:
                            vb = cb - 21
                            vi = 0
                            for tb in range(4):
                                pd = 4 + rot("ps_d", 2)
                                P.mm_group([(psb[pd][:, 0:256], hT[:, dc, tb * 128:(tb + 1) * 128], wgu[iw][:, dc, :]) for dc in range(NDC)],
                                           [t_wgu[iw]] + t_hT, [t_ps[pd]])
                                P.op("scalar", lambda e, vi=vi, tb=tb, pd=pd: e.copy(out=vst[vi][:, tb, :], in_=psb[pd][:, 0:256]),
                                     (t_ps[pd],), (t_vst[vi],))
                            dst = zv_d[t0:t0 + TT, vb * 256:(vb + 1) * 256].rearrange("(b p) f -> p b f", p=128)
                            P.op("sync", lambda e, dst=dst, vi=vi: e.dma_start(out=dst, in_=vst[vi][:]), (t_vst[vi],), (), dma=True)
                if do_c:
                    src = x1T_d.rearrange("c p t -> p c t")[:, :, t0:t0 + TT]
                    P.op("sync", lambda e, src=src: e.dma_start(out=xT[:], in_=src), (), t_xT, dma=True)
                    src = oT_d.rearrange("c p t -> p c t")[:, :, t0:t0 + TT]
                    P.op("sync", lambda e, src=src: e.dma_start(out=hT[:], in_=src), (), t_hT, dma=True)
                    for db in range(8):
                        iw = load_w_cols(wout, db * 256, 256)
                        for j in range(2):
                            dc = db * 2 + j
                            pd = 4 + rot("ps_d", 2)
                            P.mm_group([(psb[pd][:], wgu[iw][:, mc, j * 128:(j + 1) * 128], hT[:, mc, :]) for mc in range(NDC)],
                                       [t_wgu[iw]] + t_hT, [t_ps[pd]])
                            P.op(V, lambda e, dc=dc, pd=pd: e.tensor_copy(out=fT(dc), in_=psb[pd][:]), (t_ps[pd],), (t_big[dc],))
                            sumsq_chunk(fT(dc), t_big[dc], dc)
                    make_rstd()
                    residual_update(lambda dc: pcol("mix_post", dc))
                    ffn("ffn2_pre", "ffn2_posth", w2g, w2u, w2d)
                    for tb in range(4):
                        for d4 in range(4):
                            for j in range(4):
                                dc = d4 * 4 + j
                                P.op("tensor", lambda e, dc=dc, tb=tb, j=j: e.transpose(
                                    psb[7][:, j * 128:(j + 1) * 128], xT[:, dc, tb * 128:(tb + 1) * 128], identf),
                                     [t_xT[dc], t_con], (t_ps[7],), sig=(j == 3), nowait=False)
                            P.op("scalar", lambda e, tb=tb, d4=d4: e.copy(
                                out=big[:, tb * 2048 + d4 * 512: tb * 2048 + (d4 + 1) * 512], in_=psb[7][:]),
                                 (t_ps[7],), t_big)
                    dst = y_d[t0:t0 + TT, :].rearrange("(b p) d -> p b d", p=128)
                    P.op("sync", lambda e, dst=dst: e.dma_start(out=dst, in_=big[:, :].rearrange("p (b d) -> p b d", b=4)),
                         t_big, (), dma=True)

    if do_a:
        dense_phases(True, False, list(range(ntiles)))
        P.barrier()
    if do_b:
        mixer_phase(nc, P, dbg, locals())
        P.barrier()
    if do_c:
        dense_phases(False, True, list(range(ntiles)))
    P.finish()
    with nc.Block() as block:
        @block.sync
        def _(e):
            P.emit("sync", e)

        @block.gpsimd
        def _(e):
            P.emit("gpsimd", e)

        @block.scalar
        def _(e):
            P.emit("scalar", e)

        @block.vector
        def _(e):
            P.emit("vector", e)

        @block.tensor
        def _(e):
            P.emit("tensor", e)
    es.close()
    return nc


def mixer_phase(nc, P, dbg, env):
    units = dbg.get("units", list(range(NUNIT)))
    if dbg.get("nat", True):
        nat_phase(nc, P, dbg, env, units)
        P.barrier()
    if dbg.get("rwkv", True):
        rwkv_phase(nc, P, dbg, env, units)


NEG = -30000.0
SPECIAL = [(0, 29), (0, 30), (0, 31), (1, 0), (1, 1), (1, 2), (1, 3)]


def nat_phase(nc, P, dbg, env, units):
    zqk_d, zv_d, oT_d, G_d, S_d, con, t_con = (env[k] for k in ("zqk_d", "zv_d", "oT_d", "G_d", "S_d", "con", "t_con"))
    V = "vector"
    with ExitStack() as ds:
        sbd = lambda n, s, dt=F32: ds.enter_context(nc.sbuf_tensor("n_" + n, s, dt))
        psd = lambda n, s, dt=F32: ds.enter_context(nc.psum_tensor("n_" + n, s, dt))
        qT = [sbd("qT%d" % i, [128, UNIT], BF16) for i in range(2)]
        kT = [sbd("kT%d" % i, [128, UNIT + 512], BF16) for i in range(2)]
        vx = [sbd("vx%d" % i, [128, 20, 2, 80], BF16) for i in range(2)]
        vxo = [sbd("vxo%d" % i, [128, 20, 2, 80], BF16) for i in range(2)]
        G = [sbd("G%d" % i, [128, 2, 2, 7, 64]) for i in range(2)]
        Ssp = [sbd("S%d" % i, [128, 6, 64]) for i in range(2)]
        sbuf = [sbd("sb%d" % i, [128, 384]) for i in range(2)]
        pT = [sbd("pT%d" % i, [128, 384], BF16) for i in range(2)]
        onesb = sbd("onesb", [128, 64], BF16)
        rs = sbd("rs", [128, 512])
        srow = sbd("srow", [128, 512])
        ob = [sbd("ob%d" % i, [128, 512], BF16) for i in range(2)]
        ps_s = [psd("s%d" % i, [128, 512]) for i in range(2)]
        ps_o = [psd("o%d" % i, [128, 512]) for i in range(4)]
        ps_bc = psd("bc", [128, 512])
        t_qT = [Tk(), Tk()]; t_kT = [Tk(), Tk()]; t_vx = [Tk(), Tk()]; t_G = [Tk(), Tk()]; t_S = [Tk(), Tk()]
        t_sb = [Tk(), Tk()]; t_pT = [Tk(), Tk()]; t_ones = Tk(); t_rs = Tk(); t_ob = [Tk(), Tk()]
        t_ps_s = [Tk(), Tk()]; t_ps_o = [Tk() for _ in range(4)]; t_bc = Tk(); t_srow = Tk()
        P.op(V, lambda e: e.tensor_copy(out=onesb[:], in_=con[:, CC["ones"]:CC["ones"] + 64]), (t_con,), (t_ones,))
        P.op(V, lambda e: e.memset(kT[0][:], 0.0), (), (t_kT[0],))
        P.op(V, lambda e: e.memset(kT[1][:], 0.0), (), (t_kT[1],))
        for vt, tt_ in ((vx[0], t_vx[0]), (vx[1], t_vx[1]), (vxo[0], t_vx[0]), (vxo[1], t_vx[1])):
            P.op(V, lambda e, vt=vt: e.memset(vt[:], 0.0), (), (tt_,))
            P.op(V, lambda e, vt=vt: e.memset(vt[:, :, :, 64:65], 1.0), (), (tt_,))
        cnt = {"b": 0, "s": 0, "sp": 0, "g": 0, "ob": 0}
        for u in units:
            U0 = u * UNIT
            for hp in range(8):
                b = cnt["b"] % 2
                cnt["b"] += 1
                P.op("sync", lambda e, b=b, hp=hp, U0=U0: e.dma_start(out=qT[b][:], in_=zqk_d[hp, :, U0:U0 + UNIT]),
                     (), (t_qT[b],), dma=True)
                lo = U0 - 256 if u == 1 else U0
                hi = U0 + UNIT + 256 if u == 0 else U0 + UNIT
                P.op("sync", lambda e, b=b, hp=hp, lo=lo, hi=hi, U0=U0: e.dma_start(
                    out=kT[b][:, lo - (U0 - 256):hi - (U0 - 256)], in_=zqk_d[8 + hp, :, lo:hi]), (), (t_kT[b],), dma=True)
                blo = (lo - (U0 - 256)) // 128
                nb = (hi - lo) // 128
                for jh in range(2):
                    P.op("sync", lambda e, b=b, hp=hp, lo=lo, hi=hi, blo=blo, nb=nb, jh=jh: e.dma_start(
                        out=vx[b][:, blo:blo + nb, jh, 0:64],
                        in_=zv_d[lo:hi, hp * 128 + jh * 64:hp * 128 + (jh + 1) * 64].rearrange("(k p) f -> p k f", p=128)),
                         (), (t_vx[b],), dma=True)
                lob = lo - (U0 - 256)
                hib = hi - (U0 - 256)
                kmin = -((64 - lob) // 128)
                kmax = (hib - 64) // 128 - 1
                g0 = (U0 - 256) + 64 + 128 * kmin
                g1 = (U0 - 256) + 64 + 128 * (kmax + 1)
                for jh in range(2):
                    P.op("sync", lambda e, b=b, hp=hp, g0=g0, g1=g1, kmin=kmin, kmax=kmax, jh=jh: e.dma_start(
                        out=vxo[b][:, kmin:kmax + 1, jh, 0:64],
                        in_=zv_d[g0:g1, hp * 128 + jh * 64:hp * 128 + (jh + 1) * 64].rearrange("(k p) f -> p k f", p=128)),
                         (), (t_vx[b],), dma=True)
                P.op("sync", lambda e, b=b, hp=hp: e.dma_start(out=G[b][:], in_=G_d[hp]), (), (t_G[b],), dma=True)
                for i8 in range(4):
                    po = cnt["ob"] % 2
                    cnt["ob"] += 1
                    pending = [None]
                    for ii in range(8):
                        i = i8 * 8 + ii
                        for j in range(2):
                            hb = 64 * j
                            h = hp * 2 + j
                            if (u, i) in SPECIAL:
                                spi = SPECIAL.index((u, i))
                                nch = 6
                                er0 = 24 if u == 0 else -4
                                si = cnt["sp"] % 2
                                cnt["sp"] += 1
                                P.op("sync", lambda e, si=si, spi=spi, h=h: e.dma_start(out=Ssp[si][:], in_=S_d[spi, h]),
                                     (), (t_S[si],), dma=True)
                                tab = Ssp[si][:, :, :]
                                t_tab = t_S[si]
                            else:
                                ws = min(max(i - 4, 0), 24)
                                off = i - ws
                                nch = 4
                                er0 = ws
                                m0 = 7 - off
                                tab = G[b][:, j, m0 % 2, m0 // 2:m0 // 2 + 4, :]
                                t_tab = t_G[b]
                            ks = (er0 + 4) * 64
                            s_i = cnt["s"] % 2
                            cnt["s"] += 1
                            for c in range(nch):
                                P.op("tensor", lambda e, s_i=s_i, c=c, b=b, hb=hb, ks=ks, i=i: e.matmul(
                                    ps_s[s_i][:, c * 64:(c + 1) * 64], kT[b][hb:hb + 64, ks + c * 128:ks + (c + 1) * 128],
                                    qT[b][hb:hb + 64, i * 64:(i + 1) * 64], start=True, stop=True),
                                     (t_kT[b], t_qT[b]), (t_ps_s[s_i],), sig=(c == nch - 1))
                            n = nch * 64
                            P.op(V, lambda e, s_i=s_i, n=n, nch=nch, tab=tab: e.scalar_tensor_tensor(
                                out=sbuf[s_i][:, 0:n].rearrange("p (c q) -> p c q", c=nch),
                                in0=ps_s[s_i][:, 0:n].rearrange("p (c q) -> p c q", c=nch), scalar=0.125, in1=tab,
                                op0=ALU.mult, op1=ALU.add), (t_ps_s[s_i], t_tab), (t_sb[s_i],))
                            P.op("scalar", lambda e, s_i=s_i, n=n: e.activation(out=pT[s_i][:, 0:n], in_=sbuf[s_i][:, 0:n], func=AF.Exp),
                                 (t_sb[s_i],), (t_pT[s_i],))
                            def pv_stage(nch=nch, er0=er0, po=po, j=j, ii=ii, b=b, s_i=s_i):
                                pso = ps_o[po * 2 + j]
                                tpso = t_ps_o[po * 2 + j]
                                for c in range(nch):
                                    R = er0 + 4 + 2 * c
                                    vsrc = vx[b] if R % 2 == 0 else vxo[b]
                                    blk = R // 2
                                    P.op("tensor", lambda e, vsrc=vsrc, blk=blk, c=c: e.matmul(
                                        pso[0:65, ii * 64:(ii + 1) * 64], vsrc[:, blk, j, 0:65],
                                        pT[s_i][:, c * 64:(c + 1) * 64], start=(c == 0), stop=(c == nch - 1)),
                                         (t_vx[b], t_pT[s_i]), (tpso,), sig=(c == nch - 1))
                            if pending[0] is not None:
                                pending[0]()
                            pending[0] = pv_stage
                    if pending[0] is not None:
                        pending[0]()
                        pending[0] = None
                    for j in range(2):
                        pso = ps_o[po * 2 + j]
                        tpso = t_ps_o[po * 2 + j]
                        P.op("scalar", lambda e, pso=pso: e.copy(out=srow[64:65, :], in_=pso[64:65, :]), (tpso,), (t_srow,))
                        P.mm_group([(ps_bc[0:64, :], con[64:65, CC["ones"]:CC["ones"] + 64], srow[64:65, :])], (t_con, t_srow), (t_bc,))
                        P.op("scalar", lambda e: e.activation(out=rs[0:64, :], in_=ps_bc[0:64, :], func=AF.Ln), (t_bc,), (t_rs,))
                        P.op("scalar", lambda e: e.activation(out=rs[0:64, :], in_=rs[0:64, :], func=AF.Exp, scale=-1.0), (t_rs,), (t_rs,))
                        ob_ = ob[j]
                        P.op(V, lambda e, pso=pso, ob_=ob_: e.tensor_tensor(out=ob_[0:64, :], in0=pso[0:64, :], in1=rs[0:64, :], op=ALU.mult),
                             (tpso, t_rs), (t_ob[j],))
                        P.op("sync", lambda e, ob_=ob_, hp=hp, U0=U0, i8=i8, j=j: e.dma_start(
                            out=oT_d[8 + hp, 64 * j:64 * j + 64, U0 + i8 * 512:U0 + (i8 + 1) * 512], in_=ob_[0:64, :]), (t_ob[j],), (), dma=True)


def rwkv_phase(nc, P, dbg, env, units):
    zr_d, oT_d, yf_d, con, t_con, t_par, t_der, identb, t_identb = (env[k] for k in (
        "zr_d", "oT_d", "yf_d", "con", "t_con", "t_par", "t_der", "identb", "t_identb"))
    lup_dd = {"f": env["lupf_d"], "b": env["lupb_d"]}
    gup_d = env["gup_d"]
    pcol, dcol = env["pcol"], env["dcol"]
    T = UNIT
    V = "vector"
    order = [(0, "f"), (1, "f"), (2, "f"), (2, "b"), (1, "b"), (0, "b")]
    order = [(u, d) for (u, d) in order if u in units]
    if dbg.get("dump"):
        order = order if dbg.get("dump2") else order[dbg.get("dump_pass", 0):][:1]
    bonesf = con[:, CC["bones"]:CC["bones"] + 128]
    onesf = con[:, CC["ones"]:CC["ones"] + 128]
    with ExitStack() as ds:
        sbd = lambda n, s, dt=F32: ds.enter_context(nc.sbuf_tensor("r_" + n, s, dt))
        psd = lambda n, s, dt=F32: ds.enter_context(nc.psum_tensor("r_" + n, s, dt))
        F = [sbd("F%d" % i, [128, T + 2]) for i in range(10)]
        tF = [Tk() for _ in range(10)]
        TMP0, TMP1, KK, R, K, VV, E, A, L, YA = range(10)
        B = TMP1
        AR = sbd("AR", [128, 16, 256], BF16); tAR = Tk()
        Bf = sbd("Bf", [128, T], BF16); tBf = Tk()
        Kf = sbd("Kf", [128, T], BF16); tKf = Tk()
        vb = sbd("vb", [128, T], BF16); tvb = Tk()
        Btm = sbd("Btm", [128, 16, 128], BF16); tBtm = Tk()
        Ktm = sbd("Ktm", [128, 16, 128], BF16); tKtm = Tk()
        Vtm = sbd("Vtm", [128, 16, 128], BF16); tVtm = Tk()
        Am = sbd("Am", [128, 32, 512], BF16); tAm = [Tk() for _ in range(32)]
        TTs = sbd("TTs", [128, 32, 128], BF16); tTTs = [Tk() for _ in range(32)]
        Xb = sbd("Xb", [128, 8, 2, 256], BF16); tXb = [[Tk(), Tk()] for _ in range(8)]
        TTb = sbd("TTb", [128, 8, 2, 128], BF16); tTTb = [[Tk(), Tk()] for _ in range(8)]
        zs24b = sbd("zs24b", [128, T], BF16); tz24 = Tk()
        glb = sbd("glb", [128, T], BF16); tglb = Tk()
        lup = {"f": sbd("lupf", [128, 1024], BF16), "b": sbd("lupb", [128, 1024], BF16)}
        gup = sbd("gup", [128, 1024], BF16); tlw = Tk()
        mids = sbd("mids", [128, 16]); tots = sbd("tots", [128, 16]); biasm = sbd("biasm", [128, 16])
        nbiasm = sbd("nbiasm", [128, 16]); epsk = sbd("epsk", [128, 2]); tsm0 = Tk()
        scj = sbd("scj", [128, 16]); sci = sbd("sci", [128, 1]); sct = sbd("sct", [128, 16]); tsm = Tk()
        Hf = sbd("Hf", [128, 64]); Hb = sbd("Hb", [128, 64], BF16); Ht = sbd("Ht", [128, 64]); tH = Tk(); tHb = Tk(); tHt = Tk()
        Hc = sbd("Hc", [128, 2, 8, 64]); tHc = Tk()
        Xs = sbd("Xs", [128, 128], BF16); tXs = Tk()
        Ub = sbd("Ub", [128, 128], BF16); tUb = Tk()
        fin = [sbd("fin%d" % i, [128, 512]) for i in range(4)]; tfin = [Tk() for _ in range(4)]
        finb = sbd("finb", [128, 512], BF16); tfinb = Tk()
        pb = [psd("pb%d" % i, [128, 512]) for i in range(2)]; tpb = [Tk(), Tk()]
        pA = [psd("pA%d" % i, [128, 512]) for i in range(2)]; tpA = [Tk(), Tk()]
        pX = [psd("pX%d" % i, [128, 512]) for i in range(2)]; tpX = [Tk(), Tk()]
        pS = psd("pS", [128, 512]); tpS = {k: Tk() for k in "SUHY"}
        pT = psd("pT", [128, 1024], BF16); tpT = Tk()
        cnt = {"pb": 0, "pA": 0, "pX": 0}

        def rot(n):
            i = cnt[n] % 2
            cnt[n] += 1
            return i

        vop = lambda fn, r, w: P.op(V, fn, r, w)
        aop = lambda fn, r, w: P.op("scalar", fn, r, w)
        fa = lambda i: F[i][:, 0:T]
        blkc = lambda ap, k: ap[:, k * 512:(k + 1) * 512]
        for d in ("f", "b"):
            P.op("gpsimd", lambda e, d=d: e.dma_start(out=lup[d][:], in_=lup_dd[d][:, :]), (), (tlw,), dma=True)
        P.op("gpsimd", lambda e: e.dma_start(out=gup[:], in_=gup_d[:, :]), (), (tlw,), dma=True)
        vop(lambda e: e.memset(Hc[:], 0.0), (), (tHc,))
        vop(lambda e: e.memset(epsk[:, 0:1], 1e-24), (), (tsm0,))
        vop(lambda e: e.memset(epsk[:, 1:2], 64e-5), (), (tsm0,))

        def load_shift(ch, dst, u):
            U0 = u * T
            raw = F[TMP0]
            P.op("sync", lambda e: e.dma_start(out=raw[:, 1:T + 1], in_=zr_d[ch, :, U0:U0 + T]), (), (tF[TMP0],), dma=True)
            if u == 1:
                P.op("sync", lambda e: e.dma_start(out=raw[:, 0:1], in_=zr_d[ch, :, U0 - 1:U0], allow_slow_non_contiguous=True), (), (tF[TMP0],), dma=True)
                vop(lambda e: e.tensor_scalar(out=raw[:, 0:1], in0=raw[:, 0:1], scalar1=pcol("flag"), scalar2=None, op0=ALU.mult),
                    (tF[TMP0], t_par), (tF[TMP0],))
            else:
                vop(lambda e: e.memset(raw[:, 0:1], 0.0), (), (tF[TMP0],))
            if u == 0:
                P.op("sync", lambda e: e.dma_start(out=raw[:, T + 1:T + 2], in_=zr_d[ch, :, U0 + T:U0 + T + 1], allow_slow_non_contiguous=True), (), (tF[TMP0],), dma=True)
                vop(lambda e: e.tensor_scalar(out=raw[:, T + 1:T + 2], in0=raw[:, T + 1:T + 2], scalar1=pcol("flag"), scalar2=None,
                                              op0=ALU.mult), (tF[TMP0], t_par), (tF[TMP0],))
            else:
                vop(lambda e: e.memset(raw[:, T + 1:T + 2], 0.0), (), (tF[TMP0],))
            vop(lambda e: e.tensor_tensor(out=fa(TMP1), in0=raw[:, 0:T], in1=raw[:, 2:T + 2], op=ALU.add), (tF[TMP0],), (tF[TMP1],))
            vop(lambda e: e.tensor_scalar(out=fa(TMP1), in0=fa(TMP1), scalar1=dcol("hmu", ch), scalar2=None, op0=ALU.mult),
                (tF[TMP1], t_der), (tF[TMP1],))
            vop(lambda e: e.scalar_tensor_tensor(out=fa(dst), in0=raw[:, 1:T + 1], scalar=dcol("omu", ch), in1=fa(TMP1),
                                                 op0=ALU.mult, op1=ALU.add), (tF[TMP0], tF[TMP1], t_der), (tF[dst],))

        def raw_load(ch, ri, u):
            U0 = u * T
            raw = F[ri]
            P.op("sync", lambda e: e.dma_start(out=raw[:, 1:T + 1], in_=zr_d[ch, :, U0:U0 + T]), (), (tF[ri],), dma=True)
            if u == 1:
                P.op("sync", lambda e: e.dma_start(out=raw[:, 0:1], in_=zr_d[ch, :, U0 - 1:U0], allow_slow_non_contiguous=True), (), (tF[ri],), dma=True)
            if u == 0:
                P.op("sync", lambda e: e.dma_start(out=raw[:, T + 1:T + 2], in_=zr_d[ch, :, U0 + T:U0 + T + 1], allow_slow_non_contiguous=True), (), (tF[ri],), dma=True)

        def shift_from(ch, ri, dst, u, eng):
            raw = F[ri]
            xop = lambda fn, r, w: P.op(eng, fn, r, w)
            if u == 1:
                xop(lambda e: e.tensor_scalar(out=raw[:, 0:1], in0=raw[:, 0:1], scalar1=pcol("flag"), scalar2=None, op0=ALU.mult),
                    (tF[ri], t_par), (tF[ri],))
            else:
                xop(lambda e: e.memset(raw[:, 0:1], 0.0), (), (tF[ri],))
            if u == 0:
                xop(lambda e: e.tensor_scalar(out=raw[:, T + 1:T + 2], in0=raw[:, T + 1:T + 2], scalar1=pcol("flag"), scalar2=None,
                                              op0=ALU.mult), (tF[ri], t_par), (tF[ri],))
            else:
                xop(lambda e: e.memset(raw[:, T + 1:T + 2], 0.0), (), (tF[ri],))
            xop(lambda e: e.tensor_tensor(out=fa(dst), in0=raw[:, 0:T], in1=raw[:, 2:T + 2], op=ALU.add), (tF[ri],), (tF[dst],))
            xop(lambda e: e.tensor_scalar(out=fa(dst), in0=fa(dst), scalar1=dcol("hmu", ch), scalar2=None, op0=ALU.mult),
                (tF[dst], t_der), (tF[dst],))
            if eng == "gpsimd":
                xop(lambda e: e.tensor_scalar(out=raw[:, 1:T + 1], in0=raw[:, 1:T + 1], scalar1=dcol("omu", ch), scalar2=None, op0=ALU.mult),
                    (tF[ri], t_der), (tF[ri],))
                xop(lambda e: e.tensor_tensor(out=fa(dst), in0=fa(dst), in1=raw[:, 1:T + 1], op=ALU.add), (tF[ri], tF[dst]), (tF[dst],))
            else:
                xop(lambda e: e.scalar_tensor_tensor(out=fa(dst), in0=raw[:, 1:T + 1], scalar=dcol("omu", ch), in1=fa(dst),
                                                     op0=ALU.mult, op1=ALU.add), (tF[ri], tF[dst], t_der), (tF[dst],))

        def raw_loads(hp, u):
            raw_load(hp, E, u)
            raw_load(8 + hp, A, u)
            raw_load(16 + hp, L, u)

        def lora_sig(d, prow, bname, hp, dst):
            for k in range(4):
                i = rot("pb")
                P.mm_group([(pb[i][:], lup[d][prow:prow + 64, hp * 128:(hp + 1) * 128], blkc(zs24b[prow:prow + 64, :], k))],
                           (tlw, tz24), (tpb[i],))
                aop(lambda e, i=i, k=k: e.activation(out=blkc(fa(dst), k), in_=pb[i][:], func=AF.Sigmoid, bias=pcol(bname, hp)),
                    (tpb[i], t_par), (tF[dst],))

        def kd_from_a(hp):
            vop(lambda e: e.tensor_scalar(out=fa(A), in0=fa(A), scalar1=pcol("kim", hp), scalar2=dcol("omk", hp), op0=ALU.mult,
                                          op1=ALU.add), (tF[A], t_par, t_der), (tF[A],))
            vop(lambda e: e.tensor_tensor(out=fa(A), in0=fa(A), in1=fa(K), op=ALU.mult), (tF[A], tF[K]), (tF[A],))

        for (u, d) in order:
            U0 = u * T
            fwd = d == "f"
            m4 = con[:, CC["m4f"]:CC["m4f"] + 512] if fwd else con[:, CC["m4b"]:CC["m4b"] + 512]
            ml = con[:, CC["mlf"]:CC["mlf"] + 128] if fwd else con[:, CC["mlb"]:CC["mlb"] + 128]
            load_shift(24, E, u)
            aop(lambda e: e.activation(out=zs24b[0:64, :], in_=F[E][0:64, 0:T], func=AF.Tanh), (tF[E],), (tz24,))
            aop(lambda e: e.copy(out=zs24b[64:128, :], in_=F[E][64:128, 0:T]), (tF[E],), (tz24,))
            load_shift(25, E, u)
            aop(lambda e: e.activation(out=glb[:], in_=fa(E), func=AF.Sigmoid), (tF[E],), (tglb,))
            for hp in range(1 if dbg.get("dump2") else 8):
                if hp == 0:
                    raw_loads(0, u)
                shift_from(hp, E, R, u, V)
                shift_from(16 + hp, L, VV, u, V)
                shift_from(8 + hp, A, K, u, V)
                aop(lambda e, hp=hp: e.activation(out=fa(TMP0), in_=fa(K), func=AF.Square, scale=pcol("kns", hp)),
                    (tF[K], t_par), (tF[TMP0],))
                for k in range(4):
                    i = rot("pb")
                    P.mm_group([(pb[i][:], bonesf, blkc(fa(TMP0), k))], (t_con, tF[TMP0]), (tpb[i],))
                    aop(lambda e, i=i, k=k: e.activation(out=blkc(fa(TMP1), k), in_=pb[i][:], func=AF.Ln, bias=epsk[:, 0:1]), (tpb[i], tsm0), (tF[TMP1],))
                aop(lambda e: e.activation(out=fa(TMP1), in_=fa(TMP1), func=AF.Exp, scale=-0.5), (tF[TMP1],), (tF[TMP1],))
                vop(lambda e, hp=hp: e.scalar_tensor_tensor(out=fa(KK), in0=fa(K), scalar=pcol("kns", hp), in1=fa(TMP1), op0=ALU.mult,
                                                            op1=ALU.mult), (tF[K], tF[TMP1], t_par), (tF[KK],))
                if not fwd:
                    lora_sig("f", 64, "ibf", hp, A)
                    kd_from_a(hp)
                    vop(lambda e: e.tensor_copy(out=fa(TMP0), in_=fa(A)), (tF[A],), (tF[TMP0],))
                lora_sig(d, 0, "dbf" if fwd else "dbb", hp, E)
                lora_sig(d, 64, "ibf" if fwd else "ibb", hp, A)
                vop(lambda e: e.tensor_tensor(out=fa(B), in0=fa(KK), in1=fa(A), op=ALU.mult), (tF[KK], tF[A]), (tF[B],))
                kd_from_a(hp)
                if not fwd:
                    vop(lambda e: e.tensor_tensor(out=fa(TMP0), in0=fa(TMP0), in1=fa(A), op=ALU.add), (tF[TMP0], tF[A]), (tF[TMP0],))
                    vop(lambda e, hp=hp: e.scalar_tensor_tensor(out=fa(TMP0), in0=fa(TMP0), scalar=dcol("hbs", hp), in1=fa(R),
                                                                op0=ALU.mult, op1=ALU.mult), (tF[TMP0], tF[R], t_der), (tF[TMP0],))
                for j in range(16):
                    vop(lambda e, j=j: e.tensor_tensor_scan(out=F[L][:, j * 128:(j + 1) * 128], data0=F[E][:, j * 128:(j + 1) * 128],
                                                            data1=onesf, initial=0.0, op0=ALU.add, op1=ALU.mult),
                        (tF[E], t_con), (tF[L],))
                vop(lambda e: e.tensor_tensor(out=fa(E), in0=fa(L), in1=fa(E), op=ALU.subtract), (tF[L], tF[E]), (tF[E],))
                L3 = fa(L).rearrange("p (j t) -> p j t", t=128)
                vop(lambda e: e.tensor_copy(out=mids[:], in_=L3[:, :, 63]), (tF[L],), (tsm,))
                vop(lambda e: e.tensor_copy(out=tots[:], in_=L3[:, :, 127]), (tF[L],), (tsm,))
                vop(lambda e: e.tensor_scalar(out=biasm[:], in0=mids[:], scalar1=C0, scalar2=None, op0=ALU.mult), (tsm,), (tsm,))
                vop(lambda e: e.tensor_tensor(out=sct[:], in0=tots[:], in1=mids[:], op=ALU.subtract), (tsm,), (tsm,))
                if fwd:
                    vop(lambda e: e.tensor_copy(out=scj[:], in_=sct[:]), (tsm,), (tsm,))
                    vop(lambda e: e.tensor_tensor(out=scj[:, 0:15], in0=sct[:, 0:15], in1=mids[:, 1:16], op=ALU.add), (tsm,), (tsm,))
                    aop(lambda e: e.activation(out=sci[:], in_=mids[:, 0:1], func=AF.Exp, scale=-C0), (tsm,), (tsm,))
                else:
                    vop(lambda e: e.tensor_copy(out=scj[:], in_=mids[:]), (tsm,), (tsm,))
                    vop(lambda e: e.tensor_tensor(out=scj[:, 1:16], in0=mids[:, 1:16], in1=sct[:, 0:15], op=ALU.add), (tsm,), (tsm,))
                    aop(lambda e: e.activation(out=sci[:], in_=sct[:, 15:16], func=AF.Exp, scale=-C0), (tsm,), (tsm,))
                aop(lambda e: e.activation(out=scj[:], in_=scj[:], func=AF.Exp, scale=-C0), (tsm,), (tsm,))
                vop(lambda e: e.tensor_scalar(out=nbiasm[:], in0=mids[:], scalar1=-C0, scalar2=None, op0=ALU.mult), (tsm,), (tsm,))
                ARa = AR[:, :, 0:128]
                ARr = AR[:, :, 128:256]
                v3 = lambda i: fa(i).rearrange("p (j t) -> p j t", t=128)

                def exps(dst, src, sign):
                    bb = biasm if sign < 0 else nbiasm
                    for j in range(16):
                        aop(lambda e, j=j: e.activation(out=F[dst][:, j * 128:(j + 1) * 128], in_=F[src][:, j * 128:(j + 1) * 128], func=AF.Exp,
                                                        scale=sign * C0, bias=bb[:, j:j + 1]), (tF[src], tsm), (tF[dst],))
                if fwd:
                    exps(YA, L, -1.0)
                    vop(lambda e: e.tensor_tensor(out=ARr, in0=v3(R), in1=v3(YA), op=ALU.mult), (tF[R], tF[YA]), (tAR,))
                    exps(E, E, -1.0)
                    vop(lambda e: e.scalar_tensor_tensor(out=ARa, in0=v3(KK), scalar=-1.0, in1=v3(E), op0=ALU.mult, op1=ALU.mult),
                        (tF[KK], tF[E]), (tAR,))
                    exps(L, L, 1.0)
                    vop(lambda e: e.tensor_tensor(out=Bf[:], in0=fa(B), in1=fa(L), op=ALU.mult), (tF[B], tF[L]), (tBf,))
                    vop(lambda e: e.tensor_tensor(out=Kf[:], in0=fa(A), in1=fa(L), op=ALU.mult), (tF[A], tF[L]), (tKf,))
                else:
                    exps(YA, E, -1.0)
                    vop(lambda e: e.tensor_tensor(out=Bf[:], in0=fa(B), in1=fa(YA), op=ALU.mult), (tF[B], tF[YA]), (tBf,))
                    vop(lambda e: e.tensor_tensor(out=Kf[:], in0=fa(A), in1=fa(YA), op=ALU.mult), (tF[A], tF[YA]), (tKf,))
                    exps(E, E, 1.0)
                    vop(lambda e: e.tensor_tensor(out=ARr, in0=v3(R), in1=v3(E), op=ALU.mult), (tF[R], tF[E]), (tAR,))
                    exps(L, L, 1.0)
                    vop(lambda e: e.scalar_tensor_tensor(out=ARa, in0=v3(KK), scalar=-1.0, in1=v3(L), op0=ALU.mult, op1=ALU.mult),
                        (tF[KK], tF[L]), (tAR,))
                aop(lambda e: e.copy(out=vb[:], in_=fa(VV)), (tF[VV],), (tvb,))
                if hp < 7 and not dbg.get("dump2"):
                    raw_loads(hp + 1, u)
                for (src, tsrc, dst, tdst) in ((Bf, tBf, Btm, tBtm), (Kf, tKf, Ktm, tKtm), (vb, tvb, Vtm, tVtm)):
                    for half in range(2):
                        for jj in range(8):
                            j = half * 8 + jj
                            P.op("tensor", lambda e, src=src, j=j, jj=jj: e.transpose(pT[:, jj * 128:(jj + 1) * 128],
                                                                                      src[:, j * 128:(j + 1) * 128], identb[:]),
                                 (tsrc, t_identb), (tpT,), sig=(jj == 7))
                        aop(lambda e, dst=dst, half=half: e.copy(out=dst[:, half * 8:(half + 1) * 8, :],
                                                                 in_=pT[:, :].rearrange("p (j c) -> p j c", c=128)),
                            (tpT,), (tdst,))
                for hd in range(2):
                    hb = 64 * hd
                    for j in range(16):
                        pi = hd * 16 + j
                        cs = slice(j * 128, (j + 1) * 128)
                        i = rot("pA")
                        P.op("tensor", lambda e, i=i, hb=hb, cs=cs, j=j: e.matmul(pA[i][:, 0:256], Bf[hb:hb + 64, cs], AR[hb:hb + 64, j, :],
                                                                                 start=True, stop=True), (tBf, tAR), (tpA[i],), sig=False)
                        P.op("tensor", lambda e, i=i, hb=hb, cs=cs, j=j: e.matmul(pA[i][:, 256:512], Kf[hb:hb + 64, cs], AR[hb:hb + 64, j, :],
                                                                                 start=True, stop=True), (tKf, tAR), (tpA[i],))
                        vop(lambda e, i=i, pi=pi, m4=m4: e.tensor_tensor(out=Am[:, pi, :], in0=pA[i][:], in1=m4, op=ALU.mult),
                            (tpA[i], t_con), (tAm[pi],))
                SQB = [(pb[0], tpb[0]), (pb[1], tpb[1]), (pA[0], tpA[0]), (pA[1], tpA[1])]
                TTB = [(pX[0], tpX[0]), (pX[1], tpX[1])]
                for g in range(4):
                    prs = [g * 8 + q for q in range(8)]
                    for bk in range(4):
                        bank, tbank = SQB[bk]
                        mms = []
                        for q in (2 * bk, 2 * bk + 1):
                            pi = prs[q]
                            hd, j = pi // 16, pi % 16
                            hb = 64 * hd
                            c0 = (q % 2) * 256
                            P.op("tensor", lambda e, bank=bank, c0=c0, hb=hb, j=j: e.matmul(
                                bank[:, c0:c0 + 128], AR[hb:hb + 64, j, 0:128], Bf[hb:hb + 64, j * 128:(j + 1) * 128], start=True, stop=True),
                                 (tAR, tBf), (tbank,), sig=(q % 2 == 1))
                        for q in (2 * bk, 2 * bk + 1):
                            pi = prs[q]
                            c0 = (q % 2) * 256
                            vop(lambda e, q=q, ml=ml, bank=bank, c0=c0: e.tensor_tensor(out=Xb[:, q, 0, 0:128], in0=bank[:, c0:c0 + 128], in1=ml,
                                                                                        op=ALU.mult), (tbank, t_con), (tXb[q][0],))
                            aop(lambda e, q=q, pi=pi: e.copy(out=Xb[:, q, 0, 128:256], in_=Am[:, pi, 0:128]), (tAm[pi],), (tXb[q][0],))
                            vop(lambda e, q=q, pi=pi: e.tensor_tensor(out=TTb[:, q, 0, :], in0=Am[:, pi, 0:128], in1=identb[:], op=ALU.add),
                                (tAm[pi], t_identb), (tTTb[q][0],))
                    for lv in range(6):
                        cur, nxt = lv % 2, (lv + 1) % 2
                        last = lv == 5
                        n = 128 if last else 256
                        for bk in range(4):
                            bank, tbank = SQB[bk]
                            qs = (2 * bk, 2 * bk + 1)
                            nmm = 0
                            for q in qs:
                                c0 = (q % 2) * 256
                                X_ = Xb[:, q, cur, 0:128]
                                XT_ = Xb[:, q, cur, 128:256]
                                fin_ = (q % 2 == 1)
                                P.op("tensor", lambda e, bank=bank, c0=c0, X_=X_, XT_=XT_: e.matmul(bank[:, c0:c0 + 128], XT_, X_, start=True, stop=True),
                                     (tXb[q][cur],), (tbank,), sig=(last and fin_))
                                if not last:
                                    P.op("tensor", lambda e, bank=bank, c0=c0, X_=X_, XT_=XT_: e.matmul(bank[:, c0 + 128:c0 + 256], X_, XT_, start=True,
                                                                                                       stop=True), (tXb[q][cur],), (tbank,), sig=fin_)
                            q0 = qs[0]
                            aop(lambda e, bank=bank, q0=q0, nxt=nxt, n=n: e.copy(
                                out=Xb[:, q0:q0 + 2, nxt, 0:n], in_=bank[:, :].rearrange("p (a c) -> p a c", a=2)[:, :, 0:n]),
                                (tbank,), (tXb[qs[0]][nxt], tXb[qs[1]][nxt]))
                        for tb in range(2):
                            bank, tbank = TTB[tb]
                            qs = list(range(4 * tb, 4 * tb + 4))
                            for q in qs:
                                P.op("tensor", lambda e, bank=bank, q=q, nxt=nxt, cur=cur: e.matmul(
                                    bank[:, (q % 4) * 128:(q % 4 + 1) * 128], Xb[:, q, nxt, 0:128], TTb[:, q, cur, :], start=True, stop=True),
                                     (tTTb[q][cur], tXb[q][nxt]), (tbank,), sig=(q % 4 == 3))
                            q0 = qs[0]
                            bview = bank[:, :].rearrange("p (a c) -> p a c", a=4)
                            if last:
                                pi0 = prs[q0]
                                vop(lambda e, bview=bview, q0=q0, pi0=pi0, cur=cur: e.tensor_tensor(
                                    out=TTs[:, pi0:pi0 + 4, :], in0=bview, in1=TTb[:, q0:q0 + 4, cur, :], op=ALU.add),
                                    [tbank] + [tTTb[q][cur] for q in qs], [tTTs[prs[q]] for q in qs])
                            else:
                                vop(lambda e, bview=bview, q0=q0, nxt=nxt, cur=cur: e.tensor_tensor(
                                    out=TTb[:, q0:q0 + 4, nxt, :], in0=bview, in1=TTb[:, q0:q0 + 4, cur, :], op=ALU.add),
                                    [tbank] + [tTTb[q][cur] for q in qs], [tTTb[q][nxt] for q in qs])
                di = 0 if fwd else 1
                hcar = Hc[:, di, hp, :]
                linked_in = (fwd and u == 1) or ((not fwd) and u == 0)
                if linked_in:
                    vop(lambda e, hcar=hcar: e.tensor_scalar(out=Hf[:], in0=hcar, scalar1=sci[:, 0:1], scalar2=pcol("flag"), op0=ALU.mult,
                                                             op1=ALU.mult), (tHc, tsm, t_par), (tH,))
                else:
                    vop(lambda e: e.memset(Hf[:], 0.0), (), (tH,))
                aop(lambda e: e.copy(out=Hb[:], in_=Hf[:]), (tH,), (tHb,))
                jorder = list(range(16)) if fwd else list(range(15, -1, -1))
                for j in jorder:
                    for hd in range(2):
                        hb = 64 * hd
                        pi = hd * 16 + j
                        P.mm_group([(pS[:, hd * 64:(hd + 1) * 64], AR[hb:hb + 64, j, 0:128], Hb[hb:hb + 64, :]),
                                    (pS[:, hd * 64:(hd + 1) * 64], Am[:, pi, 256:384], Vtm[:, j, hb:hb + 64])],
                                   (tAR, tHb, tAm[pi], tVtm), (tpS["S"],))
                    aop(lambda e: e.copy(out=Xs[:], in_=pS[:, 0:128]), (tpS["S"],), (tXs,))
                    for hd in range(2):
                        pi = hd * 16 + j
                        P.mm_group([(pS[:, 128 + hd * 64:128 + (hd + 1) * 64], TTs[:, pi, :], Xs[:, hd * 64:(hd + 1) * 64])],
                                   (tTTs[pi], tXs), (tpS["U"],))
                    aop(lambda e: e.copy(out=Ub[:], in_=pS[:, 128:256]), (tpS["U"],), (tUb,))
                    for hd in range(2):
                        hb = 64 * hd
                        pi = hd * 16 + j
                        P.mm_group([(pS[hb:hb + 64, 320:448], Hb[hb:hb + 64, :], AR[hb:hb + 64, j, 128:256]),
                                    (pS[hb:hb + 64, 320:448], Ub[:, hd * 64:(hd + 1) * 64], Am[:, pi, 128:256]),
                                    (pS[hb:hb + 64, 320:448], Vtm[:, j, hb:hb + 64], Am[:, pi, 384:512])],
                                   (tHb, tAR, tUb, tAm[pi], tVtm), (tpS["Y"],))
                        P.mm_group([(pS[hb:hb + 64, 256:320], Btm[:, j, hb:hb + 64], Ub[:, hd * 64:(hd + 1) * 64]),
                                    (pS[hb:hb + 64, 256:320], Ktm[:, j, hb:hb + 64], Vtm[:, j, hb:hb + 64])],
                                   (tBtm, tUb, tKtm, tVtm), (tpS["H"],))
                    aop(lambda e, j=j: e.copy(out=F[YA][:, j * 128:(j + 1) * 128], in_=pS[:, 320:448]), (tpS["Y"],), (tF[YA],))
                    vop(lambda e: e.tensor_tensor(out=Ht[:], in0=pS[:, 256:320], in1=Hf[:], op=ALU.add), (tpS["H"], tH), (tHt,))
                    vop(lambda e, j=j: e.tensor_scalar(out=Hf[:], in0=Ht[:], scalar1=scj[:, j:j + 1], scalar2=None, op0=ALU.mult),
                        (tHt, tsm), (tH,))
                    aop(lambda e: e.copy(out=Hb[:], in_=Hf[:]), (tH,), (tHb,))
                vop(lambda e, hcar=hcar: e.tensor_copy(out=hcar, in_=Hf[:]), (tH,), (tHc,))
                if dbg.get("dump") and hp == 0 and (u, d) == order[0]:
                    def dump(name, ap, shape, dt, toks):
                        dd = nc.dram_tensor("dbg_" + name, shape, dt, kind="ExternalOutput").ap()
                        P.op("sync", lambda e: e.dma_start(out=dd, in_=ap), toks, (), dma=True)
                    dump("L", fa(L), [128, T], F32, (tF[L],))
                    dump("E", fa(E), [128, T], F32, (tF[E],))
                    dump("KK", fa(KK), [128, T], F32, (tF[KK],))
                    dump("R", fa(R), [128, T], F32, (tF[R],))
                    dump("Akd", fa(A), [128, T], F32, (tF[A],))
                    dump("AR", AR[:], [128, 16, 256], BF16, (tAR,))
                    dump("Bf", Bf[:], [128, T], BF16, (tBf,))
                    dump("Kf", Kf[:], [128, T], BF16, (tKf,))
                    dump("Btm", Btm[:], [128, 16, 128], BF16, (tBtm,))
                    dump("Vtm", Vtm[:], [128, 16, 128], BF16, (tVtm,))
                    dump("Am", Am[:], [128, 32, 512], BF16, tAm)
                    dump("TTs", TTs[:], [128, 32, 128], BF16, tTTs)
                    dump("YA", fa(YA), [128, T], F32, (tF[YA],))
                    dump("scj", scj[:], [128, 16], F32, (tsm,))
                    dump("mids", mids[:], [128, 16], F32, (tsm,))
                    dump("tots", tots[:], [128, 16], F32, (tsm,))
                if dbg.get("dump") and hp == 0 and not dbg.get("dump2"):
                    break
                if fwd:
                    P.op("sync", lambda e, hp=hp, U0=U0: e.dma_start(out=yf_d[hp, :, U0:U0 + T], in_=fa(YA)), (tF[YA],), (), dma=True)
                else:
                    P.op("sync", lambda e, hp=hp, U0=U0: e.dma_start(out=fa(TMP1), in_=yf_d[hp, :, U0:U0 + T]), (), (tF[TMP1],), dma=True)
                    vop(lambda e: e.tensor_tensor(out=fa(YA), in0=fa(YA), in1=fa(TMP1), op=ALU.add), (tF[YA], tF[TMP1]), (tF[YA],))
                    aop(lambda e: e.activation(out=fa(TMP1), in_=fa(YA), func=AF.Square), (tF[YA],), (tF[TMP1],))
                    for k in range(4):
                        i1 = rot("pb")
                        P.mm_group([(pb[i1][:], bonesf, blkc(fa(YA), k))], (t_con, tF[YA]), (tpb[i1],))
                        vop(lambda e, i1=i1: e.tensor_scalar(out=fin[0][:], in0=pb[i1][:], scalar1=1.0 / 64, scalar2=None, op0=ALU.mult),
                            (tpb[i1],), (tfin[0],))
                        i2 = rot("pb")
                        P.mm_group([(pb[i2][:], bonesf, blkc(fa(TMP1), k))], (t_con, tF[TMP1]), (tpb[i2],))
                        vop(lambda e: e.tensor_tensor(out=fin[1][:], in0=fin[0][:], in1=fin[0][:], op=ALU.mult), (tfin[0],), (tfin[1],))
                        vop(lambda e, i2=i2: e.scalar_tensor_tensor(out=fin[1][:], in0=pb[i2][:], scalar=1.0 / 64, in1=fin[1][:],
                                                                    op0=ALU.mult, op1=ALU.subtract), (tpb[i2], tfin[1]), (tfin[1],))
                        aop(lambda e: e.activation(out=fin[1][:], in_=fin[1][:], func=AF.Ln, bias=epsk[:, 1:2]), (tfin[1], tsm0), (tfin[1],))
                        aop(lambda e: e.activation(out=fin[1][:], in_=fin[1][:], func=AF.Exp, scale=-0.5), (tfin[1],), (tfin[1],))
                        vop(lambda e, k=k: e.tensor_tensor(out=fin[2][:], in0=blkc(fa(YA), k), in1=fin[0][:], op=ALU.subtract),
                            (tF[YA], tfin[0]), (tfin[2],))
                        vop(lambda e: e.tensor_tensor(out=fin[2][:], in0=fin[2][:], in1=fin[1][:], op=ALU.mult), (tfin[2], tfin[1]), (tfin[2],))
                        vop(lambda e, hp=hp: e.tensor_scalar(out=fin[2][:], in0=fin[2][:], scalar1=pcol("gnw", hp), scalar2=pcol("gnb", hp),
                                                             op0=ALU.mult, op1=ALU.add), (tfin[2], t_par), (tfin[2],))
                        i3 = rot("pb")
                        P.mm_group([(pb[i3][:], bonesf, blkc(fa(TMP0), k))], (t_con, tF[TMP0]), (tpb[i3],))
                        vop(lambda e, i3=i3, k=k: e.tensor_tensor(out=fin[3][:], in0=pb[i3][:], in1=blkc(fa(VV), k), op=ALU.mult),
                            (tpb[i3], tF[VV]), (tfin[3],))
                        vop(lambda e: e.tensor_tensor(out=fin[2][:], in0=fin[2][:], in1=fin[3][:], op=ALU.add), (tfin[2], tfin[3]), (tfin[2],))
                        i4 = rot("pb")
                        P.mm_group([(pb[i4][:], gup[:, hp * 128:(hp + 1) * 128], blkc(glb, k))], (tlw, tglb), (tpb[i4],))
                        vop(lambda e, i4=i4: e.tensor_tensor(out=finb[:], in0=fin[2][:], in1=pb[i4][:], op=ALU.mult), (tfin[2], tpb[i4]), (tfinb,))
                        P.op("sync", lambda e, hp=hp, U0=U0, k=k: e.dma_start(out=oT_d[hp, :, U0 + k * 512:U0 + (k + 1) * 512], in_=finb[:]),
                             (tfinb,), (), dma=True)
                    if dbg.get("dump2"):
                        def dump2(name, ap, shape, dt, toks):
                            dd = nc.dram_tensor("dbg2_" + name, shape, dt, kind="ExternalOutput").ap()
                            P.op("sync", lambda e: e.dma_start(out=dd, in_=ap), toks, (), dma=True)
                        dump2("YA", fa(YA), [128, T], F32, (tF[YA],))
                        dump2("TMP0", fa(TMP0), [128, T], F32, (tF[TMP0],))
                        dump2("TMP1", fa(TMP1), [128, T], F32, (tF[TMP1],))
                        dump2("VV", fa(VV), [128, T], F32, (tF[VV],))
                        dump2("glb", glb[:], [128, T], BF16, (tglb,))
                        for q in range(4):
                            dump2("fin%d" % q, fin[q][:], [128, 512], F32, (tfin[q],))


def _cols(v, n):
    return np.ascontiguousarray(v.reshape(n, 128).T.astype(np.float32))


def make_params(inp, flag):
    p = np.zeros((128, NPCOL), np.float32)

    def put(name, arr, n):
        p[:, PC[name]:PC[name] + n] = _cols(np.asarray(arr).reshape(-1), n)

    put("ffn1_pre", inp["ffn1_pre_g"], 16); put("ffn1_post", inp["ffn1_post_g"], 16)
    put("mix_pre", inp["mix_pre_g"], 16); put("mix_post", inp["mix_post_g"], 16)
    put("ffn2_pre", inp["ffn2_pre_g"], 16); put("ffn2_post", inp["ffn2_post_g"], 16)
    put("mu", inp["rwkv_shift_mix"], 26)
    put("dbf", inp["decay_bias_fwd"], 8); put("dbb", inp["decay_bias_bwd"], 8)
    put("ibf", inp["iclr_bias_fwd"], 8); put("ibb", inp["iclr_bias_bwd"], 8)
    put("kns", inp["key_norm_scale"], 8); put("kim", inp["key_iclr_mix"], 8)
    put("bsc", inp["bonus_scale"], 8); put("gnw", inp["gn_w"], 8); put("gnb", inp["gn_b"], 8)
    p[:, PC["flag"]] = flag
    return p


def make_consts():
    c = np.zeros((128, NCCOL), np.float32)
    c[:, 0:128] = np.eye(128)
    c[:, 128:256] = 1.0
    c[0:64, 256:320] = 1.0
    c[64:128, 320:384] = 1.0
    s = np.arange(128)[:, None]
    t = np.arange(128)[None, :]
    strict_f = (t > s).astype(np.float32)
    incl_f = (t >= s).astype(np.float32)
    strict_b = (t < s).astype(np.float32)
    incl_b = (t <= s).astype(np.float32)
    c[:, CC["m4f"]:CC["m4f"] + 512] = np.concatenate([strict_f, incl_f, strict_f, incl_f], 1)
    c[:, CC["m4b"]:CC["m4b"] + 512] = np.concatenate([strict_b, incl_b, strict_b, incl_b], 1)
    c[:, CC["mlf"]:CC["mlf"] + 128] = strict_f.T
    c[:, CC["mlb"]:CC["mlb"] + 128] = strict_b.T
    return c


def make_nat_tables(rpb, linked):
    rpb = np.asarray(rpb, np.float32)
    kc = np.arange(64)[:, None]
    qc = np.arange(64)[None, :]
    cs = np.clip(qc - 8, 0, 48)
    colok = (kc >= cs) & (kc < cs + 16)
    cidx = np.clip(kc - qc + 15, 0, 30)
    G = np.full((16, 2, 64, 2, 7, 64), NEG, np.float32)
    for par in range(2):
        for m in range(14):
            dr = m - 7 + par
            if dr < -7 or dr > 7:
                continue
            val = np.where(colok[None], rpb[:, dr + 7][:, cidx], NEG)
            G[:, par, :, m % 2, m // 2, :] = val
    G = G.reshape(8, 2, 128, 2, 7, 64).transpose(0, 2, 1, 3, 4, 5)
    S = np.full((7, 16, 2, 64, 6, 64), NEG, np.float32)
    for spi, (u, i) in enumerate(SPECIAL):
        er0 = 24 if u == 0 else -4
        for c in range(6):
            for par in range(2):
                er = er0 + 2 * c + par
                if linked:
                    gi = u * 32 + i
                    ws = min(max(gi - 4, 0), 56)
                    gr = u * 32 + er
                    ok = ws <= gr < ws + 8
                    dr = gr - gi
                else:
                    ws = min(max(i - 4, 0), 24)
                    ok = (ws <= er < ws + 8) and (0 <= er < 32)
                    dr = er - i
                if not ok:
                    continue
                S[spi, :, par, :, c, :] = np.where(colok[None], rpb[:, dr + 7][:, cidx], NEG)
    S = S.reshape(7, 16, 128, 6, 64)
    return np.ascontiguousarray(G), np.ascontiguousarray(S)


def core_units(xp, xs, c):
    if c < 4:
        return np.concatenate([xs[c], xp[c]], 0)
    b = 4 + 3 * (c - 4)
    return np.concatenate([xp[b], xp[b + 1], xp[b + 2]], 0)


WNAMES = ("ffn1_w_gate", "ffn1_w_up", "ffn1_w_down", "ffn2_w_gate", "ffn2_w_up", "ffn2_w_down", "w_in", "w_out")


def make_inputs_core(inp, x, linked, shared=None):
    m = {"x": np.ascontiguousarray(x, dtype=np.float32), "params": make_params(inp, 1.0 if linked else 0.0)}
    if shared is None:
        shared = make_shared(inp)
    m.update(shared["common"])
    G, S = shared["nat"][bool(linked)]
    m["natG"] = G
    m["natS"] = S
    return m


def make_shared(inp):
    common = {"consts": make_consts()}
    for k in WNAMES:
        common[k] = np.ascontiguousarray(np.asarray(inp[k])[0], dtype=np.float32)
    g = lambda k: np.asarray(inp[k])[0].astype(np.float32)
    common["lupf"] = np.ascontiguousarray(np.concatenate([g("decay_up_fwd"), g("iclr_up_fwd")], 0))
    common["lupb"] = np.ascontiguousarray(np.concatenate([g("decay_up_bwd"), g("iclr_up_bwd")], 0))
    common["gup"] = np.ascontiguousarray(g("gate_up"))
    nat = {True: make_nat_tables(np.asarray(inp["nat_rpb"])[0], True),
           False: make_nat_tables(np.asarray(inp["nat_rpb"])[0], False)}
    return {"common": common, "nat": nat}


_CACHE = {}


def kernel(**inp):
    inp = {k: np.asarray(v) for k, v in inp.items()}
    xp, xs = inp["x_prompt"], inp["x_sample"]
    if "nc" not in _CACHE:
        _CACHE["nc"] = build_program()
    nc = _CACHE["nc"]
    shared = make_shared(inp)
    in_maps = []
    for c in range(8):
        in_maps.append(make_inputs_core(inp, core_units(xp, xs, c), c < 4, shared))
    res = run_bass_kernel_spmd(nc, in_maps, core_ids=list(range(8)))
    yp = np.empty(xp.shape, np.float32)
    ys = np.empty(xs.shape, np.float32)
    for c in range(8):
        y = np.asarray(res.results[c]["y"], dtype=np.float32)
        if c < 4:
            ys[c] = y[:2 * UNIT]
            yp[c] = y[2 * UNIT:]
        else:
            b = 4 + 3 * (c - 4)
            yp[b] = y[:UNIT]
            yp[b + 1] = y[UNIT:2 * UNIT]
            yp[b + 2] = y[2 * UNIT:]
    return (yp, ys)
```

```python
import numpy as np
from contextlib import ExitStack
import concourse.bass as bass
import concourse.mybir as mybir
from concourse.bass_utils import run_bass_kernel_spmd

F32 = mybir.dt.float32
BF16 = mybir.dt.bfloat16
AF = mybir.ActivationFunctionType
ALU = mybir.AluOpType

D = 2048
DFF = 5632
NDC = 16
NFC = 44
TT = 512
UNIT = 2048
NUNIT = 3
NTOK = UNIT * NUNIT
RW_CH = 26
C0 = float(np.exp(-0.5))
NDS = 8


class Tk:
    __slots__ = ("w", "r")

    def __init__(self):
        self.w = None
        self.r = {}


class Prog:
    ENGS = ("sync", "gpsimd", "scalar", "vector", "tensor")

    def __init__(self, nc, es):
        self.nc = nc
        self.q = {k: [] for k in self.ENGS}
        self.csem = {}
        for k in ("scalar", "vector", "tensor", "gpsimd"):
            self.csem[k] = es.enter_context(nc.semaphore("c_" + k))
        self.ccnt = {k: 0 for k in self.csem}
        self.dpool = {}
        self.dnext = {}
        for k in ("sync", "gpsimd"):
            self.dpool[k] = [[es.enter_context(nc.semaphore("d_%s%d" % (k, i))), 0] for i in range(NDS)]
            self.dnext[k] = 0
        self.waited = {}

    def op(self, eng, fn, reads=(), writes=(), dma=False, sig=True, nowait=False, reg=True):
        deps = {}

        def need(tok):
            if tok is None:
                return
            s, v = tok
            if deps.get(s, 0) < v:
                deps[s] = v

        if not nowait:
            for t in reads:
                need(t.w)
            for t in writes:
                need(t.w)
                for s, v in t.r.items():
                    need((s, v))
        tok = None
        inc = 0
        if dma:
            slot = self.dpool[eng][self.dnext[eng]]
            self.dnext[eng] = (self.dnext[eng] + 1) % NDS
            if slot[1] > 0:
                need((slot[0], slot[1]))
            slot[1] += 16
            tok = (slot[0], slot[1])
            inc = 16
        elif sig:
            self.ccnt[eng] += 1
            tok = (self.csem[eng], self.ccnt[eng])
            inc = 1
        waits = []
        for s, v in deps.items():
            key = (eng, s)
            if self.waited.get(key, 0) >= v:
                continue
            if eng == "tensor" and s is self.csem["tensor"]:
                continue
            self.waited[key] = v
            waits.append((s, v))
        if tok is not None and reg:
            for t in reads:
                if t.r.get(tok[0], 0) < tok[1]:
                    t.r[tok[0]] = tok[1]
            for t in writes:
                t.w = tok
                t.r = {}
        self.q[eng].append((waits, fn, tok, inc))
        return tok

    def mm_group(self, mms, reads, writes):
        n = len(mms)
        for i, (o, l, r) in enumerate(mms):
            fn = (lambda e, o=o, l=l, r=r, st=(i == 0), sp=(i == n - 1): e.matmul(o, l, r, start=st, stop=sp))
            if n == 1:
                self.op("tensor", fn, reads, writes)
            elif i == 0:
                self.op("tensor", fn, reads, writes, sig=False)
            elif i == n - 1:
                self.op("tensor", fn, reads, writes, nowait=True)
            else:
                self.op("tensor", fn, (), (), sig=False, nowait=True)

    def barrier(self):
        toks = []
        for k, s in self.csem.items():
            if self.ccnt[k] > 0:
                toks.append((s, self.ccnt[k]))
        for k in self.dpool:
            for s, v in self.dpool[k]:
                if v > 0:
                    toks.append((s, v))
        for eng in self.ENGS:
            waits = []
            for s, v in toks:
                if self.waited.get((eng, s), 0) >= v:
                    continue
                self.waited[(eng, s)] = v
                waits.append((s, v))
            self.q[eng].append((waits, None, None, 0))

    def finish(self):
        waits = []
        for k in self.dpool:
            for s, v in self.dpool[k]:
                if v > 0:
                    waits.append((s, v))
        self.q["sync"].append((waits, None, None, 0))

    def emit(self, eng, e):
        for waits, fn, tok, inc in self.q[eng]:
            for s, v in waits:
                e.wait_ge(s, v)
            if fn is None:
                continue
            inst = fn(e)
            if tok is not None:
                inst.then_inc(tok[0], inc)


PC = {}
_o = 0
for _n, _w in (("ffn1_pre", 16), ("ffn1_post", 16), ("mix_pre", 16), ("mix_post", 16), ("ffn2_pre", 16),
               ("ffn2_post", 16), ("mu", 26), ("dbf", 8), ("dbb", 8), ("ibf", 8), ("ibb", 8), ("kns", 8),
               ("kim", 8), ("bsc", 8), ("gnw", 8), ("gnb", 8), ("flag", 1)):
    PC[_n] = _o
    _o += _w
NPCOL = _o
DC = {}
_o = 0
for _n, _w in (("ffn1_posth", 16), ("ffn2_posth", 16), ("hmu", 26), ("omu", 26), ("omk", 8), ("hbs", 8)):
    DC[_n] = _o
    _o += _w
NDCOL = _o

CC = {"ident": 0, "ones": 128, "bones": 256, "m4f": 384, "m4b": 896, "mlf": 1408, "mlb": 1536}
NCCOL = 1664


def build_program(dbg=None):
    dbg = dbg or {}
    ntiles = dbg.get("ntiles", NTOK // TT)
    do_a = dbg.get("A", True)
    do_b = dbg.get("B", True)
    do_c = dbg.get("C", True)
    nc = bass.Bass("TRN2", target_bir_lowering=False)
    ext = lambda n, s, dt=F32: nc.dram_tensor(n, s, dt, kind="ExternalInput").ap()
    x_d = ext("x", [NTOK, D])
    par_d = ext("params", [128, NPCOL])
    con_d = ext("consts", [128, NCCOL])
    w1g = ext("ffn1_w_gate", [D, DFF]); w1u = ext("ffn1_w_up", [D, DFF]); w1d = ext("ffn1_w_down", [DFF, D])
    w2g = ext("ffn2_w_gate", [D, DFF]); w2u = ext("ffn2_w_up", [D, DFF]); w2d = ext("ffn2_w_down", [DFF, D])
    win = ext("w_in", [D, 6400]); wout = ext("w_out", [D, D])
    lupf_d = ext("lupf", [128, 1024]); lupb_d = ext("lupb", [128, 1024]); gup_d = ext("gup", [128, 1024])
    yf_d = nc.dram_tensor("yf", [8, 128, NTOK], F32, kind="ExternalOutput").ap()
    G_d = ext("natG", [8, 128, 2, 2, 7, 64])
    S_d = ext("natS", [7, 16, 128, 6, 64])
    y_d = nc.dram_tensor("y", [NTOK, D], F32, kind="ExternalOutput").ap()
    skind = "ExternalOutput"
    x1T_d = nc.dram_tensor("x1T", [NDC, 128, NTOK], F32, kind=skind).ap()
    zr_d = nc.dram_tensor("zr", [RW_CH, 128, NTOK], F32, kind=skind).ap()
    zqk_d = nc.dram_tensor("zqk", [16, 128, NTOK], BF16, kind=skind).ap()
    zv_d = nc.dram_tensor("zv", [NTOK, 1024], BF16, kind=skind).ap()
    if dbg.get("oT_in"):
        oT_d = ext("oT", [16, 128, NTOK], BF16)
    else:
        oT_d = nc.dram_tensor("oT", [16, 128, NTOK], BF16, kind=skind).ap()

    es = ExitStack()
    P = Prog(nc, es)
    sb = lambda n, s, dt=F32: es.enter_context(nc.sbuf_tensor(n, s, dt))
    par = sb("par", [128, NPCOL]); der = sb("der", [128, NDCOL]); con = sb("con", [128, NCCOL])
    identb = sb("identb", [128, 128], BF16)
    onesb128 = sb("onesb128", [128, 128], BF16)
    epsr = sb("epsr", [128, 1]); t_epsr = Tk()
    P.op("vector", lambda e: e.memset(epsr[:], 1e-6), (), (t_epsr,))
    t_par, t_der, t_con, t_identb = Tk(), Tk(), Tk(), Tk()
    P.op("sync", lambda e: e.dma_start(out=par[:], in_=par_d[:, :]), (), (t_par,), dma=True)
    P.op("sync", lambda e: e.dma_start(out=con[:], in_=con_d[:, :]), (), (t_con,), dma=True)

    def pcol(name, i=0, n=1):
        return par[:, PC[name] + i:PC[name] + i + n]

    def dcol(name, i=0, n=1):
        return der[:, DC[name] + i:DC[name] + i + n]

    def dslice(name, n):
        return der[:, DC[name]:DC[name] + n]

    def pslice(name, n):
        return par[:, PC[name]:PC[name] + n]

    V = "vector"
    P.op(V, lambda e: e.tensor_scalar(out=dslice("ffn1_posth", 16), in0=pslice("ffn1_post", 16), scalar1=0.5,
                                      scalar2=None, op0=ALU.mult), (t_par,), (t_der,))
    P.op(V, lambda e: e.tensor_scalar(out=dslice("ffn2_posth", 16), in0=pslice("ffn2_post", 16), scalar1=0.5,
                                      scalar2=None, op0=ALU.mult), (t_par,), (t_der,))
    P.op(V, lambda e: e.tensor_scalar(out=dslice("hmu", 26), in0=pslice("mu", 26), scalar1=0.5,
                                      scalar2=None, op0=ALU.mult), (t_par,), (t_der,))
    P.op(V, lambda e: e.tensor_scalar(out=dslice("omu", 26), in0=pslice("mu", 26), scalar1=-1.0,
                                      scalar2=1.0, op0=ALU.mult, op1=ALU.add), (t_par,), (t_der,))
    P.op(V, lambda e: e.tensor_scalar(out=dslice("omk", 8), in0=pslice("kim", 8), scalar1=-1.0,
                                      scalar2=1.0, op0=ALU.mult, op1=ALU.add), (t_par,), (t_der,))
    P.op(V, lambda e: e.tensor_scalar(out=dslice("hbs", 8), in0=pslice("bsc", 8), scalar1=0.5,
                                      scalar2=None, op0=ALU.mult), (t_par,), (t_der,))
    P.op(V, lambda e: e.tensor_copy(out=identb[:], in_=con[:, 0:128]), (t_con,), (t_identb,))
    P.op(V, lambda e: e.tensor_copy(out=onesb128[:], in_=con[:, 128:256]), (t_con,), (t_identb,))
    identf = con[:, CC["ident"]:CC["ident"] + 128]
    onesf = con[:, CC["ones"]:CC["ones"] + 128]
    bonesf = con[:, CC["bones"]:CC["bones"] + 128]

    def dense_phases(do_a, do_c, tiles):
        sfx = "A" if do_a else "C"
        with ExitStack() as ds:
            sbd = lambda n, s, dt=F32: ds.enter_context(nc.sbuf_tensor(n + sfx, s, dt))
            psd = lambda n, s, dt=F32: ds.enter_context(nc.psum_tensor(n + sfx, s, dt))
            big = sbd("big", [128, 8192])
            xT = sbd("xT", [128, NDC, TT])
            hT = sbd("hT", [128, NDC, TT], BF16)
            aT = sbd("aT", [128, NFC, TT], BF16)
            wgu = [sbd("wgu%d" % i, [128, NDC, 256], BF16) for i in range(3)]
            wdb = [sbd("wdb%d" % i, [128, NFC, 128], BF16) for i in range(2)]
            sq = [sbd("sq%d" % i, [128, TT], BF16) for i in range(2)]
            sl = [sbd("sl%d" % i, [128, TT]) for i in range(2)]
            rstd = sbd("rstd", [128, TT]); rtmp = sbd("rtmp", [128, TT])
            zst = [sbd("zst%d" % i, [128, 2, TT]) for i in range(1)]
            zsb = [sbd("zsb%d" % i, [128, 2, TT], BF16) for i in range(1)]
            vst = [sbd("vst%d" % i, [128, 4, 256], BF16) for i in range(1)]
            psb = [psd("psb%d" % i, [128, TT]) for i in range(8)]
            t_big = [Tk() for _ in range(16)]
            t_xT = [Tk() for _ in range(NDC)]
            t_hT = [Tk() for _ in range(NDC)]
            t_aT = [Tk() for _ in range(NFC)]
            t_wgu = [Tk() for _ in range(3)]
            t_wdb = [Tk() for _ in range(2)]
            t_sq = [Tk(), Tk()]; t_sl = [Tk(), Tk()]
            t_rstd, t_rtmp = Tk(), Tk()
            t_zst = [Tk(), Tk()]; t_zsb = [Tk(), Tk()]; t_vst = [Tk(), Tk()]
            t_ps = [Tk() for _ in range(8)]
            cnt = {"wgu": 0, "wdb": 0, "sq": 0, "sl": 0, "zst": 0, "ps_g": 0, "ps_u": 0, "ps_d": 0, "vst": 0}

            def rot(name, n):
                i = cnt[name] % n
                cnt[name] += 1
                return i

            def fT(dc):
                return big[:, dc * TT:(dc + 1) * TT]

            def stat_begin():
                pass

            def sumsq_chunk(src_ap, src_tk, dc, from_psum_eng="scalar"):
                i = rot("sq", 2)
                P.op("scalar", lambda e: e.activation(out=sq[i][:], in_=src_ap, func=AF.Square), (src_tk,), (t_sq[i],))
                P.op("tensor", lambda e: e.matmul(psb[6][:], onesb128[:], sq[i][:], start=(dc == 0), stop=(dc == NDC - 1)),
                     (t_sq[i], t_identb), (t_ps[6],))

            def make_rstd():
                P.op("scalar", lambda e: e.activation(out=rtmp[:], in_=psb[6][:], func=AF.Ln, bias=epsr[:, 0:1],
                                                      scale=1.0 / D), (t_ps[6], t_epsr), (t_rtmp,))
                P.op("scalar", lambda e: e.activation(out=rstd[:], in_=rtmp[:], func=AF.Exp, scale=-0.5), (t_rtmp,), (t_rstd,))

            def make_h(gname):
                for dc in range(NDC):
                    P.op(V, lambda e, dc=dc: e.scalar_tensor_tensor(out=hT[:, dc, :], in0=xT[:, dc, :],
                                                                    scalar=pcol(gname, dc), in1=rstd[:],
                                                                    op0=ALU.mult, op1=ALU.mult),
                         (t_xT[dc], t_rstd, t_par), (t_hT[dc],))

            def load_w_cols(wd, c0, ncols):
                i = rot("wgu", 3)
                src = wd.rearrange("(c p) f -> p c f", p=128)[:, :, c0:c0 + ncols]
                P.op("gpsimd", lambda e: e.dma_start(out=wgu[i][:, :, 0:ncols], in_=src), (), (t_wgu[i],), dma=True)
                return i

            def residual_update(gcols_ap_fn, src_fT=True):
                for dc in range(NDC):
                    P.op(V, lambda e, dc=dc: e.scalar_tensor_tensor(out=fT(dc), in0=fT(dc), scalar=gcols_ap_fn(dc),
                                                                    in1=rstd[:], op0=ALU.mult, op1=ALU.mult),
                         (t_big[dc], t_rstd, t_par, t_der), (t_big[dc],))
                    P.op(V, lambda e, dc=dc: e.tensor_tensor(out=xT[:, dc, :], in0=xT[:, dc, :], in1=fT(dc), op=ALU.add),
                         (t_big[dc], t_xT[dc]), (t_xT[dc],))

            def ffn(pre_name, posth_name, wg, wu, wd):
                for dc in range(NDC):
                    sumsq_chunk(xT[:, dc, :], t_xT[dc], dc)
                make_rstd()
                make_h(pre_name)
                for fb in range(NFC // 2):
                    ig = load_w_cols(wg, fb * 256, 256)
                    iu = load_w_cols(wu, fb * 256, 256)
                    for j in range(2):
                        fc = fb * 2 + j
                        pg = rot("ps_g", 2)
                        pu = 2 + rot("ps_u", 2)
                        P.mm_group([(psb[pg][:], wgu[ig][:, dc, j * 128:(j + 1) * 128], hT[:, dc, :]) for dc in range(NDC)],
                                   [t_wgu[ig]] + t_hT, [t_ps[pg]])
                        P.mm_group([(psb[pu][:], wgu[iu][:, dc, j * 128:(j + 1) * 128], hT[:, dc, :]) for dc in range(NDC)],
                                   [t_wgu[iu]] + t_hT, [t_ps[pu]])
                        si = rot("sl", 2)
                        P.op("scalar", lambda e, pg=pg, si=si: e.activation(out=sl[si][:], in_=psb[pg][:], func=AF.Silu),
                             (t_ps[pg],), (t_sl[si],))
                        P.op(V, lambda e, pu=pu, si=si, fc=fc: e.tensor_tensor(out=aT[:, fc, :], in0=sl[si][:], in1=psb[pu][:],
                                                                               op=ALU.mult),
                             (t_sl[si], t_ps[pu]), (t_aT[fc],))
                for dc in range(NDC):
                    i = rot("wdb", 2)
                    src = wd.rearrange("(c p) d -> p c d", p=128)[:, :, dc * 128:(dc + 1) * 128]
                    P.op("gpsimd", lambda e, i=i, src=src: e.dma_start(out=wdb[i][:], in_=src), (), (t_wdb[i],), dma=True)
                    pd = 4 + rot("ps_d", 2)
                    P.mm_group([(psb[pd][:], wdb[i][:, fc, :], aT[:, fc, :]) for fc in range(NFC)],
                               [t_wdb[i]] + t_aT, [t_ps[pd]])
                    P.op(V, lambda e, dc=dc, pd=pd: e.tensor_copy(out=fT(dc), in_=psb[pd][:]), (t_ps[pd],), (t_big[dc],))
                    sumsq_chunk(fT(dc), t_big[dc], dc)
                make_rstd()
                residual_update(lambda dc: dcol(posth_name, dc))

            for tt in tiles:
                t0 = tt * TT
                if do_a:
                    def x_load(tq):
                        src = x_d[tq * TT:(tq + 1) * TT, :].rearrange("(b p) d -> p b d", p=128)
                        P.op("sync", lambda e, src=src: e.dma_start(out=big[:, :].rearrange("p (b d) -> p b d", b=4), in_=src),
                             (), t_big, dma=True)

                    def x_transpose():
                        for dc in range(NDC):
                            for tb in range(4):
                                P.op("tensor", lambda e, dc=dc, tb=tb: e.transpose(
                                    psb[7][:, tb * 128:(tb + 1) * 128],
                                    big[:, tb * 2048 + dc * 128: tb * 2048 + (dc + 1) * 128], identf),
                                     t_big + [t_con], (t_ps[7],), sig=(tb == 3), nowait=(tb != 0))
                            P.op("scalar", lambda e, dc=dc: e.copy(out=xT[:, dc, :], in_=psb[7][:]), (t_ps[7],), (t_xT[dc],))
                    if tt == tiles[0]:
                        x_load(tt)
                        x_transpose()
                    ffn("ffn1_pre", "ffn1_posth", w1g, w1u, w1d)
                    dst = x1T_d.rearrange("c p t -> p c t")[:, :, t0:t0 + TT]
                    P.op("sync", lambda e, dst=dst: e.dma_start(out=dst, in_=xT[:]), t_xT, (), dma=True)
                    for dc in range(NDC):
                        sumsq_chunk(xT[:, dc, :], t_xT[dc], dc)
                    make_rstd()
                    make_h("mix_pre")
                    has_next = (tt != tiles[-1])
                    if has_next:
                        x_load(tt + 1)
                    for cb in range(25):
                        if has_next and cb == 10:
                            x_transpose()
                        iw = load_w_cols(win, cb * 256, 256)
                        if cb < 21:
                            zi = 0
                            for j in range(2):
                                pd = 4 + rot("ps_d", 2)
                                P.mm_group([(psb[pd][:], wgu[iw][:, dc, j * 128:(j + 1) * 128], hT[:, dc, :]) for dc in range(NDC)],
                                           [t_wgu[iw]] + t_hT, [t_ps[pd]])
                                if cb < 13:
                                    P.op("scalar", lambda e, zi=zi, j=j, pd=pd: e.copy(out=zst[zi][:, j, :], in_=psb[pd][:]),
                                         (t_ps[pd],), (t_zst[zi],))
                                else:
                                    P.op("scalar", lambda e, zi=zi, j=j, pd=pd: e.copy(out=zsb[zi][:, j, :], in_=psb[pd][:]),
                                         (t_ps[pd],), (t_zsb[zi],))
                            if cb < 13:
                                dst = zr_d.rearrange("c p t -> p c t")[:, cb * 2:cb * 2 + 2, t0:t0 + TT]
                                P.op("sync", lambda e, dst=dst, zi=zi: e.dma_start(out=dst, in_=zst[zi][:]), (t_zst[zi],), (), dma=True)
                            else:
                                c2 = (cb - 13) * 2
                                dst = zqk_d.rearrange("c p t -> p c t")[:, c2:c2 + 2, t0:t0 + TT]
                                P.op("sync", lambda e, dst=dst, zi=zi: e.dma_start(out=dst, in_=zsb[zi][:]), (t_zsb[zi],), (), dma=True)
                        else:
                            vb = cb - 21
                            vi = 0
                            for tb in range(4):
                                pd = 4 + rot("ps_d", 2)
                                P.mm_group([(psb[pd][:, 0:256], hT[:, dc, tb * 128:(tb + 1) * 128], wgu[iw][:, dc, :]) for dc in range(NDC)],
                                           [t_wgu[iw]] + t_hT, [t_ps[pd]])
                                P.op("scalar", lambda e, vi=vi, tb=tb, pd=pd: e.copy(out=vst[vi][:, tb, :], in_=psb[pd][:, 0:256]),
                                     (t_ps[pd],), (t_vst[vi],))
                            dst = zv_d[t0:t0 + TT, vb * 256:(vb + 1) * 256].rearrange("(b p) f -> p b f", p=128)
                            P.op("sync", lambda e, dst=dst, vi=vi: e.dma_start(out=dst, in_=vst[vi][:]), (t_vst[vi],), (), dma=True)
                if do_c:
                    src = x1T_d.rearrange("c p t -> p c t")[:, :, t0:t0 + TT]
                    P.op("sync", lambda e, src=src: e.dma_start(out=xT[:], in_=src), (), t_xT, dma=True)
                    src = oT_d.rearrange("c p t -> p c t")[:, :, t0:t0 + TT]
                    P.op("sync", lambda e, src=src: e.dma_start(out=hT[:], in_=src), (), t_hT, dma=True)
                    for db in range(8):
                        iw = load_w_cols(wout, db * 256, 256)
                        for j in range(2):
                            dc = db * 2 + j
                            pd = 4 + rot("ps_d", 2)
                            P.mm_group([(psb[pd][:], wgu[iw][:, mc, j * 128:(j + 1) * 128], hT[:, mc, :]) for mc in range(NDC)],
                                       [t_wgu[iw]] + t_hT, [t_ps[pd]])
                            P.op(V, lambda e, dc=dc, pd=pd: e.tensor_copy(out=fT(dc), in_=psb[pd][:]), (t_ps[pd],), (t_big[dc],))
                            sumsq_chunk(fT(dc), t_big[dc], dc)
                    make_rstd()
                    residual_update(lambda dc: pcol("mix_post", dc))
                    ffn("ffn2_pre", "ffn2_posth", w2g, w2u, w2d)
                    for tb in range(4):
                        for d4 in range(4):
                            for j in range(4):
                                dc = d4 * 4 + j
                                P.op("tensor", lambda e, dc=dc, tb=tb, j=j: e.transpose(
                                    psb[7][:, j * 128:(j + 1) * 128], xT[:, dc, tb * 128:(tb + 1) * 128], identf),
                                     [t_xT[dc], t_con], (t_ps[7],), sig=(j == 3), nowait=False)
                            P.op("scalar", lambda e, tb=tb, d4=d4: e.copy(
                                out=big[:, tb * 2048 + d4 * 512: tb * 2048 + (d4 + 1) * 512], in_=psb[7][:]),
                                 (t_ps[7],), t_big)
                    dst = y_d[t0:t0 + TT, :].rearrange("(b p) d -> p b d", p=128)
                    P.op("sync", lambda e, dst=dst: e.dma_start(out=dst, in_=big[:, :].rearrange("p (b d) -> p b d", b=4)),
                         t_big, (), dma=True)

    if do_a:
        dense_phases(True, False, list(range(ntiles)))
        P.barrier()
    if do_b:
        mixer_phase(nc, P, dbg, locals())
        P.barrier()
    if do_c:
        dense_phases(False, True, list(range(ntiles)))
    P.finish()
    with nc.Block() as block:
        @block.sync
        def _(e):
            P.emit("sync", e)

        @block.gpsimd
        def _(e):
            P.emit("gpsimd", e)

        @block.scalar
        def _(e):
            P.emit("scalar", e)

        @block.vector
        def _(e):
            P.emit("vector", e)

        @block.tensor
        def _(e):
            P.emit("tensor", e)
    es.close()
    return nc


def mixer_phase(nc, P, dbg, env):
    units = dbg.get("units", list(range(NUNIT)))
    if dbg.get("nat", True):
        nat_phase(nc, P, dbg, env, units)
        P.barrier()
    if dbg.get("rwkv", True):
        rwkv_phase(nc, P, dbg, env, units)


NEG = -30000.0
SPECIAL = [(0, 29), (0, 30), (0, 31), (1, 0), (1, 1), (1, 2), (1, 3)]


def nat_phase(nc, P, dbg, env, units):
    zqk_d, zv_d, oT_d, G_d, S_d, con, t_con = (env[k] for k in ("zqk_d", "zv_d", "oT_d", "G_d", "S_d", "con", "t_con"))
    V = "vector"
    with ExitStack() as ds:
        sbd = lambda n, s, dt=F32: ds.enter_context(nc.sbuf_tensor("n_" + n, s, dt))
        psd = lambda n, s, dt=F32: ds.enter_context(nc.psum_tensor("n_" + n, s, dt))
        qT = [sbd("qT%d" % i, [128, UNIT], BF16) for i in range(2)]
        kT = [sbd("kT%d" % i, [128, UNIT + 512], BF16) for i in range(2)]
        vx = [sbd("vx%d" % i, [128, 20, 2, 80], BF16) for i in range(2)]
        vxo = [sbd("vxo%d" % i, [128, 20, 2, 80], BF16) for i in range(2)]
        G = [sbd("G%d" % i, [128, 2, 2, 7, 64]) for i in range(2)]
        Ssp = [sbd("S%d" % i, [128, 6, 64]) for i in range(2)]
        sbuf = [sbd("sb%d" % i, [128, 384]) for i in range(2)]
        pT = [sbd("pT%d" % i, [128, 384], BF16) for i in range(2)]
        onesb = sbd("onesb", [128, 64], BF16)
        rs = sbd("rs", [128, 512])
        srow = sbd("srow", [128, 512])
        ob = [sbd("ob%d" % i, [128, 512], BF16) for i in range(2)]
        ps_s = [psd("s%d" % i, [128, 512]) for i in range(2)]
        ps_o = [psd("o%d" % i, [128, 512]) for i in range(4)]
        ps_bc = psd("bc", [128, 512])
        t_qT = [Tk(), Tk()]; t_kT = [Tk(), Tk()]; t_vx = [Tk(), Tk()]; t_G = [Tk(), Tk()]; t_S = [Tk(), Tk()]
        t_sb = [Tk(), Tk()]; t_pT = [Tk(), Tk()]; t_ones = Tk(); t_rs = Tk(); t_ob = [Tk(), Tk()]
        t_ps_s = [Tk(), Tk()]; t_ps_o = [Tk() for _ in range(4)]; t_bc = Tk(); t_srow = Tk()
        P.op(V, lambda e: e.tensor_copy(out=onesb[:], in_=con[:, CC["ones"]:CC["ones"] + 64]), (t_con,), (t_ones,))
        P.op(V, lambda e: e.memset(kT[0][:], 0.0), (), (t_kT[0],))
        P.op(V, lambda e: e.memset(kT[1][:], 0.0), (), (t_kT[1],))
        for vt, tt_ in ((vx[0], t_vx[0]), (vx[1], t_vx[1]), (vxo[0], t_vx[0]), (vxo[1], t_vx[1])):
            P.op(V, lambda e, vt=vt: e.memset(vt[:], 0.0), (), (tt_,))
            P.op(V, lambda e, vt=vt: e.memset(vt[:, :, :, 64:65], 1.0), (), (tt_,))
        cnt = {"b": 0, "s": 0, "sp": 0, "g": 0, "ob": 0}
        def nat_loads(u, hp, b):
            U0 = u * UNIT
            P.op("sync", lambda e, b=b, hp=hp, U0=U0: e.dma_start(out=qT[b][:], in_=zqk_d[hp, :, U0:U0 + UNIT]),
                 (), (t_qT[b],), dma=True)
            lo = U0 - 256 if u == 1 else U0
            hi = U0 + UNIT + 256 if u == 0 else U0 + UNIT
            P.op("sync", lambda e, b=b, hp=hp, lo=lo, hi=hi, U0=U0: e.dma_start(
                out=kT[b][:, lo - (U0 - 256):hi - (U0 - 256)], in_=zqk_d[8 + hp, :, lo:hi]), (), (t_kT[b],), dma=True)
            blo = (lo - (U0 - 256)) // 128
            nb = (hi - lo) // 128
            for jh in range(2):
                P.op("sync", lambda e, b=b, hp=hp, lo=lo, hi=hi, blo=blo, nb=nb, jh=jh: e.dma_start(
                    out=vx[b][:, blo:blo + nb, jh, 0:64],
                    in_=zv_d[lo:hi, hp * 128 + jh * 64:hp * 128 + (jh + 1) * 64].rearrange("(k p) f -> p k f", p=128)),
                     (), (t_vx[b],), dma=True)
            lob = lo - (U0 - 256)
            hib = hi - (U0 - 256)
            kmin = -((64 - lob) // 128)
            kmax = (hib - 64) // 128 - 1
            g0 = (U0 - 256) + 64 + 128 * kmin
            g1 = (U0 - 256) + 64 + 128 * (kmax + 1)
            for jh in range(2):
                P.op("sync", lambda e, b=b, hp=hp, g0=g0, g1=g1, kmin=kmin, kmax=kmax, jh=jh: e.dma_start(
                    out=vxo[b][:, kmin:kmax + 1, jh, 0:64],
                    in_=zv_d[g0:g1, hp * 128 + jh * 64:hp * 128 + (jh + 1) * 64].rearrange("(k p) f -> p k f", p=128)),
                     (), (t_vx[b],), dma=True)
            P.op("sync", lambda e, b=b, hp=hp: e.dma_start(out=G[b][:], in_=G_d[hp]), (), (t_G[b],), dma=True)

        pairs = [(u_, hp_) for u_ in units for hp_ in range(8)]
        nat_loads(pairs[0][0], pairs[0][1], 0)
        for pidx, (u, hp) in enumerate(pairs):
            if True:
                U0 = u * UNIT
                b = pidx % 2
                if pidx + 1 < len(pairs):
                    nat_loads(pairs[pidx + 1][0], pairs[pidx + 1][1], 1 - b)
                for i8 in range(4):
                    po = cnt["ob"] % 2
                    cnt["ob"] += 1
                    pending = [None]
                    for ii in range(8):
                        i = i8 * 8 + ii
                        for j in range(2):
                            hb = 64 * j
                            h = hp * 2 + j
                            if (u, i) in SPECIAL:
                                spi = SPECIAL.index((u, i))
                                nch = 6
                                er0 = 24 if u == 0 else -4
                                si = cnt["sp"] % 2
                                cnt["sp"] += 1
                                P.op("sync", lambda e, si=si, spi=spi, h=h: e.dma_start(out=Ssp[si][:], in_=S_d[spi, h]),
                                     (), (t_S[si],), dma=True)
                                tab = Ssp[si][:, :, :]
                                t_tab = t_S[si]
                            else:
                                ws = min(max(i - 4, 0), 24)
                                off = i - ws
                                nch = 4
                                er0 = ws
                                m0 = 7 - off
                                tab = G[b][:, j, m0 % 2, m0 // 2:m0 // 2 + 4, :]
                                t_tab = t_G[b]
                            ks = (er0 + 4) * 64
                            s_i = cnt["s"] % 2
                            cnt["s"] += 1
                            for c in range(nch):
                                P.op("tensor", lambda e, s_i=s_i, c=c, b=b, hb=hb, ks=ks, i=i: e.matmul(
                                    ps_s[s_i][:, c * 64:(c + 1) * 64], kT[b][hb:hb + 64, ks + c * 128:ks + (c + 1) * 128],
                                    qT[b][hb:hb + 64, i * 64:(i + 1) * 64], start=True, stop=True),
                                     (t_kT[b], t_qT[b]), (t_ps_s[s_i],), sig=(c == nch - 1))
                            n = nch * 64
                            P.op(V, lambda e, s_i=s_i, n=n, nch=nch, tab=tab: e.scalar_tensor_tensor(
                                out=sbuf[s_i][:, 0:n].rearrange("p (c q) -> p c q", c=nch),
                                in0=ps_s[s_i][:, 0:n].rearrange("p (c q) -> p c q", c=nch), scalar=0.125, in1=tab,
                                op0=ALU.mult, op1=ALU.add), (t_ps_s[s_i], t_tab), (t_sb[s_i],))
                            P.op("scalar", lambda e, s_i=s_i, n=n: e.activation(out=pT[s_i][:, 0:n], in_=sbuf[s_i][:, 0:n], func=AF.Exp),
                                 (t_sb[s_i],), (t_pT[s_i],))
                            def pv_stage(nch=nch, er0=er0, po=po, j=j, ii=ii, b=b, s_i=s_i):
                                pso = ps_o[po * 2 + j]
                                tpso = t_ps_o[po * 2 + j]
                                for c in range(nch):
                                    R = er0 + 4 + 2 * c
                                    vsrc = vx[b] if R % 2 == 0 else vxo[b]
                                    blk = R // 2
                                    P.op("tensor", lambda e, vsrc=vsrc, blk=blk, c=c: e.matmul(
                                        pso[0:65, ii * 64:(ii + 1) * 64], vsrc[:, blk, j, 0:65],
                                        pT[s_i][:, c * 64:(c + 1) * 64], start=(c == 0), stop=(c == nch - 1)),
                                         (t_vx[b], t_pT[s_i]), (tpso,), sig=(c == nch - 1))
                            if pending[0] is not None:
                                pending[0]()
                            pending[0] = pv_stage
                    if pending[0] is not None:
                        pending[0]()
                        pending[0] = None
                    for j in range(2):
                        pso = ps_o[po * 2 + j]
                        tpso = t_ps_o[po * 2 + j]
                        P.op("scalar", lambda e, pso=pso: e.copy(out=srow[64:65, :], in_=pso[64:65, :]), (tpso,), (t_srow,))
                        P.mm_group([(ps_bc[0:64, :], con[64:65, CC["ones"]:CC["ones"] + 64], srow[64:65, :])], (t_con, t_srow), (t_bc,))
                        P.op("scalar", lambda e: e.activation(out=rs[0:64, :], in_=ps_bc[0:64, :], func=AF.Ln), (t_bc,), (t_rs,))
                        P.op("scalar", lambda e: e.activation(out=rs[0:64, :], in_=rs[0:64, :], func=AF.Exp, scale=-1.0), (t_rs,), (t_rs,))
                        ob_ = ob[j]
                        P.op(V, lambda e, pso=pso, ob_=ob_: e.tensor_tensor(out=ob_[0:64, :], in0=pso[0:64, :], in1=rs[0:64, :], op=ALU.mult),
                             (tpso, t_rs), (t_ob[j],))
                        P.op("sync", lambda e, ob_=ob_, hp=hp, U0=U0, i8=i8, j=j: e.dma_start(
                            out=oT_d[8 + hp, 64 * j:64 * j + 64, U0 + i8 * 512:U0 + (i8 + 1) * 512], in_=ob_[0:64, :]), (t_ob[j],), (), dma=True)


def rwkv_phase(nc, P, dbg, env, units):
    zr_d, oT_d, yf_d, con, t_con, t_par, t_der, identb, t_identb = (env[k] for k in (
        "zr_d", "oT_d", "yf_d", "con", "t_con", "t_par", "t_der", "identb", "t_identb"))
    lup_dd = {"f": env["lupf_d"], "b": env["lupb_d"]}
    gup_d = env["gup_d"]
    pcol, dcol = env["pcol"], env["dcol"]
    T = UNIT
    V = "vector"
    order = [(0, "f"), (1, "f"), (2, "f"), (2, "b"), (1, "b"), (0, "b")]
    order = [(u, d) for (u, d) in order if u in units]
    if dbg.get("dump"):
        order = order if dbg.get("dump2") else order[dbg.get("dump_pass", 0):][:1]
    bonesf = con[:, CC["bones"]:CC["bones"] + 128]
    onesf = con[:, CC["ones"]:CC["ones"] + 128]
    with ExitStack() as ds:
        sbd = lambda n, s, dt=F32: ds.enter_context(nc.sbuf_tensor("r_" + n, s, dt))
        psd = lambda n, s, dt=F32: ds.enter_context(nc.psum_tensor("r_" + n, s, dt))
        F = [sbd("F%d" % i, [128, T + 2]) for i in range(10)]
        tF = [Tk() for _ in range(10)]
        TMP0, TMP1, KK, R, K, VV, E, A, L, YA = range(10)
        B = TMP1
        AR = sbd("AR", [128, 16, 256], BF16); tAR = Tk()
        Bf = sbd("Bf", [128, T], BF16); tBf = Tk()
        Kf = sbd("Kf", [128, T], BF16); tKf = Tk()
        vb = sbd("vb", [128, T], BF16); tvb = Tk()
        Btm = sbd("Btm", [128, 16, 128], BF16); tBtm = Tk()
        Ktm = sbd("Ktm", [128, 16, 128], BF16); tKtm = Tk()
        Vtm = sbd("Vtm", [128, 16, 128], BF16); tVtm = Tk()
        Am = sbd("Am", [128, 32, 512], BF16); tAm = [Tk() for _ in range(32)]
        TTs = sbd("TTs", [128, 32, 128], BF16); tTTs = [Tk() for _ in range(32)]
        Xb = sbd("Xb", [128, 8, 2, 256], BF16); tXb = [[Tk(), Tk()] for _ in range(8)]
        TTb = sbd("TTb", [128, 8, 2, 128], BF16); tTTb = [[Tk(), Tk()] for _ in range(8)]
        zs24b = sbd("zs24b", [128, T], BF16); tz24 = Tk()
        glb = sbd("glb", [128, T], BF16); tglb = Tk()
        lup = {"f": sbd("lupf", [128, 1024], BF16), "b": sbd("lupb", [128, 1024], BF16)}
        gup = sbd("gup", [128, 1024], BF16); tlw = Tk()
        mids = sbd("mids", [128, 16]); tots = sbd("tots", [128, 16]); biasm = sbd("biasm", [128, 16])
        nbiasm = sbd("nbiasm", [128, 16]); epsk = sbd("epsk", [128, 2]); tsm0 = Tk()
        scj = sbd("scj", [128, 16]); sci = sbd("sci", [128, 1]); sct = sbd("sct", [128, 16]); tsm = Tk()
        Hf = sbd("Hf", [128, 64]); Hb = sbd("Hb", [128, 64], BF16); Ht = sbd("Ht", [128, 64]); tH = Tk(); tHb = Tk(); tHt = Tk()
        Hc = sbd("Hc", [128, 2, 8, 64]); tHc = Tk()
        Xs = sbd("Xs", [128, 128], BF16); tXs = Tk()
        Ub = sbd("Ub", [128, 128], BF16); tUb = Tk()
        fin = [sbd("fin%d" % i, [128, 512]) for i in range(4)]; tfin = [Tk() for _ in range(4)]
        finb = sbd("finb", [128, 512], BF16); tfinb = Tk()
        pb = [psd("pb%d" % i, [128, 512]) for i in range(2)]; tpb = [Tk(), Tk()]
        pA = [psd("pA%d" % i, [128, 512]) for i in range(2)]; tpA = [Tk(), Tk()]
        pX = [psd("pX%d" % i, [128, 512]) for i in range(2)]; tpX = [Tk(), Tk()]
        pS = psd("pS", [128, 512]); tpS = {k: Tk() for k in "SUHY"}
        pT = psd("pT", [128, 1024], BF16); tpT = Tk()
        cnt = {"pb": 0, "pA": 0, "pX": 0}

        def rot(n):
            i = cnt[n] % 2
            cnt[n] += 1
            return i

        vop = lambda fn, r, w: P.op(V, fn, r, w)
        aop = lambda fn, r, w: P.op("scalar", fn, r, w)
        fa = lambda i: F[i][:, 0:T]
        blkc = lambda ap, k: ap[:, k * 512:(k + 1) * 512]
        for d in ("f", "b"):
            P.op("gpsimd", lambda e, d=d: e.dma_start(out=lup[d][:], in_=lup_dd[d][:, :]), (), (tlw,), dma=True)
        P.op("gpsimd", lambda e: e.dma_start(out=gup[:], in_=gup_d[:, :]), (), (tlw,), dma=True)
        vop(lambda e: e.memset(Hc[:], 0.0), (), (tHc,))
        vop(lambda e: e.memset(epsk[:, 0:1], 1e-24), (), (tsm0,))
        vop(lambda e: e.memset(epsk[:, 1:2], 64e-5), (), (tsm0,))

        def load_shift(ch, dst, u):
            U0 = u * T
            raw = F[TMP0]
            P.op("sync", lambda e: e.dma_start(out=raw[:, 1:T + 1], in_=zr_d[ch, :, U0:U0 + T]), (), (tF[TMP0],), dma=True)
            if u == 1:
                P.op("sync", lambda e: e.dma_start(out=raw[:, 0:1], in_=zr_d[ch, :, U0 - 1:U0], allow_slow_non_contiguous=True), (), (tF[TMP0],), dma=True)
                vop(lambda e: e.tensor_scalar(out=raw[:, 0:1], in0=raw[:, 0:1], scalar1=pcol("flag"), scalar2=None, op0=ALU.mult),
                    (tF[TMP0], t_par), (tF[TMP0],))
            else:
                vop(lambda e: e.memset(raw[:, 0:1], 0.0), (), (tF[TMP0],))
            if u == 0:
                P.op("sync", lambda e: e.dma_start(out=raw[:, T + 1:T + 2], in_=zr_d[ch, :, U0 + T:U0 + T + 1], allow_slow_non_contiguous=True), (), (tF[TMP0],), dma=True)
                vop(lambda e: e.tensor_scalar(out=raw[:, T + 1:T + 2], in0=raw[:, T + 1:T + 2], scalar1=pcol("flag"), scalar2=None,
                                              op0=ALU.mult), (tF[TMP0], t_par), (tF[TMP0],))
            else:
                vop(lambda e: e.memset(raw[:, T + 1:T + 2], 0.0), (), (tF[TMP0],))
            vop(lambda e: e.tensor_tensor(out=fa(TMP1), in0=raw[:, 0:T], in1=raw[:, 2:T + 2], op=ALU.add), (tF[TMP0],), (tF[TMP1],))
            vop(lambda e: e.tensor_scalar(out=fa(TMP1), in0=fa(TMP1), scalar1=dcol("hmu", ch), scalar2=None, op0=ALU.mult),
                (tF[TMP1], t_der), (tF[TMP1],))
            vop(lambda e: e.scalar_tensor_tensor(out=fa(dst), in0=raw[:, 1:T + 1], scalar=dcol("omu", ch), in1=fa(TMP1),
                                                 op0=ALU.mult, op1=ALU.add), (tF[TMP0], tF[TMP1], t_der), (tF[dst],))

        def raw_load(ch, ri, u):
            U0 = u * T
            raw = F[ri]
            P.op("sync", lambda e: e.dma_start(out=raw[:, 1:T + 1], in_=zr_d[ch, :, U0:U0 + T]), (), (tF[ri],), dma=True)
            if u == 1:
                P.op("sync", lambda e: e.dma_start(out=raw[:, 0:1], in_=zr_d[ch, :, U0 - 1:U0], allow_slow_non_contiguous=True), (), (tF[ri],), dma=True)
            if u == 0:
                P.op("sync", lambda e: e.dma_start(out=raw[:, T + 1:T + 2], in_=zr_d[ch, :, U0 + T:U0 + T + 1], allow_slow_non_contiguous=True), (), (tF[ri],), dma=True)

        def shift_from(ch, ri, dst, u, eng):
            raw = F[ri]
            xop = lambda fn, r, w: P.op(eng, fn, r, w)
            if u == 1:
                xop(lambda e: e.tensor_scalar(out=raw[:, 0:1], in0=raw[:, 0:1], scalar1=pcol("flag"), scalar2=None, op0=ALU.mult),
                    (tF[ri], t_par), (tF[ri],))
            else:
                xop(lambda e: e.memset(raw[:, 0:1], 0.0), (), (tF[ri],))
            if u == 0:
                xop(lambda e: e.tensor_scalar(out=raw[:, T + 1:T + 2], in0=raw[:, T + 1:T + 2], scalar1=pcol("flag"), scalar2=None,
                                              op0=ALU.mult), (tF[ri], t_par), (tF[ri],))
            else:
                xop(lambda e: e.memset(raw[:, T + 1:T + 2], 0.0), (), (tF[ri],))
            xop(lambda e: e.tensor_tensor(out=fa(dst), in0=raw[:, 0:T], in1=raw[:, 2:T + 2], op=ALU.add), (tF[ri],), (tF[dst],))
            xop(lambda e: e.tensor_scalar(out=fa(dst), in0=fa(dst), scalar1=dcol("hmu", ch), scalar2=None, op0=ALU.mult),
                (tF[dst], t_der), (tF[dst],))
            if eng == "gpsimd":
                xop(lambda e: e.tensor_scalar(out=raw[:, 1:T + 1], in0=raw[:, 1:T + 1], scalar1=dcol("omu", ch), scalar2=None, op0=ALU.mult),
                    (tF[ri], t_der), (tF[ri],))
                xop(lambda e: e.tensor_tensor(out=fa(dst), in0=fa(dst), in1=raw[:, 1:T + 1], op=ALU.add), (tF[ri], tF[dst]), (tF[dst],))
            else:
                xop(lambda e: e.scalar_tensor_tensor(out=fa(dst), in0=raw[:, 1:T + 1], scalar=dcol("omu", ch), in1=fa(dst),
                                                     op0=ALU.mult, op1=ALU.add), (tF[ri], tF[dst], t_der), (tF[dst],))

        def raw_loads(hp, u):
            raw_load(hp, E, u)
            raw_load(8 + hp, A, u)
            raw_load(16 + hp, L, u)

        def lora_sig(d, prow, bname, hp, dst):
            for k in range(4):
                i = rot("pb")
                P.mm_group([(pb[i][:], lup[d][prow:prow + 64, hp * 128:(hp + 1) * 128], blkc(zs24b[prow:prow + 64, :], k))],
                           (tlw, tz24), (tpb[i],))
                aop(lambda e, i=i, k=k: e.activation(out=blkc(fa(dst), k), in_=pb[i][:], func=AF.Sigmoid, bias=pcol(bname, hp)),
                    (tpb[i], t_par), (tF[dst],))

        def kd_from_a(hp):
            vop(lambda e: e.tensor_scalar(out=fa(A), in0=fa(A), scalar1=pcol("kim", hp), scalar2=dcol("omk", hp), op0=ALU.mult,
                                          op1=ALU.add), (tF[A], t_par, t_der), (tF[A],))
            vop(lambda e: e.tensor_tensor(out=fa(A), in0=fa(A), in1=fa(K), op=ALU.mult), (tF[A], tF[K]), (tF[A],))

        for (u, d) in order:
            U0 = u * T
            fwd = d == "f"
            m4 = con[:, CC["m4f"]:CC["m4f"] + 512] if fwd else con[:, CC["m4b"]:CC["m4b"] + 512]
            ml = con[:, CC["mlf"]:CC["mlf"] + 128] if fwd else con[:, CC["mlb"]:CC["mlb"] + 128]
            load_shift(24, E, u)
            aop(lambda e: e.activation(out=zs24b[0:64, :], in_=F[E][0:64, 0:T], func=AF.Tanh), (tF[E],), (tz24,))
            aop(lambda e: e.copy(out=zs24b[64:128, :], in_=F[E][64:128, 0:T]), (tF[E],), (tz24,))
            load_shift(25, E, u)
            aop(lambda e: e.activation(out=glb[:], in_=fa(E), func=AF.Sigmoid), (tF[E],), (tglb,))
            for hp in range(1 if dbg.get("dump2") else 8):
                if hp == 0:
                    raw_loads(0, u)
                shift_from(hp, E, R, u, V)
                shift_from(16 + hp, L, VV, u, V)
                shift_from(8 + hp, A, K, u, V)
                aop(lambda e, hp=hp: e.activation(out=fa(TMP0), in_=fa(K), func=AF.Square, scale=pcol("kns", hp)),
                    (tF[K], t_par), (tF[TMP0],))
                for k in range(4):
                    i = rot("pb")
                    P.mm_group([(pb[i][:], bonesf, blkc(fa(TMP0), k))], (t_con, tF[TMP0]), (tpb[i],))
                    aop(lambda e, i=i, k=k: e.activation(out=blkc(fa(TMP1), k), in_=pb[i][:], func=AF.Ln, bias=epsk[:, 0:1]), (tpb[i], tsm0), (tF[TMP1],))
                aop(lambda e: e.activation(out=fa(TMP1), in_=fa(TMP1), func=AF.Exp, scale=-0.5), (tF[TMP1],), (tF[TMP1],))
                vop(lambda e, hp=hp: e.scalar_tensor_tensor(out=fa(KK), in0=fa(K), scalar=pcol("kns", hp), in1=fa(TMP1), op0=ALU.mult,
                                                            op1=ALU.mult), (tF[K], tF[TMP1], t_par), (tF[KK],))
                if not fwd:
                    lora_sig("f", 64, "ibf", hp, A)
                    kd_from_a(hp)
                    vop(lambda e: e.tensor_copy(out=fa(TMP0), in_=fa(A)), (tF[A],), (tF[TMP0],))
                lora_sig(d, 0, "dbf" if fwd else "dbb", hp, E)
                lora_sig(d, 64, "ibf" if fwd else "ibb", hp, A)
                vop(lambda e: e.tensor_tensor(out=fa(B), in0=fa(KK), in1=fa(A), op=ALU.mult), (tF[KK], tF[A]), (tF[B],))
                kd_from_a(hp)
                if not fwd:
                    vop(lambda e: e.tensor_tensor(out=fa(TMP0), in0=fa(TMP0), in1=fa(A), op=ALU.add), (tF[TMP0], tF[A]), (tF[TMP0],))
                    vop(lambda e, hp=hp: e.scalar_tensor_tensor(out=fa(TMP0), in0=fa(TMP0), scalar=dcol("hbs", hp), in1=fa(R),
                                                                op0=ALU.mult, op1=ALU.mult), (tF[TMP0], tF[R], t_der), (tF[TMP0],))
                for j in range(16):
                    vop(lambda e, j=j: e.tensor_tensor_scan(out=F[L][:, j * 128:(j + 1) * 128], data0=F[E][:, j * 128:(j + 1) * 128],
                                                            data1=onesf, initial=0.0, op0=ALU.add, op1=ALU.mult),
                        (tF[E], t_con), (tF[L],))
                vop(lambda e: e.tensor_tensor(out=fa(E), in0=fa(L), in1=fa(E), op=ALU.subtract), (tF[L], tF[E]), (tF[E],))
                L3 = fa(L).rearrange("p (j t) -> p j t", t=128)
                vop(lambda e: e.tensor_copy(out=mids[:], in_=L3[:, :, 63]), (tF[L],), (tsm,))
                vop(lambda e: e.tensor_copy(out=tots[:], in_=L3[:, :, 127]), (tF[L],), (tsm,))
                vop(lambda e: e.tensor_scalar(out=biasm[:], in0=mids[:], scalar1=C0, scalar2=None, op0=ALU.mult), (tsm,), (tsm,))
                vop(lambda e: e.tensor_tensor(out=sct[:], in0=tots[:], in1=mids[:], op=ALU.subtract), (tsm,), (tsm,))
                if fwd:
                    vop(lambda e: e.tensor_copy(out=scj[:], in_=sct[:]), (tsm,), (tsm,))
                    vop(lambda e: e.tensor_tensor(out=scj[:, 0:15], in0=sct[:, 0:15], in1=mids[:, 1:16], op=ALU.add), (tsm,), (tsm,))
                    aop(lambda e: e.activation(out=sci[:], in_=mids[:, 0:1], func=AF.Exp, scale=-C0), (tsm,), (tsm,))
                else:
                    vop(lambda e: e.tensor_copy(out=scj[:], in_=mids[:]), (tsm,), (tsm,))
                    vop(lambda e: e.tensor_tensor(out=scj[:, 1:16], in0=mids[:, 1:16], in1=sct[:, 0:15], op=ALU.add), (tsm,), (tsm,))
                    aop(lambda e: e.activation(out=sci[:], in_=sct[:, 15:16], func=AF.Exp, scale=-C0), (tsm,), (tsm,))
                aop(lambda e: e.activation(out=scj[:], in_=scj[:], func=AF.Exp, scale=-C0), (tsm,), (tsm,))
                vop(lambda e: e.tensor_scalar(out=nbiasm[:], in0=mids[:], scalar1=-C0, scalar2=None, op0=ALU.mult), (tsm,), (tsm,))
                ARa = AR[:, :, 0:128]
                ARr = AR[:, :, 128:256]
                v3 = lambda i: fa(i).rearrange("p (j t) -> p j t", t=128)

                def exps(dst, src, sign):
                    bb = biasm if sign < 0 else nbiasm
                    for j in range(16):
                        aop(lambda e, j=j: e.activation(out=F[dst][:, j * 128:(j + 1) * 128], in_=F[src][:, j * 128:(j + 1) * 128], func=AF.Exp,
                                                        scale=sign * C0, bias=bb[:, j:j + 1]), (tF[src], tsm), (tF[dst],))
                if fwd:
                    exps(YA, L, -1.0)
                    vop(lambda e: e.tensor_tensor(out=ARr, in0=v3(R), in1=v3(YA), op=ALU.mult), (tF[R], tF[YA]), (tAR,))
                    exps(E, E, -1.0)
                    vop(lambda e: e.scalar_tensor_tensor(out=ARa, in0=v3(KK), scalar=-1.0, in1=v3(E), op0=ALU.mult, op1=ALU.mult),
                        (tF[KK], tF[E]), (tAR,))
                    exps(L, L, 1.0)
                    vop(lambda e: e.tensor_tensor(out=Bf[:], in0=fa(B), in1=fa(L), op=ALU.mult), (tF[B], tF[L]), (tBf,))
                    vop(lambda e: e.tensor_tensor(out=Kf[:], in0=fa(A), in1=fa(L), op=ALU.mult), (tF[A], tF[L]), (tKf,))
                else:
                    exps(YA, E, -1.0)
                    vop(lambda e: e.tensor_tensor(out=Bf[:], in0=fa(B), in1=fa(YA), op=ALU.mult), (tF[B], tF[YA]), (tBf,))
                    vop(lambda e: e.tensor_tensor(out=Kf[:], in0=fa(A), in1=fa(YA), op=ALU.mult), (tF[A], tF[YA]), (tKf,))
                    exps(E, E, 1.0)
                    vop(lambda e: e.tensor_tensor(out=ARr, in0=v3(R), in1=v3(E), op=ALU.mult), (tF[R], tF[E]), (tAR,))
                    exps(L, L, 1.0)
                    vop(lambda e: e.scalar_tensor_tensor(out=ARa, in0=v3(KK), scalar=-1.0, in1=v3(L), op0=ALU.mult, op1=ALU.mult),
                        (tF[KK], tF[L]), (tAR,))
                aop(lambda e: e.copy(out=vb[:], in_=fa(VV)), (tF[VV],), (tvb,))
                if hp < 7 and not dbg.get("dump2"):
                    raw_loads(hp + 1, u)
                if not fwd:
                    P.op("sync", lambda e, hp=hp, U0=U0: e.dma_start(out=fa(TMP1), in_=yf_d[hp, :, U0:U0 + T]), (), (tF[TMP1],), dma=True)
                for (src, tsrc, dst, tdst) in ((Bf, tBf, Btm, tBtm), (Kf, tKf, Ktm, tKtm), (vb, tvb, Vtm, tVtm)):
                    for half in range(2):
                        for jj in range(8):
                            j = half * 8 + jj
                            P.op("tensor", lambda e, src=src, j=j, jj=jj: e.transpose(pT[:, jj * 128:(jj + 1) * 128],
                                                                                      src[:, j * 128:(j + 1) * 128], identb[:]),
                                 (tsrc, t_identb), (tpT,), sig=(jj == 7))
                        aop(lambda e, dst=dst, half=half: e.copy(out=dst[:, half * 8:(half + 1) * 8, :],
                                                                 in_=pT[:, :].rearrange("p (j c) -> p j c", c=128)),
                            (tpT,), (tdst,))
                for hd in range(2):
                    hb = 64 * hd
                    for j in range(16):
                        pi = hd * 16 + j
                        cs = slice(j * 128, (j + 1) * 128)
                        i = rot("pA")
                        P.op("tensor", lambda e, i=i, hb=hb, cs=cs, j=j: e.matmul(pA[i][:, 0:256], Bf[hb:hb + 64, cs], AR[hb:hb + 64, j, :],
                                                                                 start=True, stop=True), (tBf, tAR), (tpA[i],), sig=False)
                        P.op("tensor", lambda e, i=i, hb=hb, cs=cs, j=j: e.matmul(pA[i][:, 256:512], Kf[hb:hb + 64, cs], AR[hb:hb + 64, j, :],
                                                                                 start=True, stop=True), (tKf, tAR), (tpA[i],))
                        vop(lambda e, i=i, pi=pi, m4=m4: e.tensor_tensor(out=Am[:, pi, :], in0=pA[i][:], in1=m4, op=ALU.mult),
                            (tpA[i], t_con), (tAm[pi],))
                SQB = [(pb[0], tpb[0]), (pb[1], tpb[1]), (pA[0], tpA[0]), (pA[1], tpA[1])]
                TTB = [(pX[0], tpX[0]), (pX[1], tpX[1])]
                for g in range(4):
                    prs = [g * 8 + q for q in range(8)]
                    for bk in range(4):
                        bank, tbank = SQB[bk]
                        mms = []
                        for q in (2 * bk, 2 * bk + 1):
                            pi = prs[q]
                            hd, j = pi // 16, pi % 16
                            hb = 64 * hd
                            c0 = (q % 2) * 256
                            P.op("tensor", lambda e, bank=bank, c0=c0, hb=hb, j=j: e.matmul(
                                bank[:, c0:c0 + 128], AR[hb:hb + 64, j, 0:128], Bf[hb:hb + 64, j * 128:(j + 1) * 128], start=True, stop=True),
                                 (tAR, tBf), (tbank,), sig=(q % 2 == 1))
                        for q in (2 * bk, 2 * bk + 1):
                            pi = prs[q]
                            c0 = (q % 2) * 256
                            vop(lambda e, q=q, ml=ml, bank=bank, c0=c0: e.tensor_tensor(out=Xb[:, q, 0, 0:128], in0=bank[:, c0:c0 + 128], in1=ml,
                                                                                        op=ALU.mult), (tbank, t_con), (tXb[q][0],))
                            aop(lambda e, q=q, pi=pi: e.copy(out=Xb[:, q, 0, 128:256], in_=Am[:, pi, 0:128]), (tAm[pi],), (tXb[q][0],))
                            vop(lambda e, q=q, pi=pi: e.tensor_tensor(out=TTb[:, q, 0, :], in0=Am[:, pi, 0:128], in1=identb[:], op=ALU.add),
                                (tAm[pi], t_identb), (tTTb[q][0],))
                    for lv in range(6):
                        cur, nxt = lv % 2, (lv + 1) % 2
                        last = lv == 5
                        n = 128 if last else 256
                        for bk in range(4):
                            bank, tbank = SQB[bk]
                            qs = (2 * bk, 2 * bk + 1)
                            nmm = 0
                            for q in qs:
                                c0 = (q % 2) * 256
                                X_ = Xb[:, q, cur, 0:128]
                                XT_ = Xb[:, q, cur, 128:256]
                                fin_ = (q % 2 == 1)
                                P.op("tensor", lambda e, bank=bank, c0=c0, X_=X_, XT_=XT_: e.matmul(bank[:, c0:c0 + 128], XT_, X_, start=True, stop=True),
                                     (tXb[q][cur],), (tbank,), sig=(last and fin_))
                                if not last:
                                    P.op("tensor", lambda e, bank=bank, c0=c0, X_=X_, XT_=XT_: e.matmul(bank[:, c0 + 128:c0 + 256], X_, XT_, start=True,
                                                                                                       stop=True), (tXb[q][cur],), (tbank,), sig=fin_)
                            q0 = qs[0]
                            aop(lambda e, bank=bank, q0=q0, nxt=nxt, n=n: e.copy(
                                out=Xb[:, q0:q0 + 2, nxt, 0:n], in_=bank[:, :].rearrange("p (a c) -> p a c", a=2)[:, :, 0:n]),
                                (tbank,), (tXb[qs[0]][nxt], tXb[qs[1]][nxt]))
                        for tb in range(2):
                            bank, tbank = TTB[tb]
                            qs = list(range(4 * tb, 4 * tb + 4))
                            for q in qs:
                                P.op("tensor", lambda e, bank=bank, q=q, nxt=nxt, cur=cur: e.matmul(
                                    bank[:, (q % 4) * 128:(q % 4 + 1) * 128], Xb[:, q, nxt, 0:128], TTb[:, q, cur, :], start=True, stop=True),
                                     (tTTb[q][cur], tXb[q][nxt]), (tbank,), sig=(q % 4 == 3))
                            q0 = qs[0]
                            bview = bank[:, :].rearrange("p (a c) -> p a c", a=4)
                            if last:
                                pi0 = prs[q0]
                                vop(lambda e, bview=bview, q0=q0, pi0=pi0, cur=cur: e.tensor_tensor(
                                    out=TTs[:, pi0:pi0 + 4, :], in0=bview, in1=TTb[:, q0:q0 + 4, cur, :], op=ALU.add),
                                    [tbank] + [tTTb[q][cur] for q in qs], [tTTs[prs[q]] for q in qs])
                            else:
                                vop(lambda e, bview=bview, q0=q0, nxt=nxt, cur=cur: e.tensor_tensor(
                                    out=TTb[:, q0:q0 + 4, nxt, :], in0=bview, in1=TTb[:, q0:q0 + 4, cur, :], op=ALU.add),
                                    [tbank] + [tTTb[q][cur] for q in qs], [tTTb[q][nxt] for q in qs])
                di = 0 if fwd else 1
                hcar = Hc[:, di, hp, :]
                linked_in = (fwd and u == 1) or ((not fwd) and u == 0)
                if linked_in:
                    vop(lambda e, hcar=hcar: e.tensor_scalar(out=Hf[:], in0=hcar, scalar1=sci[:, 0:1], scalar2=pcol("flag"), op0=ALU.mult,
                                                             op1=ALU.mult), (tHc, tsm, t_par), (tH,))
                else:
                    vop(lambda e: e.memset(Hf[:], 0.0), (), (tH,))
                aop(lambda e: e.copy(out=Hb[:], in_=Hf[:]), (tH,), (tHb,))
                jorder = list(range(16)) if fwd else list(range(15, -1, -1))
                for j in jorder:
                    for hd in range(2):
                        hb = 64 * hd
                        pi = hd * 16 + j
                        P.mm_group([(pS[:, hd * 64:(hd + 1) * 64], AR[hb:hb + 64, j, 0:128], Hb[hb:hb + 64, :]),
                                    (pS[:, hd * 64:(hd + 1) * 64], Am[:, pi, 256:384], Vtm[:, j, hb:hb + 64])],
                                   (tAR, tHb, tAm[pi], tVtm), (tpS["S"],))
                    aop(lambda e: e.copy(out=Xs[:], in_=pS[:, 0:128]), (tpS["S"],), (tXs,))
                    for hd in range(2):
                        pi = hd * 16 + j
                        P.mm_group([(pS[:, 128 + hd * 64:128 + (hd + 1) * 64], TTs[:, pi, :], Xs[:, hd * 64:(hd + 1) * 64])],
                                   (tTTs[pi], tXs), (tpS["U"],))
                    aop(lambda e: e.copy(out=Ub[:], in_=pS[:, 128:256]), (tpS["U"],), (tUb,))
                    for hd in range(2):
                        hb = 64 * hd
                        pi = hd * 16 + j
                        P.mm_group([(pS[hb:hb + 64, 320:448], Hb[hb:hb + 64, :], AR[hb:hb + 64, j, 128:256]),
                                    (pS[hb:hb + 64, 320:448], Ub[:, hd * 64:(hd + 1) * 64], Am[:, pi, 128:256]),
                                    (pS[hb:hb + 64, 320:448], Vtm[:, j, hb:hb + 64], Am[:, pi, 384:512])],
                                   (tHb, tAR, tUb, tAm[pi], tVtm), (tpS["Y"],))
                        P.mm_group([(pS[hb:hb + 64, 256:320], Btm[:, j, hb:hb + 64], Ub[:, hd * 64:(hd + 1) * 64]),
                                    (pS[hb:hb + 64, 256:320], Ktm[:, j, hb:hb + 64], Vtm[:, j, hb:hb + 64])],
                                   (tBtm, tUb, tKtm, tVtm), (tpS["H"],))
                    aop(lambda e, j=j: e.copy(out=F[YA][:, j * 128:(j + 1) * 128], in_=pS[:, 320:448]), (tpS["Y"],), (tF[YA],))
                    vop(lambda e: e.tensor_tensor(out=Ht[:], in0=pS[:, 256:320], in1=Hf[:], op=ALU.add), (tpS["H"], tH), (tHt,))
                    vop(lambda e, j=j: e.tensor_scalar(out=Hf[:], in0=Ht[:], scalar1=scj[:, j:j + 1], scalar2=None, op0=ALU.mult),
                        (tHt, tsm), (tH,))
                    aop(lambda e: e.copy(out=Hb[:], in_=Hf[:]), (tH,), (tHb,))
                vop(lambda e, hcar=hcar: e.tensor_copy(out=hcar, in_=Hf[:]), (tH,), (tHc,))
                if dbg.get("dump") and hp == 0 and (u, d) == order[0]:
                    def dump(name, ap, shape, dt, toks):
                        dd = nc.dram_tensor("dbg_" + name, shape, dt, kind="ExternalOutput").ap()
                        P.op("sync", lambda e: e.dma_start(out=dd, in_=ap), toks, (), dma=True)
                    dump("L", fa(L), [128, T], F32, (tF[L],))
                    dump("E", fa(E), [128, T], F32, (tF[E],))
                    dump("KK", fa(KK), [128, T], F32, (tF[KK],))
                    dump("R", fa(R), [128, T], F32, (tF[R],))
                    dump("Akd", fa(A), [128, T], F32, (tF[A],))
                    dump("AR", AR[:], [128, 16, 256], BF16, (tAR,))
                    dump("Bf", Bf[:], [128, T], BF16, (tBf,))
                    dump("Kf", Kf[:], [128, T], BF16, (tKf,))
                    dump("Btm", Btm[:], [128, 16, 128], BF16, (tBtm,))
                    dump("Vtm", Vtm[:], [128, 16, 128], BF16, (tVtm,))
                    dump("Am", Am[:], [128, 32, 512], BF16, tAm)
                    dump("TTs", TTs[:], [128, 32, 128], BF16, tTTs)
                    dump("YA", fa(YA), [128, T], F32, (tF[YA],))
                    dump("scj", scj[:], [128, 16], F32, (tsm,))
                    dump("mids", mids[:], [128, 16], F32, (tsm,))
                    dump("tots", tots[:], [128, 16], F32, (tsm,))
                if dbg.get("dump") and hp == 0 and not dbg.get("dump2"):
                    break
                if fwd:
                    P.op("sync", lambda e, hp=hp, U0=U0: e.dma_start(out=yf_d[hp, :, U0:U0 + T], in_=fa(YA)), (tF[YA],), (), dma=True)
                else:
                    vop(lambda e: e.tensor_tensor(out=fa(YA), in0=fa(YA), in1=fa(TMP1), op=ALU.add), (tF[YA], tF[TMP1]), (tF[YA],))
                    aop(lambda e: e.activation(out=fa(TMP1), in_=fa(YA), func=AF.Square), (tF[YA],), (tF[TMP1],))
                    for k in range(4):
                        i1 = rot("pb")
                        P.mm_group([(pb[i1][:], bonesf, blkc(fa(YA), k))], (t_con, tF[YA]), (tpb[i1],))
                        vop(lambda e, i1=i1: e.tensor_scalar(out=fin[0][:], in0=pb[i1][:], scalar1=1.0 / 64, scalar2=None, op0=ALU.mult),
                            (tpb[i1],), (tfin[0],))
                        i2 = rot("pb")
                        P.mm_group([(pb[i2][:], bonesf, blkc(fa(TMP1), k))], (t_con, tF[TMP1]), (tpb[i2],))
                        vop(lambda e: e.tensor_tensor(out=fin[1][:], in0=fin[0][:], in1=fin[0][:], op=ALU.mult), (tfin[0],), (tfin[1],))
                        vop(lambda e, i2=i2: e.scalar_tensor_tensor(out=fin[1][:], in0=pb[i2][:], scalar=1.0 / 64, in1=fin[1][:],
                                                                    op0=ALU.mult, op1=ALU.subtract), (tpb[i2], tfin[1]), (tfin[1],))
                        aop(lambda e: e.activation(out=fin[1][:], in_=fin[1][:], func=AF.Ln, bias=epsk[:, 1:2]), (tfin[1], tsm0), (tfin[1],))
                        aop(lambda e: e.activation(out=fin[1][:], in_=fin[1][:], func=AF.Exp, scale=-0.5), (tfin[1],), (tfin[1],))
                        vop(lambda e, k=k: e.tensor_tensor(out=fin[2][:], in0=blkc(fa(YA), k), in1=fin[0][:], op=ALU.subtract),
                            (tF[YA], tfin[0]), (tfin[2],))
                        vop(lambda e: e.tensor_tensor(out=fin[2][:], in0=fin[2][:], in1=fin[1][:], op=ALU.mult), (tfin[2], tfin[1]), (tfin[2],))
                        vop(lambda e, hp=hp: e.tensor_scalar(out=fin[2][:], in0=fin[2][:], scalar1=pcol("gnw", hp), scalar2=pcol("gnb", hp),
                                                             op0=ALU.mult, op1=ALU.add), (tfin[2], t_par), (tfin[2],))
                        i3 = rot("pb")
                        P.mm_group([(pb[i3][:], bonesf, blkc(fa(TMP0), k))], (t_con, tF[TMP0]), (tpb[i3],))
                        vop(lambda e, i3=i3, k=k: e.tensor_tensor(out=fin[3][:], in0=pb[i3][:], in1=blkc(fa(VV), k), op=ALU.mult),
                            (tpb[i3], tF[VV]), (tfin[3],))
                        vop(lambda e: e.tensor_tensor(out=fin[2][:], in0=fin[2][:], in1=fin[3][:], op=ALU.add), (tfin[2], tfin[3]), (tfin[2],))
                        i4 = rot("pb")
                        P.mm_group([(pb[i4][:], gup[:, hp * 128:(hp + 1) * 128], blkc(glb, k))], (tlw, tglb), (tpb[i4],))
                        vop(lambda e, i4=i4: e.tensor_tensor(out=finb[:], in0=fin[2][:], in1=pb[i4][:], op=ALU.mult), (tfin[2], tpb[i4]), (tfinb,))
                        P.op("sync", lambda e, hp=hp, U0=U0, k=k: e.dma_start(out=oT_d[hp, :, U0 + k * 512:U0 + (k + 1) * 512], in_=finb[:]),
                             (tfinb,), (), dma=True)
                    if dbg.get("dump2"):
                        def dump2(name, ap, shape, dt, toks):
                            dd = nc.dram_tensor("dbg2_" + name, shape, dt, kind="ExternalOutput").ap()
                            P.op("sync", lambda e: e.dma_start(out=dd, in_=ap), toks, (), dma=True)
                        dump2("YA", fa(YA), [128, T], F32, (tF[YA],))
                        dump2("TMP0", fa(TMP0), [128, T], F32, (tF[TMP0],))
                        dump2("TMP1", fa(TMP1), [128, T], F32, (tF[TMP1],))
                        dump2("VV", fa(VV), [128, T], F32, (tF[VV],))
                        dump2("glb", glb[:], [128, T], BF16, (tglb,))
                        for q in range(4):
                            dump2("fin%d" % q, fin[q][:], [128, 512], F32, (tfin[q],))


def _cols(v, n):
    return np.ascontiguousarray(v.reshape(n, 128).T.astype(np.float32))


def make_params(inp, flag):
    p = np.zeros((128, NPCOL), np.float32)

    def put(name, arr, n):
        p[:, PC[name]:PC[name] + n] = _cols(np.asarray(arr).reshape(-1), n)

    put("ffn1_pre", inp["ffn1_pre_g"], 16); put("ffn1_post", inp["ffn1_post_g"], 16)
    put("mix_pre", inp["mix_pre_g"], 16); put("mix_post", inp["mix_post_g"], 16)
    put("ffn2_pre", inp["ffn2_pre_g"], 16); put("ffn2_post", inp["ffn2_post_g"], 16)
    put("mu", inp["rwkv_shift_mix"], 26)
    put("dbf", inp["decay_bias_fwd"], 8); put("dbb", inp["decay_bias_bwd"], 8)
    put("ibf", inp["iclr_bias_fwd"], 8); put("ibb", inp["iclr_bias_bwd"], 8)
    put("kns", inp["key_norm_scale"], 8); put("kim", inp["key_iclr_mix"], 8)
    put("bsc", inp["bonus_scale"], 8); put("gnw", inp["gn_w"], 8); put("gnb", inp["gn_b"], 8)
    p[:, PC["flag"]] = flag
    return p


def make_consts():
    c = np.zeros((128, NCCOL), np.float32)
    c[:, 0:128] = np.eye(128)
    c[:, 128:256] = 1.0
    c[0:64, 256:320] = 1.0
    c[64:128, 320:384] = 1.0
    s = np.arange(128)[:, None]
    t = np.arange(128)[None, :]
    strict_f = (t > s).astype(np.float32)
    incl_f = (t >= s).astype(np.float32)
    strict_b = (t < s).astype(np.float32)
    incl_b = (t <= s).astype(np.float32)
    c[:, CC["m4f"]:CC["m4f"] + 512] = np.concatenate([strict_f, incl_f, strict_f, incl_f], 1)
    c[:, CC["m4b"]:CC["m4b"] + 512] = np.concatenate([strict_b, incl_b, strict_b, incl_b], 1)
    c[:, CC["mlf"]:CC["mlf"] + 128] = strict_f.T
    c[:, CC["mlb"]:CC["mlb"] + 128] = strict_b.T
    return c


def make_nat_tables(rpb, linked):
    rpb = np.asarray(rpb, np.float32)
    kc = np.arange(64)[:, None]
    qc = np.arange(64)[None, :]
    cs = np.clip(qc - 8, 0, 48)
    colok = (kc >= cs) & (kc < cs + 16)
    cidx = np.clip(kc - qc + 15, 0, 30)
    G = np.full((16, 2, 64, 2, 7, 64), NEG, np.float32)
    for par in range(2):
        for m in range(14):
            dr = m - 7 + par
            if dr < -7 or dr > 7:
                continue
            val = np.where(colok[None], rpb[:, dr + 7][:, cidx], NEG)
            G[:, par, :, m % 2, m // 2, :] = val
    G = G.reshape(8, 2, 128, 2, 7, 64).transpose(0, 2, 1, 3, 4, 5)
    S = np.full((7, 16, 2, 64, 6, 64), NEG, np.float32)
    for spi, (u, i) in enumerate(SPECIAL):
        er0 = 24 if u == 0 else -4
        for c in range(6):
            for par in range(2):
                er = er0 + 2 * c + par
                if linked:
                    gi = u * 32 + i
                    ws = min(max(gi - 4, 0), 56)
                    gr = u * 32 + er
                    ok = ws <= gr < ws + 8
                    dr = gr - gi
                else:
                    ws = min(max(i - 4, 0), 24)
                    ok = (ws <= er < ws + 8) and (0 <= er < 32)
                    dr = er - i
                if not ok:
                    continue
                S[spi, :, par, :, c, :] = np.where(colok[None], rpb[:, dr + 7][:, cidx], NEG)
    S = S.reshape(7, 16, 128, 6, 64)
    return np.ascontiguousarray(G), np.ascontiguousarray(S)


def core_units(xp, xs, c):
    if c < 4:
        return np.concatenate([xs[c], xp[c]], 0)
    b = 4 + 3 * (c - 4)
    return np.concatenate([xp[b], xp[b + 1], xp[b + 2]], 0)


WNAMES = ("ffn1_w_gate", "ffn1_w_up", "ffn1_w_down", "ffn2_w_gate", "ffn2_w_up", "ffn2_w_down", "w_in", "w_out")


def make_inputs_core(inp, x, linked, shared=None):
    m = {"x": np.ascontiguousarray(x, dtype=np.float32), "params": make_params(inp, 1.0 if linked else 0.0)}
    if shared is None:
        shared = make_shared(inp)
    m.update(shared["common"])
    G, S = shared["nat"][bool(linked)]
    m["natG"] = G
    m["natS"] = S
    return m


def make_shared(inp):
    common = {"consts": make_consts()}
    for k in WNAMES:
        common[k] = np.ascontiguousarray(np.asarray(inp[k])[0], dtype=np.float32)
    g = lambda k: np.asarray(inp[k])[0].astype(np.float32)
    common["lupf"] = np.ascontiguousarray(np.concatenate([g("decay_up_fwd"), g("iclr_up_fwd")], 0))
    common["lupb"] = np.ascontiguousarray(np.concatenate([g("decay_up_bwd"), g("iclr_up_bwd")], 0))
    common["gup"] = np.ascontiguousarray(g("gate_up"))
    nat = {True: make_nat_tables(np.asarray(inp["nat_rpb"])[0], True),
           False: make_nat_tables(np.asarray(inp["nat_rpb"])[0], False)}
    return {"common": common, "nat": nat}


_CACHE = {}


def kernel(**inp):
    inp = {k: np.asarray(v) for k, v in inp.items()}
    xp, xs = inp["x_prompt"], inp["x_sample"]
    if "nc" not in _CACHE:
        _CACHE["nc"] = build_program()
    nc = _CACHE["nc"]
    shared = make_shared(inp)
    in_maps = []
    for c in range(8):
        in_maps.append(make_inputs_core(inp, core_units(xp, xs, c), c < 4, shared))
    res = run_bass_kernel_spmd(nc, in_maps, core_ids=list(range(8)))
    yp = np.empty(xp.shape, np.float32)
    ys = np.empty(xs.shape, np.float32)
    for c in range(8):
        y = np.asarray(res.results[c]["y"], dtype=np.float32)
        if c < 4:
            ys[c] = y[:2 * UNIT]
            yp[c] = y[2 * UNIT:]
        else:
            b = 4 + 3 * (c - 4)
            yp[b] = y[:UNIT]
            yp[b + 1] = y[UNIT:2 * UNIT]
            yp[b + 2] = y[2 * UNIT:]
    return (yp, ys)
```
